# Optimizing a Trainium2 kernel written in Bass

```python
import math
import jax, jax.numpy as jnp
from jax import lax
import numpy as np

D_MODEL = 1024
BATCH = 4
SEQ = 4096
DEPTH = 2

CHUNK = 64
LEFT_CHUNKS = 8
BAND = LEFT_CHUNKS + 1
ATTN_HEAD_DIM = 64
ATTN_HEADS = D_MODEL // ATTN_HEAD_DIM
MAX_REL = 256
DN_HEAD_DIM = 128
DN_HEADS = D_MODEL // DN_HEAD_DIM
DN_WIDTH = DN_HEADS * DN_HEAD_DIM
CONV_K = 4
D_FF = 4 * D_MODEL
EPS = 1e-6
N_MIXERS = 2
N_ATTN_LAYERS = (DEPTH + 1) // 2
N_DN_LAYERS = DEPTH // 2

kernel_name = "hybrid_chunk_attn_gated_deltanet_encoder"


def rmsnorm(x, w):
    xf = x.astype(jnp.float32)
    y = xf * lax.rsqrt(jnp.mean(xf * xf, axis=-1, keepdims=True) + EPS)
    return (y * w.astype(jnp.float32)).astype(x.dtype)


def l2norm(x):
    xf = x.astype(jnp.float32)
    return xf * lax.rsqrt(jnp.sum(xf * xf, axis=-1, keepdims=True) + EPS)


def chunked_rel_attention(h, w_qkv, rel_bias, w_o):
    B, S, _ = h.shape
    nc = S // CHUNK
    qkv = h @ w_qkv
    q, k, v = jnp.split(qkv, 3, axis=-1)
    q = q.reshape(B, S, ATTN_HEADS, ATTN_HEAD_DIM)
    k = k.reshape(B, S, ATTN_HEADS, ATTN_HEAD_DIM)
    v = v.reshape(B, S, ATTN_HEADS, ATTN_HEAD_DIM)
    pad = ((0, 0), (LEFT_CHUNKS * CHUNK, 0), (0, 0), (0, 0))
    kpad = jnp.pad(k, pad)
    vpad = jnp.pad(v, pad)
    q_off = jnp.arange(CHUNK)[:, None]
    k_off = jnp.arange(BAND * CHUNK)[None, :] - LEFT_CHUNKS * CHUNK
    dist = jnp.clip(q_off - k_off, -MAX_REL, MAX_REL) + MAX_REL
    bias = rel_bias[:, dist].astype(jnp.float32)
    scale = ATTN_HEAD_DIM ** -0.5

    def one_chunk(c):
        start = c * CHUNK
        q_c = lax.dynamic_slice_in_dim(q, start, CHUNK, axis=1)
        k_b = lax.dynamic_slice_in_dim(kpad, start, BAND * CHUNK, axis=1)
        v_b = lax.dynamic_slice_in_dim(vpad, start, BAND * CHUNK, axis=1)
        s = jnp.einsum('bqhd,bkhd->bhqk', q_c, k_b,
                       preferred_element_type=jnp.float32) * scale + bias[None]
        k_abs = start - LEFT_CHUNKS * CHUNK + jnp.arange(BAND * CHUNK)
        s = jnp.where((k_abs >= 0)[None, None, None, :], s, -jnp.inf)
        p = jax.nn.softmax(s, axis=-1).astype(v.dtype)
        return jnp.einsum('bhqk,bkhd->bqhd', p, v_b)

    o = lax.map(one_chunk, jnp.arange(nc))
    o = jnp.moveaxis(o, 0, 1).reshape(B, S, D_MODEL)
    return o @ w_o


def causal_conv(x, w):
    S = x.shape[1]
    xp = jnp.pad(x, ((0, 0), (CONV_K - 1, 0), (0, 0)))
    y = xp[:, 0:S] * w[0]
    for j in range(1, CONV_K):
        y = y + xp[:, j:j + S] * w[j]
    return y


def chunk_gated_delta_rule(q, k, v, g, beta):
    B, S, H, dk = q.shape
    dv = v.shape[-1]
    nc = S // CHUNK

    def to_chunks(t):
        t = t.astype(jnp.float32).reshape((B, nc, CHUNK, H) + t.shape[3:])
        return jnp.moveaxis(t, 3, 1)

    q, k, v, g, beta = map(to_chunks, (q, k, v, g, beta))
    gc = jnp.cumsum(g, axis=-1)
    idx = jnp.arange(CHUNK)
    incl = idx[:, None] >= idx[None, :]
    strict = idx[:, None] > idx[None, :]
    decay = jnp.exp(jnp.where(incl, gc[..., :, None] - gc[..., None, :], -jnp.inf))
    kb = k * beta[..., None]
    vb = v * beta[..., None]
    L = jnp.where(strict, jnp.einsum('bhnid,bhnjd->bhnij', kb, k) * decay, 0.0)
    M = L + jnp.eye(CHUNK, dtype=jnp.float32)
    rhs = jnp.concatenate([vb, kb * jnp.exp(gc)[..., None]], axis=-1)
    sol = lax.linalg.triangular_solve(M, rhs, left_side=True, lower=True,
                                      unit_diagonal=True)
    u, w = sol[..., :dv], sol[..., dv:]
    qk = jnp.einsum('bhnid,bhnjd->bhnij', q, k) * decay
    q_dec = q * jnp.exp(gc)[..., None]
    k_dec = k * jnp.exp(gc[..., -1:] - gc)[..., None]
    g_last = jnp.exp(gc[..., -1])
    xs = tuple(jnp.moveaxis(t, 2, 0) for t in (u, w, qk, q_dec, k_dec, g_last))

    def step(state, inp):
        u_c, w_c, qk_c, qd_c, kd_c, gl_c = inp
        v_new = u_c - jnp.einsum('bhck,bhkv->bhcv', w_c, state)
        o_c = jnp.einsum('bhck,bhkv->bhcv', qd_c, state) + jnp.einsum('bhij,bhjv->bhiv', qk_c, v_new)
        state = state * gl_c[..., None, None] + jnp.einsum('bhck,bhcv->bhkv', kd_c, v_new)
        return state, o_c

    s0 = jnp.zeros((B, H, dk, dv), jnp.float32)
    _, o = lax.scan(step, s0, xs)
    o = jnp.transpose(o, (1, 0, 3, 2, 4))
    return o.reshape(B, S, H, dv)


def gated_deltanet(h, w_in, conv_w, a_log, dt_bias, head_norm, w_o):
    B, S, _ = h.shape
    proj = h @ w_in
    qkv = proj[..., :3 * DN_WIDTH]
    z = proj[..., 3 * DN_WIDTH:4 * DN_WIDTH]
    a = proj[..., 4 * DN_WIDTH:4 * DN_WIDTH + DN_HEADS]
    b = proj[..., 4 * DN_WIDTH + DN_HEADS:]
    qkv = jax.nn.silu(causal_conv(qkv, conv_w))
    q, k, v = jnp.split(qkv, 3, axis=-1)
    q = l2norm(q.reshape(B, S, DN_HEADS, DN_HEAD_DIM)) * (DN_HEAD_DIM ** -0.5)
    k = l2norm(k.reshape(B, S, DN_HEADS, DN_HEAD_DIM))
    v = v.reshape(B, S, DN_HEADS, DN_HEAD_DIM)
    beta = jax.nn.sigmoid(b.astype(jnp.float32))
    g = -jnp.exp(a_log.astype(jnp.float32)) * jax.nn.softplus(
        a.astype(jnp.float32) + dt_bias.astype(jnp.float32))
    o = chunk_gated_delta_rule(q, k, v, g, beta)
    o = rmsnorm(o, head_norm) * jax.nn.silu(
        z.astype(jnp.float32).reshape(B, S, DN_HEADS, DN_HEAD_DIM))
    return o.reshape(B, S, DN_WIDTH).astype(h.dtype) @ w_o


def squared_relu_mlp(h, w_up, w_down):
    return jnp.square(jax.nn.relu(h @ w_up)) @ w_down


def setup_inputs(seed: int = 0) -> dict:
    key = jax.random.key(seed)
    ks = jax.random.split(key, 18)
    f32 = jnp.float32

    def nrm(k, shape, fan_in, gain=1.0):
        return jax.random.normal(k, shape, f32) * (gain * fan_in ** -0.5)

    def gains(k, shape):
        return 1.0 + 0.02 * jax.random.normal(k, shape, f32)

    NA, NB = N_ATTN_LAYERS, N_DN_LAYERS
    x = jax.random.normal(ks[0], (BATCH, SEQ, D_MODEL), f32)
    attn_norm = gains(ks[1], (NA, D_MODEL))
    attn_w_qkv = nrm(ks[2], (NA, D_MODEL, 3 * D_MODEL), D_MODEL)
    attn_rel_bias = 0.1 * jax.random.normal(ks[3], (NA, ATTN_HEADS, 2 * MAX_REL + 1), f32)
    attn_w_o = nrm(ks[4], (NA, D_MODEL, D_MODEL), D_MODEL)
    dn_norm = gains(ks[5], (NB, D_MODEL))
    dn_w_in = nrm(ks[6], (NB, D_MODEL, 4 * DN_WIDTH + 2 * DN_HEADS), D_MODEL)
    dn_conv_w = nrm(ks[7], (NB, CONV_K, 3 * DN_WIDTH), CONV_K)
    dn_a_log = jnp.log(jax.random.uniform(ks[8], (NB, DN_HEADS), f32, 1.0, 16.0))
    dt = jnp.exp(jax.random.uniform(ks[9], (NB, DN_HEADS), f32,
                                    math.log(1e-3), math.log(1e-1)))
    dn_dt_bias = dt + jnp.log(-jnp.expm1(-dt))
    dn_head_norm = gains(ks[10], (NB, DN_HEAD_DIM))
    dn_w_o = nrm(ks[11], (NB, DN_WIDTH, D_MODEL), DN_WIDTH)
    mlp_norm = gains(ks[12], (DEPTH, D_MODEL))
    mlp_w_up = nrm(ks[13], (DEPTH, D_MODEL, D_FF), D_MODEL)
    mlp_w_down = nrm(ks[14], (DEPTH, D_FF, D_MODEL), D_FF, gain=0.5)
    final_norm = gains(ks[15], (D_MODEL,))
    return {"x": x, "attn_norm": attn_norm, "attn_w_qkv": attn_w_qkv,
            "attn_rel_bias": attn_rel_bias, "attn_w_o": attn_w_o,
            "dn_norm": dn_norm, "dn_w_in": dn_w_in, "dn_conv_w": dn_conv_w,
            "dn_a_log": dn_a_log, "dn_dt_bias": dn_dt_bias, "dn_head_norm": dn_head_norm,
            "dn_w_o": dn_w_o, "mlp_norm": mlp_norm, "mlp_w_up": mlp_w_up,
            "mlp_w_down": mlp_w_down, "final_norm": final_norm}


def reference(x, attn_norm, attn_w_qkv, attn_rel_bias, attn_w_o,
              dn_norm, dn_w_in, dn_conv_w, dn_a_log, dn_dt_bias, dn_head_norm, dn_w_o,
              mlp_norm, mlp_w_up, mlp_w_down, final_norm):
    for i in range(DEPTH):
        j = i // N_MIXERS
        if i % N_MIXERS == 0:
            x = x + chunked_rel_attention(rmsnorm(x, attn_norm[j]), attn_w_qkv[j],
                                          attn_rel_bias[j], attn_w_o[j])
        else:
            x = x + gated_deltanet(rmsnorm(x, dn_norm[j]), dn_w_in[j], dn_conv_w[j],
                                   dn_a_log[j], dn_dt_bias[j], dn_head_norm[j], dn_w_o[j])
        x = x + squared_relu_mlp(rmsnorm(x, mlp_norm[i]), mlp_w_up[i], mlp_w_down[i])
    return rmsnorm(x, final_norm)
```

```python
import numpy as np
from contextlib import ExitStack
import concourse.bass as bass
import concourse.mybir as mybir
from concourse.bass_utils import run_bass_kernel_spmd

F32 = mybir.dt.float32
BF16 = mybir.dt.bfloat16
AF = mybir.ActivationFunctionType
ALU = mybir.AluOpType
AX = mybir.AxisListType

ENGS = ("pe", "act", "dve", "pool", "sp")


class Buf:
    __slots__ = ("name", "last_w", "readers", "const", "excl")

    def __init__(self, name="", excl=False):
        self.name = name
        self.last_w = None
        self.readers = []
        self.const = False
        self.excl = excl


class Instr:
    __slots__ = ("eng", "fn", "deps", "is_dma", "ndma", "semkey", "signal", "sig_sem",
                 "sig_val", "idx", "users", "selfwait")

    def __init__(self, eng, fn):
        self.eng = eng
        self.fn = fn
        self.deps = []
        self.is_dma = False
        self.ndma = 0
        self.semkey = None
        self.signal = False
        self.sig_sem = None
        self.sig_val = 0
        self.users = False
        self.selfwait = False


class Prog:
    def __init__(self, nc):
        self.nc = nc
        self.streams = {e: [] for e in ENGS}
        self.n = 0

    def add(self, eng, fn, reads=(), writes=(), dma=0, semkey=None, inc=16, selfwait=False):
        I = Instr(eng, fn)
        I.selfwait = selfwait
        I.idx = self.n
        self.n += 1
        if dma:
            I.is_dma = True
            I.ndma = dma * inc
            I.semkey = semkey
        raw = {}
        oth = {}
        xb = [b for b in list(reads) + list(writes) if b.excl]
        if xb:
            reads = [b for b in reads if not b.excl]
            writes = [b for b in writes if not b.excl]
            for b in xb:
                J = b.last_w
                if J is not None and J is not I and (J.eng != eng or J.is_dma or I.is_dma):
                    if J not in I.deps:
                        I.deps.append(J)
                        J.users = True
                b.last_w = I
        for b in reads:
            if b.last_w is not None:
                raw[id(b.last_w)] = b.last_w
        for b in writes:
            if b.last_w is not None:
                oth[id(b.last_w)] = b.last_w
            for r in b.readers:
                oth[id(r)] = r
        for k, J in list(raw.items()) + list(oth.items()):
            if J is I or J in I.deps:
                continue
            anydma = J.is_dma or I.is_dma
            if J.eng == eng and not anydma:
                if eng == "pe" or k not in raw:
                    continue
            I.deps.append(J)
            J.users = True
        for b in writes:
            b.last_w = I
            b.readers = []
        for b in reads:
            if b.const:
                continue
            if b.last_w is not I:
                b.readers.append(I)
        self.streams[eng].append(I)
        return I

    def emit(self, engines, final_waits=True):
        nc = self.nc
        with ExitStack() as es:
            esem = {e: es.enter_context(nc.semaphore("s_" + e)) for e in ENGS}
            dsem = {}
            ecount = {e: 0 for e in ENGS}
            dcount = {}
            for e in ENGS:
                for I in self.streams[e]:
                    if I.is_dma:
                        k = I.semkey
                        if k not in dsem:
                            dsem[k] = es.enter_context(nc.semaphore("d_" + str(k)))
                            dcount[k] = 0
                        dcount[k] += I.ndma
                        I.sig_sem = dsem[k]
                        I.sig_val = dcount[k]
                        I.signal = True
                    elif I.users:
                        ecount[e] += 1
                        I.sig_sem = esem[e]
                        I.sig_val = ecount[e]
                        I.signal = True
            self.sem_counts = dict(ecount)
            with nc.Block() as block:
                def run(e, eng):
                    waited = {}
                    for I in self.streams[e]:
                        need = {}
                        for J in I.deps:
                            s = J.sig_sem
                            v = J.sig_val
                            key = id(s)
                            if waited.get(key, 0) >= v:
                                continue
                            if key not in need or need[key][1] < v:
                                need[key] = (s, v)
                        for key, (s, v) in need.items():
                            eng.wait_ge(s, v)
                            waited[key] = v
                        if I.is_dma:
                            I.fn(eng, I.sig_sem)
                            if I.selfwait:
                                eng.wait_ge(I.sig_sem, I.sig_val)
                                waited[id(I.sig_sem)] = I.sig_val
                        else:
                            r = I.fn(eng)
                            if I.signal:
                                r.then_inc(I.sig_sem, 1)
                    if final_waits and e == "sp":
                        for k, s in dsem.items():
                            eng.wait_ge(s, dcount[k])
                        for e2 in ENGS:
                            if ecount[e2] > 0:
                                eng.wait_ge(esem[e2], ecount[e2])

                @block.tensor
                def _(eng):
                    run("pe", eng)

                @block.scalar
                def _(eng):
                    run("act", eng)

                @block.vector
                def _(eng):
                    run("dve", eng)

                @block.gpsimd
                def _(eng):
                    run("pool", eng)

                @block.sync
                def _(eng):
                    run("sp", eng)


class SbufAlloc:
    def __init__(self, nc, nbytes):
        self.t32 = nc.alloc_sbuf_tensor("arena", [128, nbytes // 4], F32)
        self.t16 = self.t32.bitcast(BF16)
        self.free_list = [(0, nbytes)]
        self.allocs = {}
        self.dead = []
        self.last = None
        self.used = 0
        self.hi = 0

    def alloc(self, shape, dtype, name=None):
        esz = 4 if dtype == F32 else 2
        n = 1
        for s in shape[1:]:
            n *= s
        nbytes = (n * esz + 63) // 64 * 64
        for k, (st, sz) in enumerate(self.free_list):
            if sz >= nbytes:
                break
        else:
            raise AssertionError("SBUF overflow %s need %d free %s" % (name, nbytes, self.free_list))
        if sz == nbytes:
            self.free_list.pop(k)
        else:
            self.free_list[k] = (st + nbytes, sz - nbytes)
        base = self.t32 if dtype == F32 else self.t16
        e0 = st // esz
        v = base[0:shape[0], e0:e0 + n]
        if len(shape) == 3:
            v = v.rearrange("p (a b) -> p a b", a=shape[1])
        elif len(shape) == 4:
            v = v.rearrange("p (a b c) -> p a b c", a=shape[1], b=shape[2])
        self.allocs[id(v)] = (v, st, nbytes, [])
        self.last = id(v)
        self.used += nbytes
        self.hi = max(self.hi, self.used)
        return v

    def newbuf(self, name=""):
        b = Buf(name)
        v, st, nb, bl = self.allocs[self.last]
        for (db, ds, de) in self.dead:
            if ds < st + nb and st < de:
                if db.last_w is not None:
                    b.readers.append(db.last_w)
                b.readers.extend(db.readers)
        bl.append(b)
        return b

    def newbufs(self, n):
        return [self.newbuf() for _ in range(n)]

    def free(self, *aps):
        for ap in aps:
            v, st, nb, bl = self.allocs.pop(id(ap))
            for b in bl:
                self.dead.append((b, st, st + nb))
            self.used -= nb
            fl = self.free_list + [(st, nb)]
            fl.sort()
            out = []
            for (a, z) in fl:
                if out and out[-1][0] + out[-1][1] == a:
                    out[-1] = (out[-1][0], out[-1][1] + z)
                else:
                    out.append((a, z))
            self.free_list = out


def op_mm(P, out, lhsT, rhs, start, stop, reads, writes, tp=None):
    if tp is None:
        P.add("pe", lambda e: e.matmul(out, lhsT, rhs, start=start, stop=stop), reads, writes)
    else:
        P.add("pe", lambda e: e.matmul(out, lhsT, rhs, start=start, stop=stop,
                                       tile_position=tp), reads, writes)


def op_tr(P, out, in_, ident, reads, writes):
    P.add("pe", lambda e: e.transpose(out, in_, ident), reads, writes)


def op_act(P, out, in_, func, reads, writes, bias=None, scale=None, accum=None):
    kw = {}
    if bias is not None:
        kw["bias"] = bias
    if scale is not None:
        kw["scale"] = scale
    if accum is not None:
        kw["accum_out"] = accum
    P.add("act", lambda e: e.activation(out=out, in_=in_, func=func, **kw), reads, writes)


def op_tt(P, eng, out, in0, in1, op, reads, writes):
    P.add(eng, lambda e: e.tensor_tensor(out=out, in0=in0, in1=in1, op=op), reads, writes)


def op_ts(P, eng, out, in0, s1, s2, op0, op1, reads, writes, accum=None):
    if accum is None:
        P.add(eng, lambda e: e.tensor_scalar(out=out, in0=in0, scalar1=s1, scalar2=s2,
                                             op0=op0, op1=op1), reads, writes)
    else:
        P.add(eng, lambda e: e.tensor_scalar(out=out, in0=in0, scalar1=s1, scalar2=s2,
                                             op0=op0, op1=op1, accum_out=accum), reads, writes)


def op_stt(P, out, in0, scalar, in1, op0, op1, reads, writes):
    P.add("dve", lambda e: e.scalar_tensor_tensor(out=out, in0=in0, scalar=scalar, in1=in1,
                                                  op0=op0, op1=op1), reads, writes)


def op_copy(P, eng, out, in_, reads, writes):
    if eng == "act":
        P.add("act", lambda e: e.activation(out=out, in_=in_, func=AF.Copy), reads, writes)
    else:
        P.add(eng, lambda e: e.tensor_copy(out=out, in_=in_), reads, writes)


def op_memset(P, eng, out, val, writes):
    P.add(eng, lambda e: e.memset(out, val), (), writes)


def op_rmax(P, out, in_, reads, writes):
    P.add("dve", lambda e: e.reduce_max(out=out, in_=in_, axis=AX.X), reads, writes)


def op_recip(P, out, in_, reads, writes):
    P.add("dve", lambda e: e.reciprocal(out=out, in_=in_), reads, writes)


def op_dma(P, eng, pairs, reads, writes, semkey):
    def fn(e, s, pairs=pairs):
        for (o, i) in pairs:
            e.dma_start(out=o, in_=i).then_inc(s, 16)
    P.add(eng, fn, reads, writes, dma=len(pairs), semkey=semkey)


D = 1024
NCH = 8
T = 2048
TH = 512
TS = T + TH
NTT = 4
DFF = 4096
EPS = 1e-6
NEG = -30000.0


class Env:
    pass


def setup_env(nc, P, consts_dram, sbuf_bytes=206 * 1024):
    E = Env()
    E.nc = nc
    E.P = P
    E.sb = SbufAlloc(nc, sbuf_bytes)
    sb = E.sb
    E.psall_h = nc.alloc_psum_tensor("psall", [128, 8, 512], F32)
    E.psall = E.psall_h
    E.ps = [E.psall[:, i, :] for i in range(8)]
    E.psall16 = E.psall.bitcast(BF16)
    E._keys = 0

    def key(name, E=E):
        E._keys += 1
        return "%s%d" % (name, E._keys)
    E.key = key
    E.bps = [Buf("ps%d" % i, excl=True) for i in range(8)]
    E.ident_f = sb.alloc([128, 128], F32)
    E.bident_f = sb.newbuf()
    E.ident_b = sb.alloc([128, 128], BF16)
    E.bident_b = sb.newbuf()
    E.ones_b = sb.alloc([128, 128], BF16)
    E.ones_f = sb.alloc([128, 128], F32)
    E.eps = sb.alloc([128, 1], F32)
    E.bconst2 = sb.newbuf()
    op_dma(P, "sp", [(E.ident_f, consts_dram["ident"])], (), [E.bident_f], "const")
    op_dma(P, "pool", [(E.ident_b, consts_dram["ident"])], (), [E.bident_b], "constb")
    op_memset(P, "dve", E.ones_b, 1.0, [E.bconst2])
    op_memset(P, "dve", E.ones_f, 1.0, [E.bconst2])
    op_memset(P, "dve", E.eps, EPS, [E.bconst2])
    return E


def rmsnorm_tile(E, xT, bx, c0, nw, bnw, hout, bh, S, ncols=512, psbank=7):
    P = E.P
    sq, bsq, rt, brt, rstd, brstd = S["sq"], S["bsq"], S["rt"], S["brt"], S["rstd"], S["brstd"]
    ps = E.ps[psbank][:, 0:ncols]
    bp = E.bps[psbank]
    for c in range(NCH):
        op_act(P, sq[:, c, 0:ncols], xT[:, c, c0:c0 + ncols], AF.Square, [bx[c]], [bsq[c]])
    for c in range(NCH):
        op_mm(P, ps, E.ones_b, sq[:, c, 0:ncols], c == 0, c == NCH - 1,
              [bsq[c], E.bconst2], [bp])
    op_act(P, rt[:, 0:ncols], ps, AF.Sqrt, [bp, E.bconst2], [brt], bias=E.eps, scale=1.0 / D)
    op_recip(P, rstd[:, 0:ncols], rt[:, 0:ncols], [brt], [brstd])
    for c in range(NCH):
        op_stt(P, hout[:, c, 0:ncols], xT[:, c, c0:c0 + ncols], nw[:, c:c + 1], rstd[:, 0:ncols],
               ALU.mult, ALU.mult, [bx[c], bnw, brstd], [bh])


def norm_scratch(E):
    sb = E.sb
    S = {}
    S["sq"] = sb.alloc([128, NCH, 512], BF16)
    S["bsq"] = sb.newbufs(NCH)
    S["rt"] = sb.alloc([128, 512], F32)
    S["brt"] = sb.newbuf()
    S["rstd"] = sb.alloc([128, 512], F32)
    S["brstd"] = sb.newbuf()
    return S


def free_norm_scratch(E, S):
    E.sb.free(S["sq"], S["rt"], S["rstd"])


def flat2(ap3):
    return ap3.rearrange("p a b -> p (a b)")


class Rot:
    def __init__(self, lst):
        self.lst = list(lst)
        self.i = 0

    def next(self):
        v = self.lst[self.i % len(self.lst)]
        self.i += 1
        return v


def mlp_block(E, nw_dram, w_up, w_down):
    P, sb = E.P, E.sb
    ps, bps = E.ps, E.bps
    xT, bx = E.xT, E.bx
    nw = sb.alloc([128, NCH], F32)
    bnw = sb.newbuf()
    op_dma(P, "sp", [(nw, nw_dram)], (), [bnw], E.key("nw"))
    hT = sb.alloc([128, NCH, T], BF16)
    bh = sb.newbufs(NTT)
    S = norm_scratch(E)
    for t in range(NTT):
        rmsnorm_tile(E, xT, [bx[c][t] for c in range(NCH)], t * 512, nw, bnw,
                     hT[:, :, t * 512:(t + 1) * 512], bh[t], S)
    free_norm_scratch(E, S)
    G = 4
    NG = DFF // (128 * G)
    upT, bup, wu, bwu, wd, bwd = [], [], [], [], [], []
    for k in range(2):
        upT.append(sb.alloc([128, G, T], BF16))
        bup.append([[sb.newbuf() for t in range(NTT)] for j in range(G)])
        wu.append(sb.alloc([128, NCH, 128 * G], BF16))
        bwu.append(sb.newbuf())
        wd.append(sb.alloc([128, G, D], BF16))
        bwd.append(sb.newbuf())
    rl, brl = [], []
    for r in range(3):
        rl.append(sb.alloc([128, 512], F32))
        brl.append(sb.newbuf())
    rrl = Rot(range(3))
    rup = Rot([0, 1, 2, 3])
    rdn = Rot([4, 5, 6, 7])
    wu_v = w_up.rearrange("(c p) f -> p c f", p=128)
    wd_v = w_down.rearrange("(j p) n -> p j n", p=128)
    ku, kd = E.key("wu"), E.key("wd")
    for g in range(NG):
        k = g % 2
        op_dma(P, "pool", [(wu[k], wu_v[:, :, g * 128 * G:(g + 1) * 128 * G])], (), [bwu[k]],
               "%s_%d" % (ku, k))
        op_dma(P, "pool", [(wd[k], wd_v[:, g * G:(g + 1) * G, :])], (), [bwd[k]], "%s_%d" % (kd, k))
        for t in range(NTT):
            tc_ = slice(t * 512, (t + 1) * 512)
            for j in range(G):
                b = rup.next()
                for c in range(NCH):
                    op_mm(P, ps[b], wu[k][:, c, j * 128:(j + 1) * 128], hT[:, c, tc_], c == 0,
                          c == NCH - 1, [bwu[k], bh[t]], [bps[b]])
                r = rrl.next()
                op_act(P, rl[r], ps[b], AF.Relu, [bps[b]], [brl[r]])
                op_tt(P, "pool", upT[k][:, j, tc_], rl[r], rl[r], ALU.mult, [brl[r]], [bup[k][j][t]])
        for t in range(NTT):
            tc_ = slice(t * 512, (t + 1) * 512)
            for m in range(NCH):
                b = rdn.next()
                for j in range(G):
                    op_mm(P, ps[b], wd[k][:, j, m * 128:(m + 1) * 128], upT[k][:, j, tc_], j == 0,
                          j == G - 1, [bwd[k], bup[k][j][t]], [bps[b]])
                op_tt(P, "dve", xT[:, m, tc_], xT[:, m, tc_], ps[b], ALU.add, [bps[b], bx[m][t]],
                      [bx[m][t]])
    sb.free(nw, hT, *upT, *wu, *wd, *rl)


def stage_a(E, d, out_x1=None, out_h1=None):
    P, sb = E.P, E.sb
    ps, bps = E.ps, E.bps
    nw_a = sb.alloc([128, NCH], F32)
    bnw_a = sb.newbuf()
    op_dma(P, "sp", [(nw_a, d["attn_norm"])], (), [bnw_a], E.key("nw"))
    hT_all = sb.alloc([128, NCH, TS], BF16)
    bh = sb.newbufs(5)
    oT = sb.alloc([128, NCH, T], BF16)
    boT = [[sb.newbuf() for t in range(NTT)] for hp in range(NCH)]
    xl, bxl = [], []
    for k in range(2):
        xl.append(sb.alloc([128, NCH, 512], F32))
        bxl.append(sb.newbuf())
    S = norm_scratch(E)
    xin = d["xT_in"].rearrange("(c p) t -> p c t", p=128)
    kx = E.key("xl")
    for tt in range(5):
        k = tt % 2
        op_dma(P, "sp", [(xl[k], xin[:, :, tt * 512:(tt + 1) * 512])], (), [bxl[k]],
               "%s_%d" % (kx, k))
        rmsnorm_tile(E, xl[k], [bxl[k]] * NCH, 0, nw_a, bnw_a,
                     hT_all[:, :, tt * 512:(tt + 1) * 512], bh[tt], S)
    sb.free(xl[0], xl[1])
    free_norm_scratch(E, S)

    mask0 = sb.alloc([128, 640], F32)
    bmask0 = sb.newbuf()
    hmb = sb.alloc([128, 1152], F32)
    bhmb = sb.newbuf()
    op_dma(P, "sp", [(mask0, d["mask0"])], (), [bmask0], E.key("mask0"))
    op_dma(P, "sp", [(hmb, d["hmask"])], (), [bhmb], E.key("hmb"))
    wqkv, bw, Bm, bBm, kT, bk, vtok, bv, qT, bq = [], [], [], [], [], [], [], [], [], []
    for k in range(2):
        wqkv.append(sb.alloc([128, 3, NCH, 128], BF16))
        bw.append(sb.newbuf())
        Bm.append(sb.alloc([128, 2, 640], F32))
        bBm.append(sb.newbuf())
        kT.append(sb.alloc([128, TS], BF16))
        bk.append(sb.newbufs(5))
        vtok.append(sb.alloc([128, TS], BF16))
        bv.append(sb.newbufs(5))
        qT.append(sb.alloc([128, T], BF16))
        bq.append(sb.newbufs(NTT))
    s_sb, bs, p_sb, bp_, pt_sb, bpt, dg, bdg, st, bst = [], [], [], [], [], [], [], [], [], []
    for s in range(2):
        s_sb.append(sb.alloc([128, 640], F32))
        bs.append(sb.newbuf())
        p_sb.append(sb.alloc([128, 640], BF16))
        bp_.append(sb.newbuf())
        pt_sb.append(sb.alloc([128, 640], BF16))
        bpt.append(sb.newbuf())
        dg.append(sb.alloc([128, 128], BF16))
        bdg.append(sb.newbuf())
        st.append(sb.alloc([128, 4], F32))
        bst.append([sb.newbuf(), sb.newbuf(), sb.newbuf()])
    wq_v = d["w_qkv"].rearrange("(c p) (s n) -> p s c n", p=128, s=3)
    kwq, kbm = E.key("wqkv"), E.key("bm")
    S_ps = [flat2(E.psall[:, 0:2, :]), flat2(E.psall[:, 2:4, :])]
    PT_ps = flat2(E.psall[:, 4:6, :])

    def load_pair(hp):
        i = hp % 2
        op_dma(P, "pool", [(wqkv[i][:, s3], wq_v[:, s3, :, hp * 128:(hp + 1) * 128])
                           for s3 in range(3)], (), [bw[i]], "%s_%d" % (kwq, i))
        op_dma(P, "sp", [(Bm[i][:, s], d["btab"][2 * hp + s]) for s in range(2)], (), [bBm[i]],
               "%s_%d" % (kbm, i))
        for s in range(2):
            op_tt(P, "pool", Bm[i][:, s], Bm[i][:, s], mask0, ALU.add, [bBm[i], bmask0], [bBm[i]])

    def proj_items(hp):
        i = hp % 2
        items = []
        for tt in range(5):
            cs = slice(tt * 512, (tt + 1) * 512)

            def kproj(tt=tt, cs=cs):
                for c in range(NCH):
                    op_mm(P, ps[7], wqkv[i][:, 1, c, :], hT_all[:, c, cs], c == 0, c == NCH - 1,
                          [bw[i], bh[tt]], [bps[7]])
                op_copy(P, "act", kT[i][:, cs], ps[7], [bps[7]], [bk[i][tt]])
            items.append(kproj)
            if tt >= 1:
                def qproj(tt=tt, cs=cs):
                    for c in range(NCH):
                        op_mm(P, ps[7], wqkv[i][:, 0, c, :], hT_all[:, c, cs], c == 0,
                              c == NCH - 1, [bw[i], bh[tt]], [bps[7]])
                    op_act(P, qT[i][:, (tt - 1) * 512:tt * 512], ps[7], AF.Copy, [bps[7]],
                           [bq[i][tt - 1]], scale=0.125)
                items.append(qproj)

            def vproj(tt=tt, cs=cs):
                for j in range(4):
                    for c in range(NCH):
                        op_mm(P, ps[7][:, j * 128:(j + 1) * 128],
                              hT_all[:, c, tt * 512 + j * 128:tt * 512 + (j + 1) * 128],
                              wqkv[i][:, 2, c, :], c == 0, c == NCH - 1, [bw[i], bh[tt]], [bps[7]])
                op_copy(P, "act", vtok[i][:, cs], ps[7], [bps[7]], [bv[i][tt]])
            items.append(vproj)
        return items

    def attn_pair(hp, p):
        i = hp % 2
        k0 = 128 * p
        kts = sorted(set([k0 // 512, (k0 + 639) // 512]))
        for s in range(2):
            rows = slice(s * 64, (s + 1) * 64)
            bS = [bps[2 * s], bps[2 * s + 1]]
            rk = [bq[i][p // 4]] + [bk[i][t_] for t_ in kts]
            op_mm(P, S_ps[s][:, 0:512], qT[i][rows, p * 128:(p + 1) * 128],
                  kT[i][rows, k0:k0 + 512], True, True, rk, bS, tp=(s * 64, 0))
            op_mm(P, S_ps[s][:, 512:640], qT[i][rows, p * 128:(p + 1) * 128],
                  kT[i][rows, k0 + 512:k0 + 640], True, True, rk, bS, tp=(s * 64, 0))
            op_tt(P, "dve", s_sb[s], S_ps[s][:, 0:640], Bm[i][:, s], ALU.add, bS + [bBm[i]], [bs[s]])
            if p < 4:
                op_tt(P, "dve", s_sb[s], s_sb[s], hmb[:, k0:k0 + 640], ALU.add, [bs[s], bhmb], [bs[s]])
            P.add("dve", lambda e, s=s: e.reduce_max(out=st[s][:, 0:1], in_=s_sb[s], axis=AX.X,
                                                    negate=True), [bs[s]], [bst[s][0]])
            op_act(P, p_sb[s], s_sb[s], AF.Exp, [bs[s], bst[s][0]], [bp_[s], bst[s][1]],
                   bias=st[s][:, 0:1], accum=st[s][:, 1:2])
            op_recip(P, st[s][:, 2:3], st[s][:, 1:2], [bst[s][1]], [bst[s][2]])
            op_ts(P, "pool", dg[s], E.ident_b, st[s][:, 2:3], 0.0, ALU.mult, ALU.add,
                  [E.bident_b, bst[s][2]], [bdg[s]])
            for kb in range(5):
                op_mm(P, PT_ps[:, kb * 128:(kb + 1) * 128], p_sb[s][:, kb * 128:(kb + 1) * 128],
                      dg[s], True, True, [bp_[s], bdg[s]], [bps[4], bps[5]])
            op_copy(P, "act", pt_sb[s], PT_ps[:, 0:640], [bps[4], bps[5]], [bpt[s]])
        oc = (p % 4) * 128
        for s in range(2):
            for kb in range(5):
                tl = p + kb
                op_mm(P, ps[6][s * 64:(s + 1) * 64, oc:oc + 128],
                      vtok[i][:, tl * 128 + s * 64:tl * 128 + (s + 1) * 64],
                      pt_sb[s][:, kb * 128:(kb + 1) * 128], kb == 0, kb == 4,
                      [bv[i][tl // 4], bpt[s]], [bps[6]], tp=(0, s * 64))
        op_copy(P, "act", oT[:, hp, p * 128:(p + 1) * 128], ps[6][:, oc:oc + 128], [bps[6]],
                [boT[hp][p // 4]])

    load_pair(0)
    for it in proj_items(0):
        it()
    for hp in range(NCH):
        nxt = []
        if hp + 1 < NCH:
            load_pair(hp + 1)
            nxt = proj_items(hp + 1)
        for p in range(16):
            attn_pair(hp, p)
            for _ in range(2):
                if nxt and p >= 1:
                    nxt.pop(0)()
        while nxt:
            nxt.pop(0)()
    sb.free(mask0, hmb, hT_all, *wqkv, *Bm, *kT, *vtok, *qT, *s_sb, *p_sb, *pt_sb, *dg, *st)

    E.xT = sb.alloc([128, NCH, T], F32)
    E.bx = [[sb.newbuf() for t in range(NTT)] for c in range(NCH)]
    xT, bx = E.xT, E.bx
    kx = E.key("xT")
    for t in range(NTT):
        op_dma(P, "sp", [(xT[:, :, t * 512:(t + 1) * 512], xin[:, :, TH + t * 512:TH + (t + 1) * 512])],
               (), [bx[c][t] for c in range(NCH)], "%s_%d" % (kx, t))
    wo = sb.alloc([128, NCH, D], BF16)
    bwo = sb.newbuf()
    op_dma(P, "pool", [(wo, d["w_o"].rearrange("(c p) n -> p c n", p=128))], (), [bwo], E.key("wo"))
    rb = Rot(range(8))
    for t in range(NTT):
        tc_ = slice(t * 512, (t + 1) * 512)
        for m in range(NCH):
            b = rb.next()
            for hp in range(NCH):
                op_mm(P, ps[b], wo[:, hp, m * 128:(m + 1) * 128], oT[:, hp, tc_], hp == 0,
                      hp == NCH - 1, [bwo, boT[hp][t]], [bps[b]])
            op_tt(P, "dve", xT[:, m, tc_], xT[:, m, tc_], ps[b], ALU.add, [bps[b], bx[m][t]],
                  [bx[m][t]])
    sb.free(oT, wo, nw_a)

    mlp_block(E, d["mlp_norm0"], d["w_up0"], d["w_down0"])

    if out_x1 is not None:
        ko = E.key("ox1")
        ov = out_x1.rearrange("(c p) t -> p c t", p=128)
        E.b_x1_scr = []
        for t in range(NTT):
            bscr = Buf()
            E.b_x1_scr.append(bscr)
            op_dma(P, "sp", [(ov[:, :, t * 512:(t + 1) * 512], xT[:, :, t * 512:(t + 1) * 512])],
                   [bx[c][t] for c in range(NCH)], [bscr], "%s_%d" % (ko, t))
    if out_h1 is not None:
        E.b_h1_src = []
        nw = sb.alloc([128, NCH], F32)
        bnw = sb.newbuf()
        op_dma(P, "sp", [(nw, d["dn_norm"])], (), [bnw], E.key("nw"))
        S = norm_scratch(E)
        h1 = [sb.alloc([128, NCH, 512], BF16) for _ in range(2)]
        bh1 = [sb.newbuf() for _ in range(2)]
        if isinstance(out_h1, list):
            hvl = [o_.rearrange("(c p) t -> p c t", p=128) for o_ in out_h1]
        else:
            hv = out_h1.rearrange("(c p) t -> p c t", p=128)
            hvl = [hv[:, :, t * 512:(t + 1) * 512] for t in range(NTT)]
        ko = E.key("oh1")
        for t in range(NTT):
            k = t % 2
            rmsnorm_tile(E, xT, [bx[c][t] for c in range(NCH)], t * 512, nw, bnw, h1[k], bh1[k], S)
            bsrc = Buf()
            E.b_h1_src.append(bsrc)
            op_dma(P, "sp", [(hvl[t], h1[k])], [bh1[k]], [bsrc], "%s_%d" % (ko, k))
        free_norm_scratch(E, S)
        sb.free(nw, *h1)


def _pc(v):
    return np.ascontiguousarray(np.asarray(v, np.float32).reshape(NCH, 128).T)


def host_consts():
    q = np.arange(128)[:, None]
    j = np.arange(640)[None, :]
    idx = np.clip(q - j + 512, -256, 256) + 256
    mask0 = np.zeros((128, 640), np.float32)
    mask0[(q < 64) & (j >= 576)] = NEG
    mask0[(q >= 64) & (j < 64)] = NEG
    return idx, mask0


def stage_a_inputs(inp, b, half):
    idx, mask0 = host_consts()
    x = inp["x"]
    lo = half * T
    xs = np.zeros((TS, D), np.float32)
    if half > 0:
        xs[:TH] = x[b, lo - TH:lo]
    xs[TH:] = x[b, lo:lo + T]
    hm = np.zeros((128, 1152), np.float32)
    if half == 0:
        hm[:, :TH] = NEG
    return {
        "ident": np.eye(128, dtype=np.float32),
        "xT_in": np.ascontiguousarray(xs.T),
        "attn_norm": _pc(inp["attn_norm"][0]),
        "w_qkv": np.ascontiguousarray(inp["attn_w_qkv"][0]),
        "btab": np.ascontiguousarray(inp["attn_rel_bias"][0][:, idx]),
        "mask0": mask0,
        "hmask": hm,
        "w_o": np.ascontiguousarray(inp["attn_w_o"][0]),
        "mlp_norm0": _pc(inp["mlp_norm"][0]),
        "w_up0": np.ascontiguousarray(inp["mlp_w_up"][0]),
        "w_down0": np.ascontiguousarray(inp["mlp_w_down"][0]),
        "dn_norm": _pc(inp["dn_norm"][0]),
    }


STAGE_A_SHAPES = {
    "ident": [128, 128], "xT_in": [D, TS], "attn_norm": [128, NCH], "w_qkv": [D, 3 * D],
    "btab": [16, 128, 640], "mask0": [128, 640], "hmask": [128, 1152], "w_o": [D, D],
    "mlp_norm0": [128, NCH], "w_up0": [D, DFF], "w_down0": [DFF, D], "dn_norm": [128, NCH],
}


def build_stage_a():
    nc = bass.Bass("TRN2", target_bir_lowering=False)
    d = {k: nc.dram_tensor(k, shp, F32, kind="ExternalInput").ap() for k, shp in STAGE_A_SHAPES.items()}
    x1 = nc.dram_tensor("x1T", [D, T], F32, kind="ExternalOutput").ap()
    h1 = nc.dram_tensor("h1T", [D, T], BF16, kind="ExternalOutput").ap()
    P = Prog(nc)
    E = setup_env(nc, P, d)
    stage_a(E, d, out_x1=x1, out_h1=h1)
    P.emit(None)
    return nc, P, E


import os
DBG_N = int(os.environ.get('DBG_N', '99'))
SEQ = 4096
NBLK = SEQ // 512
NH = 4
BIG = 30000.0


def stage_b(E, d, out_og, nblk=NBLK, nsc=4, phases=99, hv=None, og_rs=None, og_reads=None, post_block=None):
    P, sb = E.P, E.sb
    ps, bps = E.ps, E.bps
    ps16 = E.psall16
    class _BS:
        def __getitem__(self, s_):
            return bps[s_[0]]
    bslot = _BS()
    rbk = Rot([3, 4, 5, 6, 7])

    def slot32(s_, n=128):
        return E.psall[:, s_[0], s_[1] * 128:s_[1] * 128 + n]

    def slot16(s_):
        return ps16[:, s_[0], s_[1] * 256:s_[1] * 256 + 128]

    triu = sb.alloc([128, 128], F32)
    posm = sb.alloc([128, 128], F32)
    negm = sb.alloc([128, 128], F32)
    bcm = sb.newbuf()
    op_dma(P, "sp", [(triu, d["triu"]), (posm, d["posm"]), (negm, d["negm"])], (), [bcm], E.key("cm"))
    cw = sb.alloc([128, 3 * NH * 4], F32)
    alog = sb.alloc([128, NH], F32)
    dtb = sb.alloc([128, NH], F32)
    hn = sb.alloc([128, 1], F32)
    bsm = sb.newbuf()
    op_dma(P, "sp", [(cw, d["cw"]), (alog, d["alog"]), (dtb, d["dtb"]), (hn, d["hn"])], (), [bsm],
           E.key("sm"))
    nega = sb.alloc([128, NH], F32)
    bnega = sb.newbuf()
    op_act(P, nega, alog, AF.Exp, [bsm], [bnega])
    op_ts(P, "dve", nega, nega, -1.0, None, ALU.mult, ALU.bypass, [bnega], [bnega])
    W = sb.alloc([128, NH * 4, NCH, 128], BF16)
    bW = sb.newbuf()
    wv = d["w_in"].rearrange("(c p) n -> p c n", p=128)
    kW = E.key("W")
    for sec in range(4):
        op_dma(P, "pool", [(W[:, h * 4 + sec], wv[:, :, sec * 512 + h * 128:sec * 512 + (h + 1) * 128])
                           for h in range(NH)], (), [bW], kW)
    wab = sb.alloc([128, NCH, 8], BF16)
    bwab = sb.newbuf()
    op_dma(P, "pool", [(wab, wv[:, :, 2048:2056])], (), [bwab], E.key("wab"))
    dgw = sb.alloc([128, 3 * NH * 4, 128], BF16)
    bdgw = sb.newbuf()
    for idx in range(3 * NH * 4):
        op_ts(P, "pool", dgw[:, idx, :], E.ident_b, cw[:, idx:idx + 1], 0.0, ALU.mult, ALU.add,
              [E.bident_b, bsm], [bdgw])
    pcb = sb.alloc([128, 3 * NH, 516], BF16)
    bpcb = [[sb.newbuf() for s in range(3)] for h in range(NH)]
    for h in range(NH):
        for s in range(3):
            op_memset(P, "pool", pcb[:, h * 3 + s, 0:3], 0.0, [bpcb[h][s]])
    S = sb.alloc([128, NH, 128], F32)
    Sb = sb.alloc([128, NH, 128], BF16)
    bS = sb.newbufs(NH)
    bSb = [Buf() for _ in range(NH)]
    for h in range(NH):
        op_memset(P, "pool", S[:, h, :], 0.0, [bS[h]])
        op_memset(P, "pool", Sb[:, h, :], 0.0, [bSb[h]])
    hblk = [sb.alloc([128, NCH, 512], BF16) for _ in range(2)]
    bhblk = [Buf(), Buf()]
    ab_sb = sb.alloc([128, 4, 8], F32)
    bab = sb.newbuf()
    gsb = sb.alloc([128, 4, NH], F32)
    beta = sb.alloc([128, 4, NH], F32)
    bg = sb.newbufs(4)
    bbeta = [Buf() for _ in range(4)]
    tmpg = sb.alloc([128, 8, NH], F32)
    btmpg = sb.newbuf()
    y = [sb.alloc([128, 2, 512], F32) for _ in range(NH)]
    by = [[Buf() for s in range(2)] for _ in range(NH)]
    vT = sb.alloc([128, NH, 512], BF16)
    bvT = sb.newbufs(NH)
    sz = sb.alloc([128, NH, 512], F32)
    bsz = sb.newbufs(NH)
    sq = [sb.alloc([128, 512], BF16) for _ in range(2)]
    bsq = [Buf(), Buf()]
    lnv = [sb.alloc([128, 512], F32) for _ in range(2)]
    blnv = [Buf(), Buf()]
    qn = sb.alloc([128, NH, 512], BF16)
    kn = sb.alloc([128, NH, 512], BF16)
    bqn = sb.newbufs(NH)
    bkn = [Buf() for _ in range(NH)]
    og = [sb.alloc([128, NH, 512], BF16) for _ in range(2)]
    bog = [[Buf() for h in range(NH)] for _ in range(2)]
    sm = sb.alloc([128, 8, NH], F32)
    bsmv = sb.newbufs(8)
    onrm = sb.alloc([128, NH, 4], F32)
    bonrm = [sb.newbufs(3) for _ in range(NH)]
    MB = {}
    names32 = ["gB", "ts", "Ds", "ti", "Dti", "Egc", "u", "L", "U", "Lp0", "Lp1", "Up0", "Up1", "P0", "P1"]
    names16 = ["AT", "kd", "kbd", "vb", "qd", "TTb", "wT", "vnew", "on", "junk"]
    for h in range(NH):
        for nm in names32:
            MB[(nm, h)] = (sb.alloc([128, 128], F32), sb.newbuf())
        for nm in names16:
            MB[(nm, h)] = (sb.alloc([128, 128], BF16), sb.newbuf())

    if hv is None:
        hv_ = d["h1T_full"].rearrange("(c p) t -> p c t", p=128)

        def hv(nb):
            return hv_[:, :, nb * 512:(nb + 1) * 512]
    if og_rs is not None:
        m01 = sb.alloc([128, 2], F32)
        bm01 = sb.newbuf()
        op_dma(P, "sp", [(m01, d["m01"])], (), [bm01], E.key("m01"))
        ogm = [sb.alloc([128, NH, 512], BF16) for _ in range(2)]
        bogm = [sb.newbuf() for _ in range(2)]
        E.bog_src = []
    khb = E.key("hblk")
    kog = E.key("og")
    rproj = Rot([0, 1])
    ones_col = E.ones_f[:, 0:1]

    def load_hblk(nb):
        op_dma(P, "sp", [(hblk[nb % 2], hv(nb))], (og_reads(nb) if og_reads else ()), [bhblk[nb % 2]],
               "%s_%d" % (khb, nb % 2))
    load_hblk(0)
    for nb in range(nblk):
        kb_ = nb % 2
        if nb + 1 < nblk:
            load_hblk(nb + 1)
        s_ab = (rbk.next(), 0)
        for j in range(4):
            for c in range(NCH):
                op_mm(P, slot32(s_ab)[:, j * 8:(j + 1) * 8], hblk[kb_][:, c, j * 128:(j + 1) * 128],
                      wab[:, c, :], c == 0, c == NCH - 1, [bhblk[kb_], bwab], [bslot[s_ab]])
        op_copy(P, "dve", flat2(ab_sb), slot32(s_ab)[:, 0:32], [bslot[s_ab]], [bab])
        for j in range(4):
            a_ = ab_sb[:, j, 0:NH]
            b_ = ab_sb[:, j, NH:2 * NH]
            t0, t1, t2, t3, t4, t5 = (tmpg[:, i_, :] for i_ in range(6))
            op_tt(P, "dve", t0, a_, dtb, ALU.add, [bab, bsm], [btmpg])
            op_act(P, t1, t0, AF.Abs, [btmpg], [btmpg])
            op_act(P, t2, t1, AF.Exp, [btmpg], [btmpg], scale=-1.0)
            op_act(P, t3, t2, AF.Ln, [btmpg], [btmpg], bias=ones_col)
            op_stt(P, t4, t0, 0.0, t3, ALU.max, ALU.add, [btmpg], [btmpg])
            op_tt(P, "dve", gsb[:, j, :], t4, nega, ALU.mult, [btmpg, bnega], [bg[j]])
            op_act(P, t5, b_, AF.Exp, [bab], [btmpg], scale=-1.0)
            op_ts(P, "dve", t5, t5, 1.0, None, ALU.add, ALU.bypass, [btmpg], [btmpg])
            op_recip(P, beta[:, j, :], t5, [btmpg], [bbeta[j]])
        for h in range(NH):
            yi = h
            for sec in range(3):
                pb = rproj.next()
                for c in range(NCH):
                    op_mm(P, ps[pb], W[:, h * 4 + sec, c, :], hblk[kb_][:, c, :], c == 0,
                          c == NCH - 1, [bW, bhblk[kb_]], [bps[pb]])
                pc = pcb[:, h * 3 + sec, :]
                op_copy(P, "act", pc[:, 3:515], ps[pb], [bps[pb]], [bpcb[h][sec]])
                for k in range(4):
                    op_mm(P, ps[2], dgw[:, (sec * NH + h) * 4 + k, :], pc[:, k:k + 512], k == 0,
                          k == 3, [bdgw, bpcb[h][sec]], [bps[2]])
                if sec < 2:
                    op_act(P, y[yi][:, sec, :], ps[2], AF.Silu, [bps[2]], [by[yi][sec]])
                else:
                    op_act(P, vT[:, h, :], ps[2], AF.Silu, [bps[2]], [bvT[h]])
                op_copy(P, "pool", pc[:, 0:3], pc[:, 512:515], [bpcb[h][sec]], [bpcb[h][sec]])
            pb = rproj.next()
            for c in range(NCH):
                op_mm(P, ps[pb], W[:, h * 4 + 3, c, :], hblk[kb_][:, c, :], c == 0, c == NCH - 1,
                      [bW, bhblk[kb_]], [bps[pb]])
            op_act(P, sz[:, h, :], ps[pb], AF.Silu, [bps[pb]], [bsz[h]])
        for h in range(NH):
            yi = h
            for sec in range(2):
                op_tt(P, "pool", sq[sec], y[yi][:, sec, :], y[yi][:, sec, :], ALU.mult,
                      [by[yi][sec]], [bsq[sec]])
                op_mm(P, ps[2], E.ones_b, sq[sec], True, True, [E.bconst2, bsq[sec]], [bps[2]])
                op_act(P, lnv[sec], ps[2], AF.Ln, [bps[2], E.bconst2], [blnv[sec]], bias=E.eps)
                op_act(P, lnv[sec], lnv[sec], AF.Exp, [blnv[sec]], [blnv[sec]], scale=-0.5)
                dst, bdst = (qn, bqn) if sec == 0 else (kn, bkn)
                op_stt(P, dst[:, h, :], y[yi][:, sec, :], (128.0 ** -0.5) if sec == 0 else 1.0,
                       lnv[sec], ALU.mult, ALU.mult, [by[yi][sec], blnv[sec]], [bdst[h]])
        oi = nb % 2
        for j in range(nsc):
            cs = slice(j * 128, (j + 1) * 128)
            bk_g = rbk.next()
            s_gc, s_gl = (bk_g, 0), (bk_g, 1)
            op_mm(P, slot32(s_gc, NH), triu, gsb[:, j, :], True, True, [bcm, bg[j]], [bslot[s_gc]])
            op_mm(P, slot32(s_gl, NH), E.ones_f, gsb[:, j, :], True, True, [E.bconst2, bg[j]],
                  [bslot[s_gl]])
            gc, gl, egc, ebk, dk, ekd, glast = (sm[:, i_, :] for i_ in range(7))
            op_copy(P, "dve", gc, slot32(s_gc, NH), [bslot[s_gc]], [bsmv[0]])
            op_copy(P, "dve", gl, slot32(s_gl, NH), [bslot[s_gl]], [bsmv[1]])
            op_act(P, egc, gc, AF.Exp, [bsmv[0]], [bsmv[2]])
            op_tt(P, "dve", ebk, beta[:, j, :], egc, ALU.mult, [bbeta[j], bsmv[2]], [bsmv[3]])
            op_tt(P, "dve", dk, gl, gc, ALU.subtract, [bsmv[0], bsmv[1]], [bsmv[4]])
            op_act(P, ekd, dk, AF.Exp, [bsmv[4]], [bsmv[5]])
            op_act(P, glast, gl, AF.Exp, [bsmv[1]], [bsmv[6]])

            st_ = [dict() for _ in range(NH)]

            def ph_a(h, BK):
                X = st_[h]
                gB, bgB = MB[("gB", h)]
                op_ts(P, "pool", gB, E.ones_f, gsb[:, j, h:h + 1], 0.0, ALU.mult, ALU.add,
                      [E.bconst2, bg[j]], [bgB])
                X["gbc"] = (BK["gbc"], h)
                op_mm(P, slot32(X["gbc"]), gB, triu, True, True, [bgB, bcm], [bslot[X["gbc"]]])
                X["kk"] = (BK["kk"], h)
                op_mm(P, slot32(X["kk"]), kn[:, h, cs], kn[:, h, cs], True, True, [bkn[h]],
                      [bslot[X["kk"]]])
                X["qk"] = (BK["qk"], h)
                op_mm(P, slot32(X["qk"]), kn[:, h, cs], qn[:, h, cs], True, True, [bkn[h], bqn[h]],
                      [bslot[X["qk"]]])
                X["kt"] = (BK["kt"], h)
                op_tr(P, slot16(X["kt"]), kn[:, h, cs], E.ident_b, [bkn[h], E.bident_b],
                      [bslot[X["kt"]]])
                X["vt"] = (BK["vt"], h)
                op_tr(P, slot16(X["vt"]), vT[:, h, cs], E.ident_b, [bvT[h], E.bident_b],
                      [bslot[X["vt"]]])

            def ph_b(h, BK):
                X = st_[h]
                G = slot32(X["gbc"])
                bG = bslot[X["gbc"]]
                ts, bts = MB[("ts", h)]
                Ds, bDs = MB[("Ds", h)]
                ti, bti = MB[("ti", h)]
                Dti, bDti = MB[("Dti", h)]
                Egc, bEgc = MB[("Egc", h)]
                Gs, bGs = MB[("gB", h)]
                op_copy(P, "dve", Gs, G, [bG], [bGs])
                op_stt(P, ts, Gs, sm[:, 0, h:h + 1], posm, ALU.subtract, ALU.max, [bGs, bsmv[0], bcm], [bts])
                op_act(P, Ds, ts, AF.Exp, [bts], [bDs], scale=-1.0)
                op_stt(P, ti, Gs, sm[:, 0, h:h + 1], negm, ALU.subtract, ALU.min, [bGs, bsmv[0], bcm], [bti])
                op_act(P, Dti, ti, AF.Exp, [bti], [bDti])
                op_act(P, Egc, Gs, AF.Exp, [bGs], [bEgc])
                L, bL = MB[("L", h)]
                if DBG_N > 5:
                    op_stt(P, L, slot32(X["kk"]), beta[:, j, h:h + 1], Ds, ALU.mult, ALU.mult,
                           [bslot[X["kk"]], bbeta[j], bDs], [bL])
                AT, bAT = MB[("AT", h)]
                if DBG_N > 6:
                    op_tt(P, "dve", AT, slot32(X["qk"]), Dti, ALU.mult, [bslot[X["qk"]], bDti], [bAT])
                kd, bkd = MB[("kd", h)]
                kbd, bkbd = MB[("kbd", h)]
                vb, bvb = MB[("vb", h)]
                if DBG_N > 7:
                    op_act(P, kd, slot16(X["kt"]), AF.Copy, [bslot[X["kt"]], bsmv[5]], [bkd],
                           scale=sm[:, 5, h:h + 1])
                if DBG_N > 8:
                    op_act(P, kbd, slot16(X["kt"]), AF.Copy, [bslot[X["kt"]], bsmv[3]], [bkbd],
                           scale=sm[:, 3, h:h + 1])
                if DBG_N > 9:
                    op_act(P, vb, slot16(X["vt"]), AF.Copy, [bslot[X["vt"]], bbeta[j]], [bvb],
                           scale=beta[:, j, h:h + 1])
                qd, bqd = MB[("qd", h)]
                if DBG_N > 10:
                    op_tt(P, "pool", qd, qn[:, h, cs], Egc, ALU.mult, [bqn[h], bEgc], [bqd])

            def ph_c(h, BK):
                X = st_[h]
                L, bL = MB[("L", h)]
                U, bU = MB[("U", h)]
                X["ut"] = (BK["ut"], h)
                op_mm(P, slot32(X["ut"]), L, E.ident_f, True, True, [bL, E.bident_f], [bslot[X["ut"]]])
                op_copy(P, "act", U, slot32(X["ut"]), [bslot[X["ut"]]], [bU])
                P0, bP0 = MB[("P0", h)]
                op_tt(P, "pool", P0, E.ident_f, U, ALU.subtract, [E.bident_f, bU], [bP0])
                X["Lp"], X["Up"], X["Pm"] = ("L", h), ("U", h), ("P0", h)

            def ph_pow1(h, BK, m):
                X = st_[h]
                Lp, bLp = MB[X["Lp"]]
                Up, bUp = MB[X["Up"]]
                nl = ("Lp%d" % (m % 2), h)
                nu = ("Up%d" % (m % 2), h)
                s1 = (BK["s1"], h)
                op_mm(P, slot32(s1), Up, Lp, True, True, [bUp, bLp], [bslot[s1]])
                op_copy(P, "act", MB[nl][0], slot32(s1), [bslot[s1]], [MB[nl][1]])
                if m < 6:
                    s2 = (BK["s2"], h)
                    op_mm(P, slot32(s2), Lp, Up, True, True, [bUp, bLp], [bslot[s2]])
                    op_copy(P, "dve", MB[nu][0], slot32(s2), [bslot[s2]], [MB[nu][1]])
                    X["Up"] = nu
                X["Lp"] = nl

            def ph_pow2(h, BK, m):
                X = st_[h]
                nl = X["Lp"]
                Pm, bPm = MB[X["Pm"]]
                npm = ("P%d" % (m % 2), h)
                s3 = (BK["s3"], h)
                op_mm(P, slot32(s3), MB[nl][0], Pm, True, True, [MB[nl][1], bPm], [bslot[s3]])
                op_tt(P, "dve", MB[npm][0], Pm, slot32(s3), ALU.add, [bPm, bslot[s3]], [MB[npm][1]])
                X["Pm"] = npm

            def ph_d(h, BK):
                X = st_[h]
                TT_, bTT = MB[("TTb", h)]
                op_copy(P, "pool", TT_, MB[X["Pm"]][0], [MB[X["Pm"]][1]], [bTT])
                vb, bvb = MB[("vb", h)]
                kbd, bkbd = MB[("kbd", h)]
                u, bu = MB[("u", h)]
                wT, bwT = MB[("wT", h)]
                s1 = (BK["s1"], h)
                op_mm(P, slot32(s1), TT_, vb, True, True, [bTT, bvb], [bslot[s1]])
                op_copy(P, "act", u, slot32(s1), [bslot[s1]], [bu])
                s2 = (BK["s2"], h)
                op_mm(P, slot32(s2), kbd, TT_, True, True, [bTT, bkbd], [bslot[s2]])
                op_copy(P, "dve", wT, slot32(s2), [bslot[s2]], [bwT])

            def ph_e1(h, BK):
                u, bu = MB[("u", h)]
                wT, bwT = MB[("wT", h)]
                vnew, bvn = MB[("vnew", h)]
                s1 = (BK["s1"], h)
                op_mm(P, slot32(s1), wT, Sb[:, h, :], True, True, [bwT, bSb[h]], [bslot[s1]])
                op_tt(P, "dve", vnew, u, slot32(s1), ALU.subtract, [bu, bslot[s1]], [bvn])

            def ph_e2(h, BK):
                X = st_[h]
                vnew, bvn = MB[("vnew", h)]
                qd, bqd = MB[("qd", h)]
                AT, bAT = MB[("AT", h)]
                kd, bkd = MB[("kd", h)]
                X["o"] = (BK["o"], h)
                op_mm(P, slot32(X["o"]), qd, Sb[:, h, :], True, False, [bqd, bSb[h]], [bslot[X["o"]]])
                op_mm(P, slot32(X["o"]), AT, vnew, False, True, [bAT, bvn], [bslot[X["o"]]])
                s2 = (BK["s2"], h)
                op_mm(P, slot32(s2), kd, vnew, True, True, [bkd, bvn], [bslot[s2]])
                op_stt(P, S[:, h, :], S[:, h, :], sm[:, 6, h:h + 1], slot32(s2), ALU.mult, ALU.add,
                       [bS[h], bsmv[6], bslot[s2]], [bS[h]])
                op_copy(P, "pool", Sb[:, h, :], S[:, h, :], [bS[h]], [bSb[h]])

            def ph_f(h, BK):
                X = st_[h]
                o_ps = slot32(X["o"])
                bo = bslot[X["o"]]
                junk, bjunk = MB[("junk", h)]
                on, bon = MB[("on", h)]
                ssq, lv, rstd = onrm[:, h, 0:1], onrm[:, h, 1:2], onrm[:, h, 2:3]
                b0, b1, b2 = bonrm[h]
                op_act(P, junk, o_ps, AF.Square, [bo], [bjunk, b0], accum=ssq)
                op_act(P, lv, ssq, AF.Ln, [b0, E.bconst2], [b1], bias=E.eps, scale=1.0 / 128)
                op_act(P, rstd, lv, AF.Exp, [b1], [b2], scale=-0.5)
                op_act(P, on, o_ps, AF.Copy, [bo, b2], [bon], scale=rstd)
                s1 = (BK["s1"], h)
                op_tr(P, slot16(s1), on, E.ident_b, [bon, E.bident_b], [bslot[s1]])
                op_stt(P, og[oi][:, h, cs], slot16(s1), hn[:, 0:1], sz[:, h, cs], ALU.mult, ALU.mult,
                       [bslot[s1], bsm, bsz[h]], [bog[oi][h]])

            phl = [(ph_a, ("gbc", "kk", "qk", "kt", "vt")), (ph_b, ()), (ph_c, ("ut",))]
            for m in range(1, 7):
                phl.append((lambda h, BK, m=m: ph_pow1(h, BK, m), ("s1", "s2")))
                phl.append((lambda h, BK, m=m: ph_pow2(h, BK, m), ("s3",)))
            phl += [(ph_d, ("s1", "s2")), (ph_e1, ("s1",)), (ph_e2, ("o", "s2")), (ph_f, ("s1",))]
            BKo = None
            for (ph, names) in phl[:phases]:
                BK = {nm: rbk.next() for nm in names}
                if "o" in BK:
                    BKo = BK["o"]
                if ph is ph_f:
                    BK["o"] = BKo
                for h in range(NH):
                    ph(h, BK)
        if og_rs is None:
            ov = out_og.rearrange("(h p) t -> p h t", p=128)
            op_dma(P, "sp", [(ov[:, :, nb * 512:(nb + 1) * 512], og[oi])], bog[oi], (), "%s_%d" % (kog, oi))
        else:
            ovs = og_rs(nb).rearrange("(g h p) t -> p g h t", g=2, p=128)
            blk_bufs = []
            for g_ in range(2):
                bsrc = Buf()
                blk_bufs.append(bsrc)
                op_ts(P, "pool", ogm[g_], og[oi], m01[:, g_:g_ + 1], 0.0, ALU.mult, ALU.add,
                      bog[oi] + [bm01], [bogm[g_]])
                op_dma(P, "sp", [(ovs[:, g_], ogm[g_])], [bogm[g_]], [bsrc], "%s_%d" % (kog, g_))
            E.bog_src.append(blk_bufs)
            if post_block is not None:
                post_block(nb)
    if "dbg" in d:
        hh = int(os.environ.get("DBG_H", "3"))
        names = ["L", "U", "AT", "TTb", "u", "vnew", "kd", "kbd", "vb", "qd", "Ds", "Dti"]
        dbg = sb.alloc([128, 128 * len(names)], F32)
        bdbg = sb.newbuf()
        for ii, nm in enumerate(names):
            op_copy(P, "dve", dbg[:, ii * 128:(ii + 1) * 128], MB[(nm, hh)][0], [MB[(nm, hh)][1]], [bdbg])
        op_dma(P, "sp", [(d["dbg"], dbg)], [bdbg], (), E.key("dbg"))


STAGE_B_SHAPES = {
    "ident": ([128, 128], F32), "h1T_full": ([D, SEQ], BF16), "w_in": ([D, 2056], F32),
    "cw": ([128, 48], F32), "alog": ([128, NH], F32), "dtb": ([128, NH], F32), "hn": ([128, 1], F32),
    "triu": ([128, 128], F32), "posm": ([128, 128], F32), "negm": ([128, 128], F32),
}


def stage_b_inputs(inp, hg, h1T_full=None):
    hs = slice(hg * NH, (hg + 1) * NH)
    w = inp["dn_w_in"][0]
    cols = []
    for sec in range(4):
        cols.append(w[:, sec * 1024 + hg * 512:sec * 1024 + (hg + 1) * 512])
    cols.append(w[:, 4096 + hg * NH:4096 + (hg + 1) * NH])
    cols.append(w[:, 4104 + hg * NH:4104 + (hg + 1) * NH])
    cwf = inp["dn_conv_w"][0]
    cw = cwf.reshape(4, 3, 8, 128)[:, :, hs, :]
    cw = np.ascontiguousarray(cw.transpose(3, 1, 2, 0)).reshape(128, 48)
    ii = np.arange(128)[:, None]
    jj = np.arange(128)[None, :]
    return {
        "ident": np.eye(128, dtype=np.float32),
        "h1T_full": h1T_full,
        "w_in": np.ascontiguousarray(np.concatenate(cols, axis=1)),
        "cw": cw,
        "alog": np.ascontiguousarray(np.broadcast_to(inp["dn_a_log"][0][hs][None, :], (128, NH))),
        "dtb": np.ascontiguousarray(np.broadcast_to(inp["dn_dt_bias"][0][hs][None, :], (128, NH))),
        "hn": np.ascontiguousarray(inp["dn_head_norm"][0].reshape(128, 1)),
        "triu": (ii <= jj).astype(np.float32),
        "posm": np.where(ii > jj, 0.0, BIG).astype(np.float32),
        "negm": np.where(jj >= ii, 0.0, -BIG).astype(np.float32),
    }


def build_stage_b(**kw):
    nc = bass.Bass("TRN2", target_bir_lowering=False)
    d = {k: nc.dram_tensor(k, shp, dt, kind="ExternalInput").ap() for k, (shp, dt) in STAGE_B_SHAPES.items()}
    og = nc.dram_tensor("ogT", [NH * 128, SEQ], BF16, kind="ExternalOutput").ap()
    if os.environ.get("DBG_OUT"):
        d["dbg"] = nc.dram_tensor("dbg", [128, 128 * 12], F32, kind="ExternalOutput").ap()
    P = Prog(nc)
    E = setup_env(nc, P, d)
    stage_b(E, d, og, **kw)
    P.emit(None)
    return nc, P, E


def oproj_residual(E, oT, boT, wo_dram):
    P, sb = E.P, E.sb
    ps, bps = E.ps, E.bps
    xT, bx = E.xT, E.bx
    wo = sb.alloc([128, NCH, D], BF16)
    bwo = sb.newbuf()
    op_dma(P, "pool", [(wo, wo_dram.rearrange("(c p) n -> p c n", p=128))], (), [bwo], E.key("wo"))
    rb = Rot(range(8))
    for t in range(NTT):
        tc_ = slice(t * 512, (t + 1) * 512)
        for m in range(NCH):
            b = rb.next()
            for hp in range(NCH):
                op_mm(P, ps[b], wo[:, hp, m * 128:(m + 1) * 128], oT[:, hp, tc_], hp == 0,
                      hp == NCH - 1, [bwo, boT[hp][t]], [bps[b]])
            op_tt(P, "dve", xT[:, m, tc_], xT[:, m, tc_], ps[b], ALU.add, [bps[b], bx[m][t]],
                  [bx[m][t]])
    sb.free(wo)


def stage_c(E, d, out, load_x=True):
    P, sb = E.P, E.sb
    if load_x:
        E.xT = sb.alloc([128, NCH, T], F32)
        E.bx = [[sb.newbuf() for t in range(NTT)] for c in range(NCH)]
        xin = d["x1T"].rearrange("(c p) t -> p c t", p=128)
        kx = E.key("xT")
        for t in range(NTT):
            op_dma(P, "sp", [(E.xT[:, :, t * 512:(t + 1) * 512], xin[:, :, t * 512:(t + 1) * 512])],
                   ([E.b_x1_scr[t]] if hasattr(E, "b_x1_scr") else ()),
                   [E.bx[c][t] for c in range(NCH)], "%s_%d" % (kx, t))
    xT, bx = E.xT, E.bx
    oT = sb.alloc([128, NCH, T], BF16)
    boT = [[sb.newbuf() for t in range(NTT)] for hp in range(NCH)]
    if isinstance(d["ogT_own"], list):
        ovl = [o_.rearrange("(c p) t -> p c t", p=128) for o_ in d["ogT_own"]]
    else:
        ov = d["ogT_own"].rearrange("(c p) t -> p c t", p=128)
        ovl = [ov[:, :, t * 512:(t + 1) * 512] for t in range(NTT)]
    ko = E.key("og")
    for t in range(NTT):
        cr = [E.c_reads[t]] if hasattr(E, "c_reads") else ()
        op_dma(P, "sp", [(oT[:, :, t * 512:(t + 1) * 512], ovl[t])],
               cr, [boT[hp][t] for hp in range(NCH)], "%s_%d" % (ko, t))
    oproj_residual(E, oT, boT, d["dn_w_o"])
    sb.free(oT)
    mlp_block(E, d["mlp_norm1"], d["w_up1"], d["w_down1"])
    nw = sb.alloc([128, NCH], F32)
    bnw = sb.newbuf()
    op_dma(P, "sp", [(nw, d["final_norm"])], (), [bnw], E.key("nw"))
    S = norm_scratch(E)
    yo = [sb.alloc([128, NCH, 512], F32) for _ in range(2)]
    byo = [sb.newbuf() for _ in range(2)]
    outv = out.rearrange("(c p) t -> p c t", p=128)
    ko = E.key("out")
    for t in range(NTT):
        k = t % 2
        rmsnorm_tile(E, xT, [bx[c][t] for c in range(NCH)], t * 512, nw, bnw, yo[k], byo[k], S)
        op_dma(P, "sp", [(outv[:, :, t * 512:(t + 1) * 512], yo[k])], [byo[k]], (), "%s_%d" % (ko, k))
    free_norm_scratch(E, S)
    sb.free(nw, *yo)


STAGE_C_SHAPES = {
    "ident": ([128, 128], F32), "x1T": ([D, T], F32), "ogT_own": ([D, T], BF16),
    "dn_w_o": ([D, D], F32), "mlp_norm1": ([128, NCH], F32), "w_up1": ([D, DFF], F32),
    "w_down1": ([DFF, D], F32), "final_norm": ([128, NCH], F32),
}


def stage_c_inputs(inp, x1T=None, ogT_own=None):
    return {
        "ident": np.eye(128, dtype=np.float32),
        "x1T": x1T,
        "ogT_own": ogT_own,
        "dn_w_o": np.ascontiguousarray(inp["dn_w_o"][0]),
        "mlp_norm1": _pc(inp["mlp_norm"][1]),
        "w_up1": np.ascontiguousarray(inp["mlp_w_up"][1]),
        "w_down1": np.ascontiguousarray(inp["mlp_w_down"][1]),
        "final_norm": _pc(inp["final_norm"]),
    }


def build_stage_c():
    nc = bass.Bass("TRN2", target_bir_lowering=False)
    d = {k: nc.dram_tensor(k, shp, dt, kind="ExternalInput").ap() for k, (shp, dt) in STAGE_C_SHAPES.items()}
    out = nc.dram_tensor("outT", [D, T], F32, kind="ExternalOutput").ap()
    P = Prog(nc)
    E = setup_env(nc, P, d)
    stage_c(E, d, out)
    P.emit(None)
    return nc, P, E


BATCH = 4
CORES = [(b, s) for b in range(BATCH) for s in range(2)]


def kernel_unfused(**inp):
    inp = {k: np.asarray(v) for k, v in inp.items()}
    ids = list(range(8))
    ncA, _, _ = build_stage_a()
    resA = run_bass_kernel_spmd(ncA, [stage_a_inputs(inp, b, s) for (b, s) in CORES], core_ids=ids)
    x1 = [np.asarray(r["x1T"]) for r in resA.results]
    h1 = [np.asarray(r["h1T"]) for r in resA.results]
    ncB, _, _ = build_stage_b()
    mapsB = []
    for (b, hg) in CORES:
        h1_full = np.ascontiguousarray(np.concatenate([h1[2 * b], h1[2 * b + 1]], axis=1))
        mapsB.append(stage_b_inputs(inp, hg, h1_full))
    resB = run_bass_kernel_spmd(ncB, mapsB, core_ids=ids)
    og = [np.asarray(r["ogT"]) for r in resB.results]
    ncC, _, _ = build_stage_c()
    mapsC = []
    for ci, (b, s) in enumerate(CORES):
        og_own = np.ascontiguousarray(np.concatenate(
            [og[2 * b][:, s * T:(s + 1) * T], og[2 * b + 1][:, s * T:(s + 1) * T]], axis=0))
        mapsC.append(stage_c_inputs(inp, x1[ci], og_own))
    resC = run_bass_kernel_spmd(ncC, mapsC, core_ids=ids)
    out = np.empty((BATCH, SEQ, D), np.float32)
    for ci, (b, s) in enumerate(CORES):
        out[b, s * T:(s + 1) * T, :] = np.asarray(resC.results[ci]["outT"]).T
    return out


PAIRS = [[0, 1], [2, 3], [4, 5], [6, 7]]


def fused_shapes():
    sh = {}
    for k, v in STAGE_A_SHAPES.items():
        sh[k] = (v, F32)
    for k, v in STAGE_B_SHAPES.items():
        if k not in ("h1T_full",):
            sh[k] = v
    for k, v in STAGE_C_SHAPES.items():
        if k not in ("x1T", "ogT_own"):
            sh[k] = v
    sh["m01"] = ([128, 2], F32)
    return sh


def fused_inputs(inp, b, s):
    m = {}
    m.update(stage_a_inputs(inp, b, s))
    mb = stage_b_inputs(inp, s, None)
    mb.pop("h1T_full")
    m.update(mb)
    mc = stage_c_inputs(inp, None, None)
    mc.pop("x1T")
    mc.pop("ogT_own")
    m.update(mc)
    m01 = np.zeros((128, 2), np.float32)
    m01[:, s] = 1.0
    m["m01"] = m01
    return m


def build_fused():
    nc = bass.Bass("TRN2", target_bir_lowering=False)
    d = {k: nc.dram_tensor(k, shp, dt, kind="ExternalInput").ap() for k, (shp, dt) in fused_shapes().items()}
    out = nc.dram_tensor("outT", [D, T], F32, kind="ExternalOutput").ap()
    x1_scr = nc.dram_tensor("x1_scr", [D, T], F32).ap()
    h1_src = [nc.dram_tensor("h1_src%d" % t, [D, 512], BF16).ap() for t in range(NTT)]
    h1_all = [nc.dram_tensor("h1_all%d" % t, [2 * D, 512], BF16).ap() for t in range(NTT)]
    og_src = [nc.dram_tensor("og_src%d" % t, [2 * D, 512], BF16).ap() for t in range(NTT)]
    og_own = [nc.dram_tensor("og_own%d" % t, [D, 512], BF16).ap() for t in range(NTT)]
    P = Prog(nc)
    E = setup_env(nc, P, d)
    sb = E.sb
    stage_a(E, d, out_x1=x1_scr, out_h1=h1_src)
    sb.free(E.xT)
    b_h1all = [Buf() for _ in range(NTT)]
    for t in range(NTT):
        def ag(e, s_, t=t):
            e.collective_compute("AllGather", ALU.bypass, replica_groups=PAIRS, ins=[h1_src[t].opt()],
                                 outs=[h1_all[t].opt()]).then_inc(s_)
        P.add("pool", ag, [E.b_h1_src[t]], [b_h1all[t]], dma=1, semkey="cc_ag", inc=1, selfwait=True)

    def hv(nb):
        return h1_all[nb % 4].rearrange("(r c p) t -> p r c t", r=2, p=128)[:, nb // 4]

    def og_rs(nb):
        return og_src[nb % 4].rearrange("(s f) t -> s f t", s=2)[nb // 4]
    b_ogown = [Buf() for _ in range(NTT)]

    def post_block(nb):
        if nb < 4:
            return
        j = nb - 4

        def rs(e, s_, j=j):
            e.collective_compute("ReduceScatter", ALU.add, replica_groups=PAIRS, ins=[og_src[j].opt()],
                                 outs=[og_own[j].opt()]).then_inc(s_)
        P.add("pool", rs, E.bog_src[j] + E.bog_src[j + 4], [b_ogown[j]], dma=1, semkey="cc_rs", inc=1,
              selfwait=True)
    mark = dict(sb.allocs)
    stage_b(E, d, None, hv=hv, og_rs=og_rs, og_reads=lambda nb: [b_h1all[nb % 4]], post_block=post_block)
    for k_, (v_, st_, nb_, bl_) in list(sb.allocs.items()):
        if k_ not in mark:
            sb.free(v_)
    d["x1T"] = x1_scr
    d["ogT_own"] = og_own
    E.c_reads = b_ogown
    stage_c(E, d, out, load_x=True)
    P.emit(None)
    return nc, P, E


def kernel(**inp):
    inp = {k: np.asarray(v) for k, v in inp.items()}
    nc, _, _ = build_fused()
    res = run_bass_kernel_spmd(nc, [fused_inputs(inp, b, s) for (b, s) in CORES], core_ids=list(range(8)))
    out = np.empty((BATCH, SEQ, D), np.float32)
    for ci, (b, s) in enumerate(CORES):
        out[b, s * T:(s + 1) * T, :] = np.asarray(res.results[ci]["outT"]).T
    return out
```

```python
import os
import numpy as np
from contextlib import ExitStack
import concourse.bass as bass
import concourse.mybir as mybir
from concourse.bass_utils import run_bass_kernel_spmd

F32 = mybir.dt.float32
BF16 = mybir.dt.bfloat16
AF = mybir.ActivationFunctionType
ALU = mybir.AluOpType
AX = mybir.AxisListType

ENGS = ("pe", "act", "dve", "pool", "sp")


class Buf:
    __slots__ = ("name", "last_w", "readers", "const", "excl")

    def __init__(self, name="", excl=False):
        self.name = name
        self.last_w = None
        self.readers = []
        self.const = False
        self.excl = excl


class Instr:
    __slots__ = ("eng", "fn", "deps", "is_dma", "ndma", "semkey", "signal", "sig_sem",
                 "sig_val", "idx", "users", "selfwait")

    def __init__(self, eng, fn):
        self.eng = eng
        self.fn = fn
        self.deps = []
        self.is_dma = False
        self.ndma = 0
        self.semkey = None
        self.signal = False
        self.sig_sem = None
        self.sig_val = 0
        self.users = False
        self.selfwait = False


class Prog:
    def __init__(self, nc):
        self.nc = nc
        self.streams = {e: [] for e in ENGS}
        self.n = 0

    def add(self, eng, fn, reads=(), writes=(), dma=0, semkey=None, inc=16, selfwait=False):
        I = Instr(eng, fn)
        I.selfwait = selfwait
        I.idx = self.n
        self.n += 1
        if dma:
            I.is_dma = True
            I.ndma = dma * inc
            I.semkey = semkey
        raw = {}
        oth = {}
        xb = [b for b in list(reads) + list(writes) if b.excl]
        if xb:
            reads = [b for b in reads if not b.excl]
            writes = [b for b in writes if not b.excl]
            for b in xb:
                J = b.last_w
                if J is not None and J is not I and (J.eng != eng or J.is_dma or I.is_dma):
                    if J not in I.deps:
                        I.deps.append(J)
                        J.users = True
                b.last_w = I
        for b in reads:
            if b.last_w is not None:
                raw[id(b.last_w)] = b.last_w
        for b in writes:
            if b.last_w is not None:
                oth[id(b.last_w)] = b.last_w
            for r in b.readers:
                oth[id(r)] = r
        for k, J in list(raw.items()) + list(oth.items()):
            if J is I or J in I.deps:
                continue
            anydma = J.is_dma or I.is_dma
            if J.eng == eng and not anydma:
                if eng == "pe" or k not in raw:
                    continue
            I.deps.append(J)
            J.users = True
        for b in writes:
            b.last_w = I
            b.readers = []
        for b in reads:
            if b.const:
                continue
            if b.last_w is not I:
                b.readers.append(I)
        self.streams[eng].append(I)
        return I

    def emit(self, engines, final_waits=True):
        nc = self.nc
        with ExitStack() as es:
            esem = {e: es.enter_context(nc.semaphore("s_" + e)) for e in ENGS}
            dsem = {}
            ecount = {e: 0 for e in ENGS}
            dcount = {}
            for e in ENGS:
                for I in self.streams[e]:
                    if I.is_dma:
                        k = I.semkey
                        if k not in dsem:
                            dsem[k] = es.enter_context(nc.semaphore("d_" + str(k)))
                            dcount[k] = 0
                        dcount[k] += I.ndma
                        I.sig_sem = dsem[k]
                        I.sig_val = dcount[k]
                        I.signal = True
                    elif I.users:
                        ecount[e] += 1
                        I.sig_sem = esem[e]
                        I.sig_val = ecount[e]
                        I.signal = True
            self.sem_counts = dict(ecount)
            with nc.Block() as block:
                def run(e, eng):
                    waited = {}
                    for I in self.streams[e]:
                        need = {}
                        for J in I.deps:
                            s = J.sig_sem
                            v = J.sig_val
                            key = id(s)
                            if waited.get(key, 0) >= v:
                                continue
                            if key not in need or need[key][1] < v:
                                need[key] = (s, v)
                        for key, (s, v) in need.items():
                            eng.wait_ge(s, v)
                            waited[key] = v
                        if I.is_dma:
                            I.fn(eng, I.sig_sem)
                            if I.selfwait:
                                eng.wait_ge(I.sig_sem, I.sig_val)
                                waited[id(I.sig_sem)] = I.sig_val
                        else:
                            r = I.fn(eng)
                            if I.signal:
                                r.then_inc(I.sig_sem, 1)
                    if final_waits and e == "sp":
                        for k, s in dsem.items():
                            eng.wait_ge(s, dcount[k])
                        for e2 in ENGS:
                            if ecount[e2] > 0:
                                eng.wait_ge(esem[e2], ecount[e2])

                @block.tensor
                def _(eng):
                    run("pe", eng)

                @block.scalar
                def _(eng):
                    run("act", eng)

                @block.vector
                def _(eng):
                    run("dve", eng)

                @block.gpsimd
                def _(eng):
                    run("pool", eng)

                @block.sync
                def _(eng):
                    run("sp", eng)


class SbufAlloc:
    def __init__(self, nc, nbytes):
        self.t32 = nc.alloc_sbuf_tensor("arena", [128, nbytes // 4], F32)
        self.t16 = self.t32.bitcast(BF16)
        self.free_list = [(0, nbytes)]
        self.allocs = {}
        self.dead = []
        self.last = None
        self.used = 0
        self.hi = 0

    def alloc(self, shape, dtype, name=None):
        esz = 4 if dtype == F32 else 2
        n = 1
        for s in shape[1:]:
            n *= s
        nbytes = (n * esz + 63) // 64 * 64
        for k, (st, sz) in enumerate(self.free_list):
            if sz >= nbytes:
                break
        else:
            raise AssertionError("SBUF overflow %s need %d free %s" % (name, nbytes, self.free_list))
        if sz == nbytes:
            self.free_list.pop(k)
        else:
            self.free_list[k] = (st + nbytes, sz - nbytes)
        base = self.t32 if dtype == F32 else self.t16
        e0 = st // esz
        v = base[0:shape[0], e0:e0 + n]
        if len(shape) == 3:
            v = v.rearrange("p (a b) -> p a b", a=shape[1])
        elif len(shape) == 4:
            v = v.rearrange("p (a b c) -> p a b c", a=shape[1], b=shape[2])
        self.allocs[id(v)] = (v, st, nbytes, [])
        self.last = id(v)
        self.used += nbytes
        self.hi = max(self.hi, self.used)
        return v

    def newbuf(self, name=""):
        b = Buf(name)
        v, st, nb, bl = self.allocs[self.last]
        for (db, ds, de) in self.dead:
            if ds < st + nb and st < de:
                if db.last_w is not None:
                    b.readers.append(db.last_w)
                b.readers.extend(db.readers)
        bl.append(b)
        return b

    def newbufs(self, n):
        return [self.newbuf() for _ in range(n)]

    def free(self, *aps):
        for ap in aps:
            v, st, nb, bl = self.allocs.pop(id(ap))
            for b in bl:
                self.dead.append((b, st, st + nb))
            self.used -= nb
            fl = self.free_list + [(st, nb)]
            fl.sort()
            out = []
            for (a, z) in fl:
                if out and out[-1][0] + out[-1][1] == a:
                    out[-1] = (out[-1][0], out[-1][1] + z)
                else:
                    out.append((a, z))
            self.free_list = out


def op_mm(P, out, lhsT, rhs, start, stop, reads, writes, tp=None):
    if tp is None:
        P.add("pe", lambda e: e.matmul(out, lhsT, rhs, start=start, stop=stop), reads, writes)
    else:
        P.add("pe", lambda e: e.matmul(out, lhsT, rhs, start=start, stop=stop,
                                       tile_position=tp), reads, writes)


def op_tr(P, out, in_, ident, reads, writes):
    P.add("pe", lambda e: e.transpose(out, in_, ident), reads, writes)


def op_act(P, out, in_, func, reads, writes, bias=None, scale=None, accum=None):
    kw = {}
    if bias is not None:
        kw["bias"] = bias
    if scale is not None:
        kw["scale"] = scale
    if accum is not None:
        kw["accum_out"] = accum
    P.add("act", lambda e: e.activation(out=out, in_=in_, func=func, **kw), reads, writes)


def op_tt(P, eng, out, in0, in1, op, reads, writes):
    P.add(eng, lambda e: e.tensor_tensor(out=out, in0=in0, in1=in1, op=op), reads, writes)


def op_ts(P, eng, out, in0, s1, s2, op0, op1, reads, writes, accum=None):
    if accum is None:
        P.add(eng, lambda e: e.tensor_scalar(out=out, in0=in0, scalar1=s1, scalar2=s2,
                                             op0=op0, op1=op1), reads, writes)
    else:
        P.add(eng, lambda e: e.tensor_scalar(out=out, in0=in0, scalar1=s1, scalar2=s2,
                                             op0=op0, op1=op1, accum_out=accum), reads, writes)


def op_stt(P, out, in0, scalar, in1, op0, op1, reads, writes):
    P.add("dve", lambda e: e.scalar_tensor_tensor(out=out, in0=in0, scalar=scalar, in1=in1,
                                                  op0=op0, op1=op1), reads, writes)


def op_copy(P, eng, out, in_, reads, writes):
    if eng == "act":
        P.add("act", lambda e: e.activation(out=out, in_=in_, func=AF.Copy), reads, writes)
    else:
        P.add(eng, lambda e: e.tensor_copy(out=out, in_=in_), reads, writes)


def op_memset(P, eng, out, val, writes):
    P.add(eng, lambda e: e.memset(out, val), (), writes)


def op_rmax(P, out, in_, reads, writes):
    P.add("dve", lambda e: e.reduce_max(out=out, in_=in_, axis=AX.X), reads, writes)


def op_recip(P, out, in_, reads, writes):
    P.add("dve", lambda e: e.reciprocal(out=out, in_=in_), reads, writes)


def op_dma(P, eng, pairs, reads, writes, semkey):
    def fn(e, s, pairs=pairs):
        for (o, i) in pairs:
            e.dma_start(out=o, in_=i).then_inc(s, 16)
    P.add(eng, fn, reads, writes, dma=len(pairs), semkey=semkey)


D = 1024
NCH = 8
T = 2048
TH = 512
TS = T + TH
NTT = 4
DFF = 4096
EPS = 1e-6
NEG = -30000.0


class Env:
    pass


def setup_env(nc, P, consts_dram, sbuf_bytes=206 * 1024):
    E = Env()
    E.nc = nc
    E.P = P
    E.sb = SbufAlloc(nc, sbuf_bytes)
    sb = E.sb
    E.psall_h = nc.alloc_psum_tensor("psall", [128, 8, 512], F32)
    E.psall = E.psall_h
    E.ps = [E.psall[:, i, :] for i in range(8)]
    E.psall16 = E.psall.bitcast(BF16)
    E._keys = 0

    def key(name, E=E):
        E._keys += 1
        return "%s%d" % (name, E._keys)
    E.key = key
    E.bps = [Buf("ps%d" % i, excl=True) for i in range(8)]
    E.ident_f = sb.alloc([128, 128], F32)
    E.bident_f = sb.newbuf()
    E.ident_b = sb.alloc([128, 128], BF16)
    E.bident_b = sb.newbuf()
    E.ones_b = sb.alloc([128, 128], BF16)
    E.ones_f = sb.alloc([128, 128], F32)
    E.eps = sb.alloc([128, 1], F32)
    E.bconst2 = sb.newbuf()
    op_dma(P, "sp", [(E.ident_f, consts_dram["ident"])], (), [E.bident_f], "const")
    op_dma(P, "pool", [(E.ident_b, consts_dram["ident"])], (), [E.bident_b], "constb")
    op_memset(P, "dve", E.ones_b, 1.0, [E.bconst2])
    op_memset(P, "dve", E.ones_f, 1.0, [E.bconst2])
    op_memset(P, "dve", E.eps, EPS, [E.bconst2])
    return E


def rmsnorm_tile(E, xT, bx, c0, nw, bnw, hout, bh, S, ncols=512, psbank=7):
    P = E.P
    sq, bsq, rt, brt, rstd, brstd = S["sq"], S["bsq"], S["rt"], S["brt"], S["rstd"], S["brstd"]
    ps = E.ps[psbank][:, 0:ncols]
    bp = E.bps[psbank]
    for c in range(NCH):
        op_act(P, sq[:, c, 0:ncols], xT[:, c, c0:c0 + ncols], AF.Square, [bx[c]], [bsq[c]])
    for c in range(NCH):
        op_mm(P, ps, E.ones_b, sq[:, c, 0:ncols], c == 0, c == NCH - 1,
              [bsq[c], E.bconst2], [bp])
    op_act(P, rt[:, 0:ncols], ps, AF.Sqrt, [bp, E.bconst2], [brt], bias=E.eps, scale=1.0 / D)
    op_recip(P, rstd[:, 0:ncols], rt[:, 0:ncols], [brt], [brstd])
    for c in range(NCH):
        op_stt(P, hout[:, c, 0:ncols], xT[:, c, c0:c0 + ncols], nw[:, c:c + 1], rstd[:, 0:ncols],
               ALU.mult, ALU.mult, [bx[c], bnw, brstd], [bh])


def norm_scratch(E):
    sb = E.sb
    S = {}
    S["sq"] = sb.alloc([128, NCH, 512], BF16)
    S["bsq"] = sb.newbufs(NCH)
    S["rt"] = sb.alloc([128, 512], F32)
    S["brt"] = sb.newbuf()
    S["rstd"] = sb.alloc([128, 512], F32)
    S["brstd"] = sb.newbuf()
    return S


def free_norm_scratch(E, S):
    E.sb.free(S["sq"], S["rt"], S["rstd"])


def flat2(ap3):
    return ap3.rearrange("p a b -> p (a b)")


class Rot:
    def __init__(self, lst):
        self.lst = list(lst)
        self.i = 0

    def next(self):
        v = self.lst[self.i % len(self.lst)]
        self.i += 1
        return v


def mlp_block(E, nw_dram, w_up, w_down):
    P, sb = E.P, E.sb
    ps, bps = E.ps, E.bps
    xT, bx = E.xT, E.bx
    nw = sb.alloc([128, NCH], F32)
    bnw = sb.newbuf()
    op_dma(P, "sp", [(nw, nw_dram)], (), [bnw], E.key("nw"))
    hT = sb.alloc([128, NCH, T], BF16)
    bh = sb.newbufs(NTT)
    S = norm_scratch(E)
    for t in range(NTT):
        rmsnorm_tile(E, xT, [bx[c][t] for c in range(NCH)], t * 512, nw, bnw,
                     hT[:, :, t * 512:(t + 1) * 512], bh[t], S)
    free_norm_scratch(E, S)
    G = 4
    NG = DFF // (128 * G)
    upT, bup, wu, bwu, wd, bwd = [], [], [], [], [], []
    for k in range(2):
        upT.append(sb.alloc([128, G, T], BF16))
        bup.append([[sb.newbuf() for t in range(NTT)] for j in range(G)])
        wu.append(sb.alloc([128, NCH, 128 * G], BF16))
        bwu.append(sb.newbuf())
        wd.append(sb.alloc([128, G, D], BF16))
        bwd.append(sb.newbuf())
    rl, brl = [], []
    for r in range(3):
        rl.append(sb.alloc([128, 512], F32))
        brl.append(sb.newbuf())
    rrl = Rot(range(3))
    rup = Rot([0, 1, 2, 3])
    rdn = Rot([4, 5, 6, 7])
    wu_v = w_up.rearrange("(c p) f -> p c f", p=128)
    wd_v = w_down.rearrange("(j p) n -> p j n", p=128)
    ku, kd = E.key("wu"), E.key("wd")
    for g in range(NG):
        k = g % 2
        if g < int(os.environ.get("MLP_NLOAD", "99")):
            op_dma(P, "pool", [(wu[k], wu_v[:, :, g * 128 * G:(g + 1) * 128 * G])], (), [bwu[k]],
                   "%s_%d" % (ku, k))
            op_dma(P, "pool", [(wd[k], wd_v[:, g * G:(g + 1) * G, :])], (), [bwd[k]], "%s_%d" % (kd, k))
        for t in range(NTT):
            tc_ = slice(t * 512, (t + 1) * 512)
            for j in range(G):
                b = rup.next()
                for c in range(NCH):
                    op_mm(P, ps[b], wu[k][:, c, j * 128:(j + 1) * 128], hT[:, c, tc_], c == 0,
                          c == NCH - 1, [bwu[k], bh[t]], [bps[b]])
                r = rrl.next()
                op_act(P, rl[r], ps[b], AF.Relu, [bps[b]], [brl[r]])
                op_tt(P, "pool", upT[k][:, j, tc_], rl[r], rl[r], ALU.mult, [brl[r]], [bup[k][j][t]])
        for t in range(NTT):
            tc_ = slice(t * 512, (t + 1) * 512)
            for m in range(NCH):
                b = rdn.next()
                for j in range(G):
                    op_mm(P, ps[b], wd[k][:, j, m * 128:(m + 1) * 128], upT[k][:, j, tc_], j == 0,
                          j == G - 1, [bwd[k], bup[k][j][t]], [bps[b]])
                op_tt(P, "dve", xT[:, m, tc_], xT[:, m, tc_], ps[b], ALU.add, [bps[b], bx[m][t]],
                      [bx[m][t]])
    sb.free(nw, hT, *upT, *wu, *wd, *rl)


def stage_a(E, d, out_x1=None, out_h1=None):
    P, sb = E.P, E.sb
    ps, bps = E.ps, E.bps
    nw_a = sb.alloc([128, NCH], F32)
    bnw_a = sb.newbuf()
    op_dma(P, "sp", [(nw_a, d["attn_norm"])], (), [bnw_a], E.key("nw"))
    hT_all = sb.alloc([128, NCH, TS], BF16)
    bh = sb.newbufs(5)
    oT = sb.alloc([128, NCH, T], BF16)
    boT = [[sb.newbuf() for t in range(NTT)] for hp in range(NCH)]
    xl, bxl = [], []
    for k in range(2):
        xl.append(sb.alloc([128, NCH, 512], F32))
        bxl.append(sb.newbuf())
    S = norm_scratch(E)
    xin = d["xT_in"].rearrange("(c p) t -> p c t", p=128)
    kx = E.key("xl")
    for tt in range(5):
        k = tt % 2
        op_dma(P, "sp", [(xl[k], xin[:, :, tt * 512:(tt + 1) * 512])], (), [bxl[k]],
               "%s_%d" % (kx, k))
        rmsnorm_tile(E, xl[k], [bxl[k]] * NCH, 0, nw_a, bnw_a,
                     hT_all[:, :, tt * 512:(tt + 1) * 512], bh[tt], S)
    sb.free(xl[0], xl[1])
    free_norm_scratch(E, S)

    mask0 = sb.alloc([128, 640], F32)
    bmask0 = sb.newbuf()
    hmb = sb.alloc([128, 1152], F32)
    bhmb = sb.newbuf()
    op_dma(P, "sp", [(mask0, d["mask0"])], (), [bmask0], E.key("mask0"))
    op_dma(P, "sp", [(hmb, d["hmask"])], (), [bhmb], E.key("hmb"))
    wqkv, bw, Bm, bBm, kT, bk, vtok, bv, qT, bq = [], [], [], [], [], [], [], [], [], []
    for k in range(2):
        wqkv.append(sb.alloc([128, 3, NCH, 128], BF16))
        bw.append(sb.newbuf())
        Bm.append(sb.alloc([128, 2, 640], F32))
        bBm.append(sb.newbuf())
        kT.append(sb.alloc([128, TS], BF16))
        bk.append(sb.newbufs(5))
        vtok.append(sb.alloc([128, TS], BF16))
        bv.append(sb.newbufs(5))
        qT.append(sb.alloc([128, T], BF16))
        bq.append(sb.newbufs(NTT))
    s_sb, bs, p_sb, bp_, pt_sb, bpt, dg, bdg, st, bst = [], [], [], [], [], [], [], [], [], []
    for s in range(4):
        s_sb.append(sb.alloc([128, 640], F32))
        bs.append(sb.newbuf())
        p_sb.append(sb.alloc([128, 640], BF16))
        bp_.append(sb.newbuf())
        pt_sb.append(sb.alloc([128, 640], BF16))
        bpt.append(sb.newbuf())
        dg.append(sb.alloc([128, 128], BF16))
        bdg.append(sb.newbuf())
        st.append(sb.alloc([128, 4], F32))
        bst.append([sb.newbuf(), sb.newbuf(), sb.newbuf()])
    wq_v = d["w_qkv"].rearrange("(c p) (s n) -> p s c n", p=128, s=3)
    kwq, kbm = E.key("wqkv"), E.key("bm")
    S_ps = [flat2(E.psall[:, 0:2, :]), flat2(E.psall[:, 2:4, :])]
    PT_ps = flat2(E.psall[:, 4:6, :])

    def load_pair(hp):
        i = hp % 2
        op_dma(P, "pool", [(wqkv[i][:, s3], wq_v[:, s3, :, hp * 128:(hp + 1) * 128])
                           for s3 in range(3)], (), [bw[i]], "%s_%d" % (kwq, i))
        op_dma(P, "sp", [(Bm[i][:, s], d["btab"][2 * hp + s]) for s in range(2)], (), [bBm[i]],
               "%s_%d" % (kbm, i))
        for s in range(2):
            op_tt(P, "pool", Bm[i][:, s], Bm[i][:, s], mask0, ALU.add, [bBm[i], bmask0], [bBm[i]])

    def proj_items(hp):
        i = hp % 2
        items = []
        for tt in range(5):
            cs = slice(tt * 512, (tt + 1) * 512)

            def kproj(tt=tt, cs=cs):
                for c in range(NCH):
                    op_mm(P, ps[7], wqkv[i][:, 1, c, :], hT_all[:, c, cs], c == 0, c == NCH - 1,
                          [bw[i], bh[tt]], [bps[7]])
                op_copy(P, "act", kT[i][:, cs], ps[7], [bps[7]], [bk[i][tt]])
            items.append(kproj)
            if tt >= 1:
                def qproj(tt=tt, cs=cs):
                    for c in range(NCH):
                        op_mm(P, ps[7], wqkv[i][:, 0, c, :], hT_all[:, c, cs], c == 0,
                              c == NCH - 1, [bw[i], bh[tt]], [bps[7]])
                    op_act(P, qT[i][:, (tt - 1) * 512:tt * 512], ps[7], AF.Copy, [bps[7]],
                           [bq[i][tt - 1]], scale=0.125)
                items.append(qproj)

            def vproj(tt=tt, cs=cs):
                for j in range(4):
                    for c in range(NCH):
                        op_mm(P, ps[7][:, j * 128:(j + 1) * 128],
                              hT_all[:, c, tt * 512 + j * 128:tt * 512 + (j + 1) * 128],
                              wqkv[i][:, 2, c, :], c == 0, c == NCH - 1, [bw[i], bh[tt]], [bps[7]])
                op_copy(P, "act", vtok[i][:, cs], ps[7], [bps[7]], [bv[i][tt]])
            items.append(vproj)
        return items

    def attn_s1(hp, p):
        i = hp % 2
        k0 = 128 * p
        kts = sorted(set([k0 // 512, (k0 + 639) // 512]))
        for s in range(2):
            z = s * 2 + (p % 2)
            rows = slice(s * 64, (s + 1) * 64)
            bS = [bps[2 * s], bps[2 * s + 1]]
            rk = [bq[i][p // 4]] + [bk[i][t_] for t_ in kts]
            op_mm(P, S_ps[s][:, 0:512], qT[i][rows, p * 128:(p + 1) * 128],
                  kT[i][rows, k0:k0 + 512], True, True, rk, bS, tp=(s * 64, 0))
            op_mm(P, S_ps[s][:, 512:640], qT[i][rows, p * 128:(p + 1) * 128],
                  kT[i][rows, k0 + 512:k0 + 640], True, True, rk, bS, tp=(s * 64, 0))
            op_tt(P, "dve", s_sb[z], S_ps[s][:, 0:640], Bm[i][:, s], ALU.add, bS + [bBm[i]], [bs[z]])
            if p < 4:
                op_tt(P, "dve", s_sb[z], s_sb[z], hmb[:, k0:k0 + 640], ALU.add, [bs[z], bhmb], [bs[z]])
            P.add("dve", lambda e, z=z: e.reduce_max(out=st[z][:, 0:1], in_=s_sb[z], axis=AX.X,
                                                    negate=True), [bs[z]], [bst[z][0]])
            op_act(P, p_sb[z], s_sb[z], AF.Exp, [bs[z], bst[z][0]], [bp_[z], bst[z][1]],
                   bias=st[z][:, 0:1], accum=st[z][:, 1:2])
            op_recip(P, st[z][:, 2:3], st[z][:, 1:2], [bst[z][1]], [bst[z][2]])
            op_ts(P, "pool", dg[z], E.ident_b, st[z][:, 2:3], 0.0, ALU.mult, ALU.add,
                  [E.bident_b, bst[z][2]], [bdg[z]])

    def attn_s2(hp, p):
        for s in range(2):
            z = s * 2 + (p % 2)
            for kb in range(5):
                op_mm(P, PT_ps[:, kb * 128:(kb + 1) * 128], p_sb[z][:, kb * 128:(kb + 1) * 128],
                      dg[z], True, True, [bp_[z], bdg[z]], [bps[4], bps[5]])
            op_copy(P, "act", pt_sb[z], PT_ps[:, 0:640], [bps[4], bps[5]], [bpt[z]])

    def attn_s3(hp, p):
        i = hp % 2
        oc = (p % 4) * 128
        for s in range(2):
            z = s * 2 + (p % 2)
            for kb in range(5):
                tl = p + kb
                op_mm(P, ps[6][s * 64:(s + 1) * 64, oc:oc + 128],
                      vtok[i][:, tl * 128 + s * 64:tl * 128 + (s + 1) * 64],
                      pt_sb[z][:, kb * 128:(kb + 1) * 128], kb == 0, kb == 4,
                      [bv[i][tl // 4], bpt[z]], [bps[6]], tp=(0, s * 64))
        op_copy(P, "act", oT[:, hp, p * 128:(p + 1) * 128], ps[6][:, oc:oc + 128], [bps[6]],
                [boT[hp][p // 4]])

    load_pair(0)
    for it in proj_items(0):
        it()
    for hp in range(NCH):
        nxt = []
        if hp + 1 < NCH:
            load_pair(hp + 1)
            nxt = proj_items(hp + 1)
        for step in range(18):
            if step < 16:
                attn_s1(hp, step)
            if 0 <= step - 1 < 16:
                attn_s2(hp, step - 1)
            if 0 <= step - 2 < 16:
                attn_s3(hp, step - 2)
            for _ in range(2):
                if nxt and step >= 1:
                    nxt.pop(0)()
        while nxt:
            nxt.pop(0)()
    sb.free(mask0, hmb, hT_all, *wqkv, *Bm, *kT, *vtok, *qT, *s_sb, *p_sb, *pt_sb, *dg, *st)

    E.xT = sb.alloc([128, NCH, T], F32)
    E.bx = [[sb.newbuf() for t in range(NTT)] for c in range(NCH)]
    xT, bx = E.xT, E.bx
    kx = E.key("xT")
    for t in range(NTT):
        op_dma(P, "sp", [(xT[:, :, t * 512:(t + 1) * 512], xin[:, :, TH + t * 512:TH + (t + 1) * 512])],
               (), [bx[c][t] for c in range(NCH)], "%s_%d" % (kx, t))
    wo = sb.alloc([128, NCH, D], BF16)
    bwo = sb.newbuf()
    op_dma(P, "pool", [(wo, d["w_o"].rearrange("(c p) n -> p c n", p=128))], (), [bwo], E.key("wo"))
    rb = Rot(range(8))
    for t in range(NTT):
        tc_ = slice(t * 512, (t + 1) * 512)
        for m in range(NCH):
            b = rb.next()
            for hp in range(NCH):
                op_mm(P, ps[b], wo[:, hp, m * 128:(m + 1) * 128], oT[:, hp, tc_], hp == 0,
                      hp == NCH - 1, [bwo, boT[hp][t]], [bps[b]])
            op_tt(P, "dve", xT[:, m, tc_], xT[:, m, tc_], ps[b], ALU.add, [bps[b], bx[m][t]],
                  [bx[m][t]])
    sb.free(oT, wo, nw_a)

    mlp_block(E, d["mlp_norm0"], d["w_up0"], d["w_down0"])

    if out_x1 is not None:
        ko = E.key("ox1")
        ov = out_x1.rearrange("(c p) t -> p c t", p=128)
        E.b_x1_scr = []
        for t in range(NTT):
            bscr = Buf()
            E.b_x1_scr.append(bscr)
            op_dma(P, "sp", [(ov[:, :, t * 512:(t + 1) * 512], xT[:, :, t * 512:(t + 1) * 512])],
                   [bx[c][t] for c in range(NCH)], [bscr], "%s_%d" % (ko, t))
    if out_h1 is not None:
        E.b_h1_src = []
        nw = sb.alloc([128, NCH], F32)
        bnw = sb.newbuf()
        op_dma(P, "sp", [(nw, d["dn_norm"])], (), [bnw], E.key("nw"))
        S = norm_scratch(E)
        h1 = [sb.alloc([128, NCH, 512], BF16) for _ in range(2)]
        bh1 = [sb.newbuf() for _ in range(2)]
        if isinstance(out_h1, list):
            hvl = [o_.rearrange("(c p) t -> p c t", p=128) for o_ in out_h1]
        else:
            hv = out_h1.rearrange("(c p) t -> p c t", p=128)
            hvl = [hv[:, :, t * 512:(t + 1) * 512] for t in range(NTT)]
        ko = E.key("oh1")
        for t in range(NTT):
            k = t % 2
            rmsnorm_tile(E, xT, [bx[c][t] for c in range(NCH)], t * 512, nw, bnw, h1[k], bh1[k], S)
            bsrc = Buf()
            E.b_h1_src.append(bsrc)
            op_dma(P, "sp", [(hvl[t], h1[k])], [bh1[k]], [bsrc], "%s_%d" % (ko, k))
        free_norm_scratch(E, S)
        sb.free(nw, *h1)


def _pc(v):
    return np.ascontiguousarray(np.asarray(v, np.float32).reshape(NCH, 128).T)


def host_consts():
    q = np.arange(128)[:, None]
    j = np.arange(640)[None, :]
    idx = np.clip(q - j + 512, -256, 256) + 256
    mask0 = np.zeros((128, 640), np.float32)
    mask0[(q < 64) & (j >= 576)] = NEG
    mask0[(q >= 64) & (j < 64)] = NEG
    return idx, mask0


def stage_a_inputs(inp, b, half):
    idx, mask0 = host_consts()
    x = inp["x"]
    lo = half * T
    xs = np.zeros((TS, D), np.float32)
    if half > 0:
        xs[:TH] = x[b, lo - TH:lo]
    xs[TH:] = x[b, lo:lo + T]
    hm = np.zeros((128, 1152), np.float32)
    if half == 0:
        hm[:, :TH] = NEG
    return {
        "ident": np.eye(128, dtype=np.float32),
        "xT_in": np.ascontiguousarray(xs.T),
        "attn_norm": _pc(inp["attn_norm"][0]),
        "w_qkv": np.ascontiguousarray(inp["attn_w_qkv"][0]),
        "btab": np.ascontiguousarray(inp["attn_rel_bias"][0][:, idx]),
        "mask0": mask0,
        "hmask": hm,
        "w_o": np.ascontiguousarray(inp["attn_w_o"][0]),
        "mlp_norm0": _pc(inp["mlp_norm"][0]),
        "w_up0": np.ascontiguousarray(inp["mlp_w_up"][0]),
        "w_down0": np.ascontiguousarray(inp["mlp_w_down"][0]),
        "dn_norm": _pc(inp["dn_norm"][0]),
    }


STAGE_A_SHAPES = {
    "ident": [128, 128], "xT_in": [D, TS], "attn_norm": [128, NCH], "w_qkv": [D, 3 * D],
    "btab": [16, 128, 640], "mask0": [128, 640], "hmask": [128, 1152], "w_o": [D, D],
    "mlp_norm0": [128, NCH], "w_up0": [D, DFF], "w_down0": [DFF, D], "dn_norm": [128, NCH],
}


def build_stage_a():
    nc = bass.Bass("TRN2", target_bir_lowering=False)
    d = {k: nc.dram_tensor(k, shp, F32, kind="ExternalInput").ap() for k, shp in STAGE_A_SHAPES.items()}
    x1 = nc.dram_tensor("x1T", [D, T], F32, kind="ExternalOutput").ap()
    h1 = nc.dram_tensor("h1T", [D, T], BF16, kind="ExternalOutput").ap()
    P = Prog(nc)
    E = setup_env(nc, P, d)
    stage_a(E, d, out_x1=x1, out_h1=h1)
    P.emit(None)
    return nc, P, E


DBG_N = int(os.environ.get('DBG_N', '99'))
NEU_R = os.environ.get('NEU_R', '0') == '1'
F32R = mybir.dt.float32r


def _r(ap):
    return ap.bitcast(F32R) if NEU_R else ap
SEQ = 4096
NBLK = SEQ // 512
NH = 4
BIG = 30000.0


def stage_b(E, d, out_og, nblk=NBLK, nsc=4, phases=99, hv=None, og_rs=None, og_reads=None, post_block=None):
    P, sb = E.P, E.sb
    ps, bps = E.ps, E.bps
    ps16 = E.psall16
    class _BS:
        def __getitem__(self, s_):
            return bps[s_[0]]
    bslot = _BS()
    rbk = Rot([3, 4, 5, 6, 7])

    def slot32(s_, n=128):
        return E.psall[:, s_[0], s_[1] * 128:s_[1] * 128 + n]

    def slot16(s_):
        return ps16[:, s_[0], s_[1] * 256:s_[1] * 256 + 128]

    triu = sb.alloc([128, 128], F32)
    posm = sb.alloc([128, 128], F32)
    negm = sb.alloc([128, 128], F32)
    bcm = sb.newbuf()
    op_dma(P, "sp", [(triu, d["triu"]), (posm, d["posm"]), (negm, d["negm"])], (), [bcm], E.key("cm"))
    cw = sb.alloc([128, 3 * NH * 4], F32)
    alog = sb.alloc([128, NH], F32)
    dtb = sb.alloc([128, NH], F32)
    hn = sb.alloc([128, 1], F32)
    bsm = sb.newbuf()
    op_dma(P, "sp", [(cw, d["cw"]), (alog, d["alog"]), (dtb, d["dtb"]), (hn, d["hn"])], (), [bsm],
           E.key("sm"))
    nega = sb.alloc([128, NH], F32)
    bnega = sb.newbuf()
    op_act(P, nega, alog, AF.Exp, [bsm], [bnega])
    op_ts(P, "dve", nega, nega, -1.0, None, ALU.mult, ALU.bypass, [bnega], [bnega])
    W = sb.alloc([128, NH * 4, NCH, 128], BF16)
    bW = sb.newbuf()
    wv = d["w_in"].rearrange("(c p) n -> p c n", p=128)
    kW = E.key("W")
    for sec in range(4):
        op_dma(P, "pool", [(W[:, h * 4 + sec], wv[:, :, sec * 512 + h * 128:sec * 512 + (h + 1) * 128])
                           for h in range(NH)], (), [bW], kW)
    wab = sb.alloc([128, NCH, 8], BF16)
    bwab = sb.newbuf()
    op_dma(P, "pool", [(wab, wv[:, :, 2048:2056])], (), [bwab], E.key("wab"))
    dgw = sb.alloc([128, 3 * NH * 4, 128], BF16)
    bdgw = sb.newbuf()
    for idx in range(3 * NH * 4):
        op_ts(P, "pool", dgw[:, idx, :], E.ident_b, cw[:, idx:idx + 1], 0.0, ALU.mult, ALU.add,
              [E.bident_b, bsm], [bdgw])
    pcb = sb.alloc([128, 3 * NH, 516], BF16)
    bpcb = [[sb.newbuf() for s in range(3)] for h in range(NH)]
    for h in range(NH):
        for s in range(3):
            op_memset(P, "pool", pcb[:, h * 3 + s, 0:3], 0.0, [bpcb[h][s]])
    S = sb.alloc([128, NH, 128], F32)
    Sb = sb.alloc([128, NH, 128], BF16)
    bS = sb.newbufs(NH)
    bSb = [Buf() for _ in range(NH)]
    for h in range(NH):
        op_memset(P, "pool", S[:, h, :], 0.0, [bS[h]])
        op_memset(P, "pool", Sb[:, h, :], 0.0, [bSb[h]])
    hblk = [sb.alloc([128, NCH, 512], BF16) for _ in range(2)]
    bhblk = [Buf(), Buf()]
    ab_sb = sb.alloc([128, 4, 8], F32)
    bab = sb.newbuf()
    gsb = sb.alloc([128, 4, NH], F32)
    beta = sb.alloc([128, 4, NH], F32)
    bg = sb.newbufs(4)
    bbeta = [Buf() for _ in range(4)]
    tmpg = sb.alloc([128, 8, NH], F32)
    btmpg = sb.newbuf()
    y = [sb.alloc([128, 2, 512], F32) for _ in range(NH)]
    by = [[Buf() for s in range(2)] for _ in range(NH)]
    vT = sb.alloc([128, NH, 512], BF16)
    bvT = sb.newbufs(NH)
    sz = sb.alloc([128, NH, 512], F32)
    bsz = sb.newbufs(NH)
    sq = [sb.alloc([128, 512], BF16) for _ in range(2)]
    bsq = [Buf(), Buf()]
    lnv = [sb.alloc([128, 512], F32) for _ in range(2)]
    blnv = [Buf(), Buf()]
    qn = sb.alloc([128, NH, 512], BF16)
    kn = sb.alloc([128, NH, 512], BF16)
    bqn = sb.newbufs(NH)
    bkn = [Buf() for _ in range(NH)]
    og = [sb.alloc([128, NH, 512], BF16) for _ in range(2)]
    bog = [[Buf() for h in range(NH)] for _ in range(2)]
    sm = sb.alloc([128, 8, NH], F32)
    bsmv = sb.newbufs(8)
    onrm = sb.alloc([128, NH, 4], F32)
    bonrm = [sb.newbufs(3) for _ in range(NH)]
    MB = {}
    names32 = ["gB", "ts", "Ds", "ti", "Dti", "Egc", "u", "L", "U", "Lp0", "Lp1", "Up0", "Up1", "P0", "P1"]
    names16 = ["AT", "kd", "kbd", "vb", "qd", "TTb", "wT", "vnew", "on", "junk"]
    for h in range(NH):
        for nm in names32:
            MB[(nm, h)] = (sb.alloc([128, 128], F32), sb.newbuf())
        for nm in names16:
            MB[(nm, h)] = (sb.alloc([128, 128], BF16), sb.newbuf())

    if hv is None:
        hv_ = d["h1T_full"].rearrange("(c p) t -> p c t", p=128)

        def hv(nb):
            return hv_[:, :, nb * 512:(nb + 1) * 512]
    if og_rs is not None:
        m01 = sb.alloc([128, 2], F32)
        bm01 = sb.newbuf()
        op_dma(P, "sp", [(m01, d["m01"])], (), [bm01], E.key("m01"))
        ogm = [sb.alloc([128, NH, 512], BF16) for _ in range(2)]
        bogm = [sb.newbuf() for _ in range(2)]
        E.bog_src = []
    khb = E.key("hblk")
    kog = E.key("og")
    rproj = Rot([0, 1])
    ones_col = E.ones_f[:, 0:1]

    def load_hblk(nb):
        op_dma(P, "sp", [(hblk[nb % 2], hv(nb))], (og_reads(nb) if og_reads else ()), [bhblk[nb % 2]],
               "%s_%d" % (khb, nb % 2))
    load_hblk(0)
    for nb in range(nblk):
        kb_ = nb % 2
        if nb + 1 < nblk:
            load_hblk(nb + 1)
        s_ab = (rbk.next(), 0)
        for j in range(4):
            for c in range(NCH):
                op_mm(P, slot32(s_ab)[:, j * 8:(j + 1) * 8], hblk[kb_][:, c, j * 128:(j + 1) * 128],
                      wab[:, c, :], c == 0, c == NCH - 1, [bhblk[kb_], bwab], [bslot[s_ab]])
        op_copy(P, "dve", flat2(ab_sb), slot32(s_ab)[:, 0:32], [bslot[s_ab]], [bab])
        for j in range(4):
            a_ = ab_sb[:, j, 0:NH]
            b_ = ab_sb[:, j, NH:2 * NH]
            t0, t1, t2, t3, t4, t5 = (tmpg[:, i_, :] for i_ in range(6))
            op_tt(P, "dve", t0, a_, dtb, ALU.add, [bab, bsm], [btmpg])
            op_act(P, t1, t0, AF.Abs, [btmpg], [btmpg])
            op_act(P, t2, t1, AF.Exp, [btmpg], [btmpg], scale=-1.0)
            op_act(P, t3, t2, AF.Ln, [btmpg], [btmpg], bias=ones_col)
            op_stt(P, t4, t0, 0.0, t3, ALU.max, ALU.add, [btmpg], [btmpg])
            op_tt(P, "dve", gsb[:, j, :], t4, nega, ALU.mult, [btmpg, bnega], [bg[j]])
            op_act(P, t5, b_, AF.Exp, [bab], [btmpg], scale=-1.0)
            op_ts(P, "dve", t5, t5, 1.0, None, ALU.add, ALU.bypass, [btmpg], [btmpg])
            op_recip(P, beta[:, j, :], t5, [btmpg], [bbeta[j]])
        for h in range(NH):
            yi = h
            for sec in range(3):
                pb = rproj.next()
                for c in range(NCH):
                    op_mm(P, ps[pb], W[:, h * 4 + sec, c, :], hblk[kb_][:, c, :], c == 0,
                          c == NCH - 1, [bW, bhblk[kb_]], [bps[pb]])
                pc = pcb[:, h * 3 + sec, :]
                op_copy(P, "act", pc[:, 3:515], ps[pb], [bps[pb]], [bpcb[h][sec]])
                for k in range(4):
                    op_mm(P, ps[2], dgw[:, (sec * NH + h) * 4 + k, :], pc[:, k:k + 512], k == 0,
                          k == 3, [bdgw, bpcb[h][sec]], [bps[2]])
                if sec < 2:
                    op_act(P, y[yi][:, sec, :], ps[2], AF.Silu, [bps[2]], [by[yi][sec]])
                else:
                    op_act(P, vT[:, h, :], ps[2], AF.Silu, [bps[2]], [bvT[h]])
                op_copy(P, "pool", pc[:, 0:3], pc[:, 512:515], [bpcb[h][sec]], [bpcb[h][sec]])
            pb = rproj.next()
            for c in range(NCH):
                op_mm(P, ps[pb], W[:, h * 4 + 3, c, :], hblk[kb_][:, c, :], c == 0, c == NCH - 1,
                      [bW, bhblk[kb_]], [bps[pb]])
            op_act(P, sz[:, h, :], ps[pb], AF.Silu, [bps[pb]], [bsz[h]])
        for h in range(NH):
            yi = h
            for sec in range(2):
                op_tt(P, "pool", sq[sec], y[yi][:, sec, :], y[yi][:, sec, :], ALU.mult,
                      [by[yi][sec]], [bsq[sec]])
                op_mm(P, ps[2], E.ones_b, sq[sec], True, True, [E.bconst2, bsq[sec]], [bps[2]])
                op_act(P, lnv[sec], ps[2], AF.Ln, [bps[2], E.bconst2], [blnv[sec]], bias=E.eps)
                op_act(P, lnv[sec], lnv[sec], AF.Exp, [blnv[sec]], [blnv[sec]], scale=-0.5)
                dst, bdst = (qn, bqn) if sec == 0 else (kn, bkn)
                op_stt(P, dst[:, h, :], y[yi][:, sec, :], (128.0 ** -0.5) if sec == 0 else 1.0,
                       lnv[sec], ALU.mult, ALU.mult, [by[yi][sec], blnv[sec]], [bdst[h]])
        oi = nb % 2
        for j in range(nsc):
            cs = slice(j * 128, (j + 1) * 128)
            bk_g = rbk.next()
            s_gc, s_gl = (bk_g, 0), (bk_g, 1)
            op_mm(P, slot32(s_gc, NH), triu, gsb[:, j, :], True, True, [bcm, bg[j]], [bslot[s_gc]])
            op_mm(P, slot32(s_gl, NH), E.ones_f, gsb[:, j, :], True, True, [E.bconst2, bg[j]],
                  [bslot[s_gl]])
            gc, gl, egc, ebk, dk, ekd, glast = (sm[:, i_, :] for i_ in range(7))
            op_copy(P, "dve", gc, slot32(s_gc, NH), [bslot[s_gc]], [bsmv[0]])
            op_copy(P, "dve", gl, slot32(s_gl, NH), [bslot[s_gl]], [bsmv[1]])
            op_act(P, egc, gc, AF.Exp, [bsmv[0]], [bsmv[2]])
            op_tt(P, "dve", ebk, beta[:, j, :], egc, ALU.mult, [bbeta[j], bsmv[2]], [bsmv[3]])
            op_tt(P, "dve", dk, gl, gc, ALU.subtract, [bsmv[0], bsmv[1]], [bsmv[4]])
            op_act(P, ekd, dk, AF.Exp, [bsmv[4]], [bsmv[5]])
            op_act(P, glast, gl, AF.Exp, [bsmv[1]], [bsmv[6]])

            st_ = [dict() for _ in range(NH)]

            def ph_a(h, BK):
                X = st_[h]
                gB, bgB = MB[("gB", h)]
                op_ts(P, "pool", gB, E.ones_f, gsb[:, j, h:h + 1], 0.0, ALU.mult, ALU.add,
                      [E.bconst2, bg[j]], [bgB])
                X["gbc"] = (BK["gbc"], h)
                op_mm(P, slot32(X["gbc"]), gB, triu, True, True, [bgB, bcm], [bslot[X["gbc"]]])
                X["kk"] = (BK["kk"], h)
                op_mm(P, slot32(X["kk"]), kn[:, h, cs], kn[:, h, cs], True, True, [bkn[h]],
                      [bslot[X["kk"]]])
                X["qk"] = (BK["qk"], h)
                op_mm(P, slot32(X["qk"]), kn[:, h, cs], qn[:, h, cs], True, True, [bkn[h], bqn[h]],
                      [bslot[X["qk"]]])
                X["kt"] = (BK["kt"], h)
                op_tr(P, slot16(X["kt"]), kn[:, h, cs], E.ident_b, [bkn[h], E.bident_b],
                      [bslot[X["kt"]]])
                X["vt"] = (BK["vt"], h)
                op_tr(P, slot16(X["vt"]), vT[:, h, cs], E.ident_b, [bvT[h], E.bident_b],
                      [bslot[X["vt"]]])

            def ph_b(h, BK):
                X = st_[h]
                G = slot32(X["gbc"])
                bG = bslot[X["gbc"]]
                ts, bts = MB[("ts", h)]
                Ds, bDs = MB[("Ds", h)]
                ti, bti = MB[("ti", h)]
                Dti, bDti = MB[("Dti", h)]
                Egc, bEgc = MB[("Egc", h)]
                Gs, bGs = MB[("gB", h)]
                op_copy(P, "dve", Gs, G, [bG], [bGs])
                op_stt(P, ts, Gs, sm[:, 0, h:h + 1], posm, ALU.subtract, ALU.max, [bGs, bsmv[0], bcm], [bts])
                op_act(P, Ds, ts, AF.Exp, [bts], [bDs], scale=-1.0)
                op_stt(P, ti, Gs, sm[:, 0, h:h + 1], negm, ALU.subtract, ALU.min, [bGs, bsmv[0], bcm], [bti])
                op_act(P, Dti, ti, AF.Exp, [bti], [bDti])
                op_act(P, Egc, Gs, AF.Exp, [bGs], [bEgc])
                L, bL = MB[("L", h)]
                if DBG_N > 5:
                    op_stt(P, _r(L), slot32(X["kk"]), beta[:, j, h:h + 1], Ds, ALU.mult, ALU.mult,
                           [bslot[X["kk"]], bbeta[j], bDs], [bL])
                AT, bAT = MB[("AT", h)]
                if DBG_N > 6:
                    op_tt(P, "dve", AT, slot32(X["qk"]), Dti, ALU.mult, [bslot[X["qk"]], bDti], [bAT])
                kd, bkd = MB[("kd", h)]
                kbd, bkbd = MB[("kbd", h)]
                vb, bvb = MB[("vb", h)]
                if DBG_N > 7:
                    op_act(P, kd, slot16(X["kt"]), AF.Copy, [bslot[X["kt"]], bsmv[5]], [bkd],
                           scale=sm[:, 5, h:h + 1])
                if DBG_N > 8:
                    op_act(P, kbd, slot16(X["kt"]), AF.Copy, [bslot[X["kt"]], bsmv[3]], [bkbd],
                           scale=sm[:, 3, h:h + 1])
                if DBG_N > 9:
                    op_act(P, vb, slot16(X["vt"]), AF.Copy, [bslot[X["vt"]], bbeta[j]], [bvb],
                           scale=beta[:, j, h:h + 1])
                qd, bqd = MB[("qd", h)]
                if DBG_N > 10:
                    op_tt(P, "pool", qd, qn[:, h, cs], Egc, ALU.mult, [bqn[h], bEgc], [bqd])

            def ph_c(h, BK):
                X = st_[h]
                L, bL = MB[("L", h)]
                U, bU = MB[("U", h)]
                X["ut"] = (BK["ut"], h)
                op_mm(P, slot32(X["ut"]), L, E.ident_f, True, True, [bL, E.bident_f], [bslot[X["ut"]]])
                op_copy(P, "act", _r(U), slot32(X["ut"]), [bslot[X["ut"]]], [bU])
                P0, bP0 = MB[("P0", h)]
                op_tt(P, "pool", _r(P0), E.ident_f, U, ALU.subtract, [E.bident_f, bU], [bP0])
                X["Lp"], X["Up"], X["Pm"] = ("L", h), ("U", h), ("P0", h)

            def ph_pow1(h, BK, m):
                X = st_[h]
                Lp, bLp = MB[X["Lp"]]
                Up, bUp = MB[X["Up"]]
                nl = ("Lp%d" % (m % 2), h)
                nu = ("Up%d" % (m % 2), h)
                s1 = (BK["s1"], h)
                op_mm(P, slot32(s1), _r(Up), _r(Lp), True, True, [bUp, bLp], [bslot[s1]])
                op_copy(P, "act", _r(MB[nl][0]), slot32(s1), [bslot[s1]], [MB[nl][1]])
                if m < 6:
                    s2 = (BK["s2"], h)
                    op_mm(P, slot32(s2), _r(Lp), _r(Up), True, True, [bUp, bLp], [bslot[s2]])
                    op_copy(P, "dve", _r(MB[nu][0]), slot32(s2), [bslot[s2]], [MB[nu][1]])
                    X["Up"] = nu
                X["Lp"] = nl

            def ph_pow2(h, BK, m):
                X = st_[h]
                nl = X["Lp"]
                Pm, bPm = MB[X["Pm"]]
                npm = ("P%d" % (m % 2), h)
                s3 = (BK["s3"], h)
                op_mm(P, slot32(s3), _r(MB[nl][0]), _r(Pm), True, True, [MB[nl][1], bPm], [bslot[s3]])
                op_tt(P, "dve", _r(MB[npm][0]), Pm, slot32(s3), ALU.add, [bPm, bslot[s3]], [MB[npm][1]])
                X["Pm"] = npm

            def ph_d(h, BK):
                X = st_[h]
                TT_, bTT = MB[("TTb", h)]
                op_copy(P, "pool", TT_, MB[X["Pm"]][0], [MB[X["Pm"]][1]], [bTT])
                vb, bvb = MB[("vb", h)]
                kbd, bkbd = MB[("kbd", h)]
                u, bu = MB[("u", h)]
                wT, bwT = MB[("wT", h)]
                s1 = (BK["s1"], h)
                op_mm(P, slot32(s1), TT_, vb, True, True, [bTT, bvb], [bslot[s1]])
                op_copy(P, "act", u, slot32(s1), [bslot[s1]], [bu])
                s2 = (BK["s2"], h)
                op_mm(P, slot32(s2), kbd, TT_, True, True, [bTT, bkbd], [bslot[s2]])
                op_copy(P, "dve", wT, slot32(s2), [bslot[s2]], [bwT])

            def ph_e1(h, BK):
                u, bu = MB[("u", h)]
                wT, bwT = MB[("wT", h)]
                vnew, bvn = MB[("vnew", h)]
                s1 = (BK["s1"], h)
                op_mm(P, slot32(s1), wT, Sb[:, h, :], True, True, [bwT, bSb[h]], [bslot[s1]])
                op_tt(P, "dve", vnew, u, slot32(s1), ALU.subtract, [bu, bslot[s1]], [bvn])

            def ph_e2(h, BK):
                X = st_[h]
                vnew, bvn = MB[("vnew", h)]
                qd, bqd = MB[("qd", h)]
                AT, bAT = MB[("AT", h)]
                kd, bkd = MB[("kd", h)]
                X["o"] = (BK["o"], h)
                op_mm(P, slot32(X["o"]), qd, Sb[:, h, :], True, False, [bqd, bSb[h]], [bslot[X["o"]]])
                op_mm(P, slot32(X["o"]), AT, vnew, False, True, [bAT, bvn], [bslot[X["o"]]])
                s2 = (BK["s2"], h)
                op_mm(P, slot32(s2), kd, vnew, True, True, [bkd, bvn], [bslot[s2]])
                op_stt(P, S[:, h, :], S[:, h, :], sm[:, 6, h:h + 1], slot32(s2), ALU.mult, ALU.add,
                       [bS[h], bsmv[6], bslot[s2]], [bS[h]])
                op_copy(P, "pool", Sb[:, h, :], S[:, h, :], [bS[h]], [bSb[h]])

            def ph_f(h, BK):
                X = st_[h]
                o_ps = slot32(X["o"])
                bo = bslot[X["o"]]
                junk, bjunk = MB[("junk", h)]
                on, bon = MB[("on", h)]
                ssq, lv, rstd = onrm[:, h, 0:1], onrm[:, h, 1:2], onrm[:, h, 2:3]
                b0, b1, b2 = bonrm[h]
                op_act(P, junk, o_ps, AF.Square, [bo], [bjunk, b0], accum=ssq)
                op_act(P, lv, ssq, AF.Ln, [b0, E.bconst2], [b1], bias=E.eps, scale=1.0 / 128)
                op_act(P, rstd, lv, AF.Exp, [b1], [b2], scale=-0.5)
                op_act(P, on, o_ps, AF.Copy, [bo, b2], [bon], scale=rstd)
                s1 = (BK["s1"], h)
                op_tr(P, slot16(s1), on, E.ident_b, [bon, E.bident_b], [bslot[s1]])
                op_stt(P, og[oi][:, h, cs], slot16(s1), hn[:, 0:1], sz[:, h, cs], ALU.mult, ALU.mult,
                       [bslot[s1], bsm, bsz[h]], [bog[oi][h]])

            phl = [(ph_a, ("gbc", "kk", "qk", "kt", "vt")), (ph_b, ()), (ph_c, ("ut",))]
            for m in range(1, 7):
                phl.append((lambda h, BK, m=m: ph_pow1(h, BK, m), ("s1", "s2")))
                phl.append((lambda h, BK, m=m: ph_pow2(h, BK, m), ("s3",)))
            phl += [(ph_d, ("s1", "s2")), (ph_e1, ("s1",)), (ph_e2, ("o", "s2")), (ph_f, ("s1",))]
            BKo = None
            for (ph, names) in phl[:phases]:
                BK = {nm: rbk.next() for nm in names}
                if "o" in BK:
                    BKo = BK["o"]
                if ph is ph_f:
                    BK["o"] = BKo
                for h in range(NH):
                    ph(h, BK)
        if og_rs is None:
            ov = out_og.rearrange("(h p) t -> p h t", p=128)
            op_dma(P, "sp", [(ov[:, :, nb * 512:(nb + 1) * 512], og[oi])], bog[oi], (), "%s_%d" % (kog, oi))
        else:
            ovs = og_rs(nb).rearrange("(g h p) t -> p g h t", g=2, p=128)
            blk_bufs = []
            for g_ in range(2):
                bsrc = Buf()
                blk_bufs.append(bsrc)
                op_ts(P, "pool", ogm[g_], og[oi], m01[:, g_:g_ + 1], 0.0, ALU.mult, ALU.add,
                      bog[oi] + [bm01], [bogm[g_]])
                op_dma(P, "sp", [(ovs[:, g_], ogm[g_])], [bogm[g_]], [bsrc], "%s_%d" % (kog, g_))
            E.bog_src.append(blk_bufs)
            if post_block is not None:
                post_block(nb)
    if "dbg" in d:
        hh = int(os.environ.get("DBG_H", "3"))
        names = ["L", "U", "AT", "TTb", "u", "vnew", "kd", "kbd", "vb", "qd", "Ds", "Dti"]
        dbg = sb.alloc([128, 128 * len(names)], F32)
        bdbg = sb.newbuf()
        for ii, nm in enumerate(names):
            op_copy(P, "dve", dbg[:, ii * 128:(ii + 1) * 128], MB[(nm, hh)][0], [MB[(nm, hh)][1]], [bdbg])
        op_dma(P, "sp", [(d["dbg"], dbg)], [bdbg], (), E.key("dbg"))


STAGE_B_SHAPES = {
    "ident": ([128, 128], F32), "h1T_full": ([D, SEQ], BF16), "w_in": ([D, 2056], F32),
    "cw": ([128, 48], F32), "alog": ([128, NH], F32), "dtb": ([128, NH], F32), "hn": ([128, 1], F32),
    "triu": ([128, 128], F32), "posm": ([128, 128], F32), "negm": ([128, 128], F32),
}


def stage_b_inputs(inp, hg, h1T_full=None):
    hs = slice(hg * NH, (hg + 1) * NH)
    w = inp["dn_w_in"][0]
    cols = []
    for sec in range(4):
        cols.append(w[:, sec * 1024 + hg * 512:sec * 1024 + (hg + 1) * 512])
    cols.append(w[:, 4096 + hg * NH:4096 + (hg + 1) * NH])
    cols.append(w[:, 4104 + hg * NH:4104 + (hg + 1) * NH])
    cwf = inp["dn_conv_w"][0]
    cw = cwf.reshape(4, 3, 8, 128)[:, :, hs, :]
    cw = np.ascontiguousarray(cw.transpose(3, 1, 2, 0)).reshape(128, 48)
    ii = np.arange(128)[:, None]
    jj = np.arange(128)[None, :]
    return {
        "ident": np.eye(128, dtype=np.float32),
        "h1T_full": h1T_full,
        "w_in": np.ascontiguousarray(np.concatenate(cols, axis=1)),
        "cw": cw,
        "alog": np.ascontiguousarray(np.broadcast_to(inp["dn_a_log"][0][hs][None, :], (128, NH))),
        "dtb": np.ascontiguousarray(np.broadcast_to(inp["dn_dt_bias"][0][hs][None, :], (128, NH))),
        "hn": np.ascontiguousarray(inp["dn_head_norm"][0].reshape(128, 1)),
        "triu": (ii <= jj).astype(np.float32),
        "posm": np.where(ii > jj, 0.0, BIG).astype(np.float32),
        "negm": np.where(jj >= ii, 0.0, -BIG).astype(np.float32),
    }


def build_stage_b(**kw):
    nc = bass.Bass("TRN2", target_bir_lowering=False)
    d = {k: nc.dram_tensor(k, shp, dt, kind="ExternalInput").ap() for k, (shp, dt) in STAGE_B_SHAPES.items()}
    og = nc.dram_tensor("ogT", [NH * 128, SEQ], BF16, kind="ExternalOutput").ap()
    if os.environ.get("DBG_OUT"):
        d["dbg"] = nc.dram_tensor("dbg", [128, 128 * 12], F32, kind="ExternalOutput").ap()
    P = Prog(nc)
    E = setup_env(nc, P, d)
    stage_b(E, d, og, **kw)
    P.emit(None)
    return nc, P, E


def oproj_residual(E, oT, boT, wo_dram):
    P, sb = E.P, E.sb
    ps, bps = E.ps, E.bps
    xT, bx = E.xT, E.bx
    wo = sb.alloc([128, NCH, D], BF16)
    bwo = sb.newbuf()
    op_dma(P, "pool", [(wo, wo_dram.rearrange("(c p) n -> p c n", p=128))], (), [bwo], E.key("wo"))
    rb = Rot(range(8))
    for t in range(NTT):
        tc_ = slice(t * 512, (t + 1) * 512)
        for m in range(NCH):
            b = rb.next()
            for hp in range(NCH):
                op_mm(P, ps[b], wo[:, hp, m * 128:(m + 1) * 128], oT[:, hp, tc_], hp == 0,
                      hp == NCH - 1, [bwo, boT[hp][t]], [bps[b]])
            op_tt(P, "dve", xT[:, m, tc_], xT[:, m, tc_], ps[b], ALU.add, [bps[b], bx[m][t]],
                  [bx[m][t]])
    sb.free(wo)


def stage_c(E, d, out, load_x=True):
    P, sb = E.P, E.sb
    if load_x:
        E.xT = sb.alloc([128, NCH, T], F32)
        E.bx = [[sb.newbuf() for t in range(NTT)] for c in range(NCH)]
        xin = d["x1T"].rearrange("(c p) t -> p c t", p=128)
        kx = E.key("xT")
        for t in range(NTT):
            op_dma(P, "sp", [(E.xT[:, :, t * 512:(t + 1) * 512], xin[:, :, t * 512:(t + 1) * 512])],
                   ([E.b_x1_scr[t]] if hasattr(E, "b_x1_scr") else ()),
                   [E.bx[c][t] for c in range(NCH)], "%s_%d" % (kx, t))
    xT, bx = E.xT, E.bx
    oT = sb.alloc([128, NCH, T], BF16)
    boT = [[sb.newbuf() for t in range(NTT)] for hp in range(NCH)]
    if isinstance(d["ogT_own"], list):
        ovl = [o_.rearrange("(c p) t -> p c t", p=128) for o_ in d["ogT_own"]]
    else:
        ov = d["ogT_own"].rearrange("(c p) t -> p c t", p=128)
        ovl = [ov[:, :, t * 512:(t + 1) * 512] for t in range(NTT)]
    ko = E.key("og")
    for t in range(NTT):
        cr = [E.c_reads[t]] if hasattr(E, "c_reads") else ()
        op_dma(P, "sp", [(oT[:, :, t * 512:(t + 1) * 512], ovl[t])],
               cr, [boT[hp][t] for hp in range(NCH)], "%s_%d" % (ko, t))
    oproj_residual(E, oT, boT, d["dn_w_o"])
    sb.free(oT)
    mlp_block(E, d["mlp_norm1"], d["w_up1"], d["w_down1"])
    nw = sb.alloc([128, NCH], F32)
    bnw = sb.newbuf()
    op_dma(P, "sp", [(nw, d["final_norm"])], (), [bnw], E.key("nw"))
    S = norm_scratch(E)
    yo = [sb.alloc([128, NCH, 512], F32) for _ in range(2)]
    byo = [sb.newbuf() for _ in range(2)]
    outv = out.rearrange("(c p) t -> p c t", p=128)
    ko = E.key("out")
    for t in range(NTT):
        k = t % 2
        rmsnorm_tile(E, xT, [bx[c][t] for c in range(NCH)], t * 512, nw, bnw, yo[k], byo[k], S)
        op_dma(P, "sp", [(outv[:, :, t * 512:(t + 1) * 512], yo[k])], [byo[k]], (), "%s_%d" % (ko, k))
    free_norm_scratch(E, S)
    sb.free(nw, *yo)


STAGE_C_SHAPES = {
    "ident": ([128, 128], F32), "x1T": ([D, T], F32), "ogT_own": ([D, T], BF16),
    "dn_w_o": ([D, D], F32), "mlp_norm1": ([128, NCH], F32), "w_up1": ([D, DFF], F32),
    "w_down1": ([DFF, D], F32), "final_norm": ([128, NCH], F32),
}


def stage_c_inputs(inp, x1T=None, ogT_own=None):
    return {
        "ident": np.eye(128, dtype=np.float32),
        "x1T": x1T,
        "ogT_own": ogT_own,
        "dn_w_o": np.ascontiguousarray(inp["dn_w_o"][0]),
        "mlp_norm1": _pc(inp["mlp_norm"][1]),
        "w_up1": np.ascontiguousarray(inp["mlp_w_up"][1]),
        "w_down1": np.ascontiguousarray(inp["mlp_w_down"][1]),
        "final_norm": _pc(inp["final_norm"]),
    }


def build_stage_c():
    nc = bass.Bass("TRN2", target_bir_lowering=False)
    d = {k: nc.dram_tensor(k, shp, dt, kind="ExternalInput").ap() for k, (shp, dt) in STAGE_C_SHAPES.items()}
    out = nc.dram_tensor("outT", [D, T], F32, kind="ExternalOutput").ap()
    P = Prog(nc)
    E = setup_env(nc, P, d)
    stage_c(E, d, out)
    P.emit(None)
    return nc, P, E


BATCH = 4
CORES = [(b, s) for b in range(BATCH) for s in range(2)]


def kernel_unfused(**inp):
    inp = {k: np.asarray(v) for k, v in inp.items()}
    ids = list(range(8))
    ncA, _, _ = build_stage_a()
    resA = run_bass_kernel_spmd(ncA, [stage_a_inputs(inp, b, s) for (b, s) in CORES], core_ids=ids)
    x1 = [np.asarray(r["x1T"]) for r in resA.results]
    h1 = [np.asarray(r["h1T"]) for r in resA.results]
    ncB, _, _ = build_stage_b()
    mapsB = []
    for (b, hg) in CORES:
        h1_full = np.ascontiguousarray(np.concatenate([h1[2 * b], h1[2 * b + 1]], axis=1))
        mapsB.append(stage_b_inputs(inp, hg, h1_full))
    resB = run_bass_kernel_spmd(ncB, mapsB, core_ids=ids)
    og = [np.asarray(r["ogT"]) for r in resB.results]
    ncC, _, _ = build_stage_c()
    mapsC = []
    for ci, (b, s) in enumerate(CORES):
        og_own = np.ascontiguousarray(np.concatenate(
            [og[2 * b][:, s * T:(s + 1) * T], og[2 * b + 1][:, s * T:(s + 1) * T]], axis=0))
        mapsC.append(stage_c_inputs(inp, x1[ci], og_own))
    resC = run_bass_kernel_spmd(ncC, mapsC, core_ids=ids)
    out = np.empty((BATCH, SEQ, D), np.float32)
    for ci, (b, s) in enumerate(CORES):
        out[b, s * T:(s + 1) * T, :] = np.asarray(resC.results[ci]["outT"]).T
    return out


PAIRS = [[0, 1], [2, 3], [4, 5], [6, 7]]


def fused_shapes():
    sh = {}
    for k, v in STAGE_A_SHAPES.items():
        sh[k] = (v, F32)
    for k, v in STAGE_B_SHAPES.items():
        if k not in ("h1T_full",):
            sh[k] = v
    for k, v in STAGE_C_SHAPES.items():
        if k not in ("x1T", "ogT_own"):
            sh[k] = v
    sh["m01"] = ([128, 2], F32)
    return sh


def fused_inputs(inp, b, s):
    m = {}
    m.update(stage_a_inputs(inp, b, s))
    mb = stage_b_inputs(inp, s, None)
    mb.pop("h1T_full")
    m.update(mb)
    mc = stage_c_inputs(inp, None, None)
    mc.pop("x1T")
    mc.pop("ogT_own")
    m.update(mc)
    m01 = np.zeros((128, 2), np.float32)
    m01[:, s] = 1.0
    m["m01"] = m01
    return m


def build_fused():
    nc = bass.Bass("TRN2", target_bir_lowering=False)
    d = {k: nc.dram_tensor(k, shp, dt, kind="ExternalInput").ap() for k, (shp, dt) in fused_shapes().items()}
    out = nc.dram_tensor("outT", [D, T], F32, kind="ExternalOutput").ap()
    x1_scr = nc.dram_tensor("x1_scr", [D, T], F32).ap()
    h1_src = [nc.dram_tensor("h1_src%d" % t, [D, 512], BF16).ap() for t in range(NTT)]
    h1_all = [nc.dram_tensor("h1_all%d" % t, [2 * D, 512], BF16).ap() for t in range(NTT)]
    og_src = [nc.dram_tensor("og_src%d" % t, [2 * D, 512], BF16).ap() for t in range(NTT)]
    og_own = [nc.dram_tensor("og_own%d" % t, [D, 512], BF16).ap() for t in range(NTT)]
    P = Prog(nc)
    E = setup_env(nc, P, d)
    sb = E.sb
    stage_a(E, d, out_x1=x1_scr, out_h1=h1_src)
    sb.free(E.xT)
    b_h1all = [Buf() for _ in range(NTT)]
    for t in range(NTT):
        def ag(e, s_, t=t):
            e.collective_compute("AllGather", ALU.bypass, replica_groups=PAIRS, ins=[h1_src[t].opt()],
                                 outs=[h1_all[t].opt()]).then_inc(s_)
        P.add("pool", ag, [E.b_h1_src[t]], [b_h1all[t]], dma=1, semkey="cc_ag", inc=1, selfwait=True)

    def hv(nb):
        return h1_all[nb % 4].rearrange("(r c p) t -> p r c t", r=2, p=128)[:, nb // 4]

    def og_rs(nb):
        return og_src[nb % 4].rearrange("(s f) t -> s f t", s=2)[nb // 4]
    b_ogown = [Buf() for _ in range(NTT)]

    def post_block(nb):
        if nb < 4:
            return
        j = nb - 4

        def rs(e, s_, j=j):
            e.collective_compute("ReduceScatter", ALU.add, replica_groups=PAIRS, ins=[og_src[j].opt()],
                                 outs=[og_own[j].opt()]).then_inc(s_)
        P.add("pool", rs, E.bog_src[j] + E.bog_src[j + 4], [b_ogown[j]], dma=1, semkey="cc_rs", inc=1,
              selfwait=True)
    mark = dict(sb.allocs)
    stage_b(E, d, None, hv=hv, og_rs=og_rs, og_reads=lambda nb: [b_h1all[nb % 4]], post_block=post_block)
    for k_, (v_, st_, nb_, bl_) in list(sb.allocs.items()):
        if k_ not in mark:
            sb.free(v_)
    d["x1T"] = x1_scr
    d["ogT_own"] = og_own
    E.c_reads = b_ogown
    stage_c(E, d, out, load_x=True)
    P.emit(None)
    return nc, P, E


def kernel(**inp):
    inp = {k: np.asarray(v) for k, v in inp.items()}
    nc, _, _ = build_fused()
    res = run_bass_kernel_spmd(nc, [fused_inputs(inp, b, s) for (b, s) in CORES], core_ids=list(range(8)))
    out = np.empty((BATCH, SEQ, D), np.float32)
    for ci, (b, s) in enumerate(CORES):
        out[b, s * T:(s + 1) * T, :] = np.asarray(res.results[ci]["outT"]).T
    return out
```

```python
import os
import numpy as np
from contextlib import ExitStack
import concourse.bass as bass
import concourse.mybir as mybir
from concourse.bass_utils import run_bass_kernel_spmd

F32 = mybir.dt.float32
BF16 = mybir.dt.bfloat16
AF = mybir.ActivationFunctionType
ALU = mybir.AluOpType
AX = mybir.AxisListType

ENGS = ("pe", "act", "dve", "pool", "sp")


class Buf:
    __slots__ = ("name", "last_w", "readers", "const", "excl")

    def __init__(self, name="", excl=False):
        self.name = name
        self.last_w = None
        self.readers = []
        self.const = False
        self.excl = excl


class Instr:
    __slots__ = ("eng", "fn", "deps", "is_dma", "ndma", "semkey", "signal", "sig_sem",
                 "sig_val", "idx", "users", "selfwait")

    def __init__(self, eng, fn):
        self.eng = eng
        self.fn = fn
        self.deps = []
        self.is_dma = False
        self.ndma = 0
        self.semkey = None
        self.signal = False
        self.sig_sem = None
        self.sig_val = 0
        self.users = False
        self.selfwait = False


class Prog:
    def __init__(self, nc):
        self.nc = nc
        self.streams = {e: [] for e in ENGS}
        self.n = 0

    def add(self, eng, fn, reads=(), writes=(), dma=0, semkey=None, inc=16, selfwait=False):
        I = Instr(eng, fn)
        I.selfwait = selfwait
        I.idx = self.n
        self.n += 1
        if dma:
            I.is_dma = True
            I.ndma = dma * inc
            I.semkey = semkey
        raw = {}
        oth = {}
        xb = [b for b in list(reads) + list(writes) if b.excl]
        if xb:
            reads = [b for b in reads if not b.excl]
            writes = [b for b in writes if not b.excl]
            for b in xb:
                J = b.last_w
                if J is not None and J is not I and (J.eng != eng or J.is_dma or I.is_dma):
                    if J not in I.deps:
                        I.deps.append(J)
                        J.users = True
                b.last_w = I
        for b in reads:
            if b.last_w is not None:
                raw[id(b.last_w)] = b.last_w
        for b in writes:
            if b.last_w is not None:
                oth[id(b.last_w)] = b.last_w
            for r in b.readers:
                oth[id(r)] = r
        for k, J in list(raw.items()) + list(oth.items()):
            if J is I or J in I.deps:
                continue
            anydma = J.is_dma or I.is_dma
            if J.eng == eng and not anydma:
                if eng == "pe" or k not in raw:
                    continue
            I.deps.append(J)
            J.users = True
        for b in writes:
            b.last_w = I
            b.readers = []
        for b in reads:
            if b.const:
                continue
            if b.last_w is not I:
                b.readers.append(I)
        self.streams[eng].append(I)
        return I

    def emit(self, engines, final_waits=True):
        nc = self.nc
        with ExitStack() as es:
            esem = {e: es.enter_context(nc.semaphore("s_" + e)) for e in ENGS}
            dsem = {}
            ecount = {e: 0 for e in ENGS}
            dcount = {}
            for e in ENGS:
                for I in self.streams[e]:
                    if I.is_dma:
                        k = I.semkey
                        if k not in dsem:
                            dsem[k] = es.enter_context(nc.semaphore("d_" + str(k)))
                            dcount[k] = 0
                        dcount[k] += I.ndma
                        I.sig_sem = dsem[k]
                        I.sig_val = dcount[k]
                        I.signal = True
                    elif I.users:
                        ecount[e] += 1
                        I.sig_sem = esem[e]
                        I.sig_val = ecount[e]
                        I.signal = True
            self.sem_counts = dict(ecount)
            with nc.Block() as block:
                def run(e, eng):
                    waited = {}
                    for I in self.streams[e]:
                        need = {}
                        for J in I.deps:
                            s = J.sig_sem
                            v = J.sig_val
                            key = id(s)
                            if waited.get(key, 0) >= v:
                                continue
                            if key not in need or need[key][1] < v:
                                need[key] = (s, v)
                        for key, (s, v) in need.items():
                            eng.wait_ge(s, v)
                            waited[key] = v
                        if I.is_dma:
                            I.fn(eng, I.sig_sem)
                            if I.selfwait:
                                eng.wait_ge(I.sig_sem, I.sig_val)
                                waited[id(I.sig_sem)] = I.sig_val
                        else:
                            r = I.fn(eng)
                            if I.signal:
                                r.then_inc(I.sig_sem, 1)
                    if final_waits and e == "sp":
                        for k, s in dsem.items():
                            eng.wait_ge(s, dcount[k])
                        for e2 in ENGS:
                            if ecount[e2] > 0:
                                eng.wait_ge(esem[e2], ecount[e2])

                @block.tensor
                def _(eng):
                    run("pe", eng)

                @block.scalar
                def _(eng):
                    run("act", eng)

                @block.vector
                def _(eng):
                    run("dve", eng)

                @block.gpsimd
                def _(eng):
                    run("pool", eng)

                @block.sync
                def _(eng):
                    run("sp", eng)


class SbufAlloc:
    def __init__(self, nc, nbytes):
        self.t32 = nc.alloc_sbuf_tensor("arena", [128, nbytes // 4], F32)
        self.t16 = self.t32.bitcast(BF16)
        self.free_list = [(0, nbytes)]
        self.allocs = {}
        self.dead = []
        self.last = None
        self.used = 0
        self.hi = 0

    def alloc(self, shape, dtype, name=None):
        esz = 4 if dtype == F32 else 2
        n = 1
        for s in shape[1:]:
            n *= s
        nbytes = (n * esz + 63) // 64 * 64
        for k, (st, sz) in enumerate(self.free_list):
            if sz >= nbytes:
                break
        else:
            raise AssertionError("SBUF overflow %s need %d free %s" % (name, nbytes, self.free_list))
        if sz == nbytes:
            self.free_list.pop(k)
        else:
            self.free_list[k] = (st + nbytes, sz - nbytes)
        base = self.t32 if dtype == F32 else self.t16
        e0 = st // esz
        v = base[0:shape[0], e0:e0 + n]
        if len(shape) == 3:
            v = v.rearrange("p (a b) -> p a b", a=shape[1])
        elif len(shape) == 4:
            v = v.rearrange("p (a b c) -> p a b c", a=shape[1], b=shape[2])
        self.allocs[id(v)] = (v, st, nbytes, [])
        self.last = id(v)
        self.used += nbytes
        self.hi = max(self.hi, self.used)
        return v

    def newbuf(self, name=""):
        b = Buf(name)
        v, st, nb, bl = self.allocs[self.last]
        for (db, ds, de) in self.dead:
            if ds < st + nb and st < de:
                if db.last_w is not None:
                    b.readers.append(db.last_w)
                b.readers.extend(db.readers)
        bl.append(b)
        return b

    def newbufs(self, n):
        return [self.newbuf() for _ in range(n)]

    def free(self, *aps):
        for ap in aps:
            v, st, nb, bl = self.allocs.pop(id(ap))
            for b in bl:
                self.dead.append((b, st, st + nb))
            self.used -= nb
            fl = self.free_list + [(st, nb)]
            fl.sort()
            out = []
            for (a, z) in fl:
                if out and out[-1][0] + out[-1][1] == a:
                    out[-1] = (out[-1][0], out[-1][1] + z)
                else:
                    out.append((a, z))
            self.free_list = out


def op_mm(P, out, lhsT, rhs, start, stop, reads, writes, tp=None):
    if tp is None:
        P.add("pe", lambda e: e.matmul(out, lhsT, rhs, start=start, stop=stop), reads, writes)
    else:
        P.add("pe", lambda e: e.matmul(out, lhsT, rhs, start=start, stop=stop,
                                       tile_position=tp), reads, writes)


def op_tr(P, out, in_, ident, reads, writes):
    P.add("pe", lambda e: e.transpose(out, in_, ident), reads, writes)


def op_act(P, out, in_, func, reads, writes, bias=None, scale=None, accum=None):
    kw = {}
    if bias is not None:
        kw["bias"] = bias
    if scale is not None:
        kw["scale"] = scale
    if accum is not None:
        kw["accum_out"] = accum
    P.add("act", lambda e: e.activation(out=out, in_=in_, func=func, **kw), reads, writes)


def op_tt(P, eng, out, in0, in1, op, reads, writes):
    P.add(eng, lambda e: e.tensor_tensor(out=out, in0=in0, in1=in1, op=op), reads, writes)


def op_ts(P, eng, out, in0, s1, s2, op0, op1, reads, writes, accum=None):
    if accum is None:
        P.add(eng, lambda e: e.tensor_scalar(out=out, in0=in0, scalar1=s1, scalar2=s2,
                                             op0=op0, op1=op1), reads, writes)
    else:
        P.add(eng, lambda e: e.tensor_scalar(out=out, in0=in0, scalar1=s1, scalar2=s2,
                                             op0=op0, op1=op1, accum_out=accum), reads, writes)


def op_stt(P, out, in0, scalar, in1, op0, op1, reads, writes):
    P.add("dve", lambda e: e.scalar_tensor_tensor(out=out, in0=in0, scalar=scalar, in1=in1,
                                                  op0=op0, op1=op1), reads, writes)


def op_copy(P, eng, out, in_, reads, writes):
    if eng == "act":
        P.add("act", lambda e: e.activation(out=out, in_=in_, func=AF.Copy), reads, writes)
    else:
        P.add(eng, lambda e: e.tensor_copy(out=out, in_=in_), reads, writes)


def op_memset(P, eng, out, val, writes):
    P.add(eng, lambda e: e.memset(out, val), (), writes)


def op_rmax(P, out, in_, reads, writes):
    P.add("dve", lambda e: e.reduce_max(out=out, in_=in_, axis=AX.X), reads, writes)


def op_recip(P, out, in_, reads, writes):
    P.add("dve", lambda e: e.reciprocal(out=out, in_=in_), reads, writes)


def op_dma(P, eng, pairs, reads, writes, semkey):
    def fn(e, s, pairs=pairs):
        for (o, i) in pairs:
            e.dma_start(out=o, in_=i).then_inc(s, 16)
    P.add(eng, fn, reads, writes, dma=len(pairs), semkey=semkey)


D = 1024
NCH = 8
T = 2048
TH = 512
TS = T + TH
NTT = 4
DFF = 4096
EPS = 1e-6
NEG = -30000.0


class Env:
    pass


def setup_env(nc, P, consts_dram, sbuf_bytes=206 * 1024):
    E = Env()
    E.nc = nc
    E.P = P
    E.sb = SbufAlloc(nc, sbuf_bytes)
    sb = E.sb
    E.psall_h = nc.alloc_psum_tensor("psall", [128, 8, 512], F32)
    E.psall = E.psall_h
    E.ps = [E.psall[:, i, :] for i in range(8)]
    E.psall16 = E.psall.bitcast(BF16)
    E._keys = 0

    def key(name, E=E):
        E._keys += 1
        return "%s%d" % (name, E._keys)
    E.key = key
    E.bps = [Buf("ps%d" % i, excl=True) for i in range(8)]
    E.ident_f = sb.alloc([128, 128], F32)
    E.bident_f = sb.newbuf()
    E.ident_b = sb.alloc([128, 128], BF16)
    E.bident_b = sb.newbuf()
    E.ones_b = sb.alloc([128, 128], BF16)
    E.ones_f = sb.alloc([128, 128], F32)
    E.eps = sb.alloc([128, 1], F32)
    E.bconst2 = sb.newbuf()
    op_dma(P, "sp", [(E.ident_f, consts_dram["ident"])], (), [E.bident_f], "const")
    op_dma(P, "pool", [(E.ident_b, consts_dram["ident"])], (), [E.bident_b], "constb")
    op_memset(P, "dve", E.ones_b, 1.0, [E.bconst2])
    op_memset(P, "dve", E.ones_f, 1.0, [E.bconst2])
    op_memset(P, "dve", E.eps, EPS, [E.bconst2])
    return E


def rmsnorm_tile(E, xT, bx, c0, nw, bnw, hout, bh, S, ncols=512, psbank=7):
    P = E.P
    sq, bsq, rt, brt, rstd, brstd = S["sq"], S["bsq"], S["rt"], S["brt"], S["rstd"], S["brstd"]
    ps = E.ps[psbank][:, 0:ncols]
    bp = E.bps[psbank]
    for c in range(NCH):
        op_act(P, sq[:, c, 0:ncols], xT[:, c, c0:c0 + ncols], AF.Square, [bx[c]], [bsq[c]])
    for c in range(NCH):
        op_mm(P, ps, E.ones_b, sq[:, c, 0:ncols], c == 0, c == NCH - 1,
              [bsq[c], E.bconst2], [bp])
    op_act(P, rt[:, 0:ncols], ps, AF.Sqrt, [bp, E.bconst2], [brt], bias=E.eps, scale=1.0 / D)
    op_recip(P, rstd[:, 0:ncols], rt[:, 0:ncols], [brt], [brstd])
    for c in range(NCH):
        op_stt(P, hout[:, c, 0:ncols], xT[:, c, c0:c0 + ncols], nw[:, c:c + 1], rstd[:, 0:ncols],
               ALU.mult, ALU.mult, [bx[c], bnw, brstd], [bh])


def norm_scratch(E):
    sb = E.sb
    S = {}
    S["sq"] = sb.alloc([128, NCH, 512], BF16)
    S["bsq"] = sb.newbufs(NCH)
    S["rt"] = sb.alloc([128, 512], F32)
    S["brt"] = sb.newbuf()
    S["rstd"] = sb.alloc([128, 512], F32)
    S["brstd"] = sb.newbuf()
    return S


def free_norm_scratch(E, S):
    E.sb.free(S["sq"], S["rt"], S["rstd"])


def flat2(ap3):
    return ap3.rearrange("p a b -> p (a b)")


class Rot:
    def __init__(self, lst):
        self.lst = list(lst)
        self.i = 0

    def next(self):
        v = self.lst[self.i % len(self.lst)]
        self.i += 1
        return v


def mlp_block(E, nw_dram, w_up, w_down):
    P, sb = E.P, E.sb
    ps, bps = E.ps, E.bps
    xT, bx = E.xT, E.bx
    nw = sb.alloc([128, NCH], F32)
    bnw = sb.newbuf()
    op_dma(P, "sp", [(nw, nw_dram)], (), [bnw], E.key("nw"))
    hT = sb.alloc([128, NCH, T], BF16)
    bh = sb.newbufs(NTT)
    S = norm_scratch(E)
    for t in range(NTT):
        rmsnorm_tile(E, xT, [bx[c][t] for c in range(NCH)], t * 512, nw, bnw,
                     hT[:, :, t * 512:(t + 1) * 512], bh[t], S)
    free_norm_scratch(E, S)
    G = 4
    NG = DFF // (128 * G)
    upT, bup, wu, bwu, wd, bwd = [], [], [], [], [], []
    for k in range(2):
        upT.append(sb.alloc([128, G, T], BF16))
        bup.append([[sb.newbuf() for t in range(NTT)] for j in range(G)])
        wu.append(sb.alloc([128, NCH, 128 * G], BF16))
        bwu.append(sb.newbuf())
        wd.append(sb.alloc([128, G, D], BF16))
        bwd.append(sb.newbuf())
    rl, brl = [], []
    for r in range(3):
        rl.append(sb.alloc([128, 512], F32))
        brl.append(sb.newbuf())
    rrl = Rot(range(3))
    rup = Rot([0, 1, 2, 3])
    rdn = Rot([4, 5, 6, 7])
    wu_v = w_up.rearrange("(c p) f -> p c f", p=128)
    wd_v = w_down.rearrange("(j p) n -> p j n", p=128)
    ku, kd = E.key("wu"), E.key("wd")
    for g in range(NG):
        k = g % 2
        if g < int(os.environ.get("MLP_NLOAD", "99")):
            op_dma(P, "pool", [(wu[k], wu_v[:, :, g * 128 * G:(g + 1) * 128 * G])], (), [bwu[k]],
                   "%s_%d" % (ku, k))
            op_dma(P, "pool", [(wd[k], wd_v[:, g * G:(g + 1) * G, :])], (), [bwd[k]], "%s_%d" % (kd, k))
        for t in range(NTT):
            tc_ = slice(t * 512, (t + 1) * 512)
            for j in range(G):
                b = rup.next()
                for c in range(NCH):
                    op_mm(P, ps[b], wu[k][:, c, j * 128:(j + 1) * 128], hT[:, c, tc_], c == 0,
                          c == NCH - 1, [bwu[k], bh[t]], [bps[b]])
                r = rrl.next()
                op_act(P, rl[r], ps[b], AF.Relu, [bps[b]], [brl[r]])
                op_tt(P, "pool", upT[k][:, j, tc_], rl[r], rl[r], ALU.mult, [brl[r]], [bup[k][j][t]])
        for t in range(NTT):
            tc_ = slice(t * 512, (t + 1) * 512)
            for m in range(NCH):
                b = rdn.next()
                for j in range(G):
                    op_mm(P, ps[b], wd[k][:, j, m * 128:(m + 1) * 128], upT[k][:, j, tc_], j == 0,
                          j == G - 1, [bwd[k], bup[k][j][t]], [bps[b]])
                op_tt(P, "dve", xT[:, m, tc_], xT[:, m, tc_], ps[b], ALU.add, [bps[b], bx[m][t]],
                      [bx[m][t]])
    sb.free(nw, hT, *upT, *wu, *wd, *rl)


def stage_a(E, d, out_x1=None, out_h1=None):
    P, sb = E.P, E.sb
    ps, bps = E.ps, E.bps
    nw_a = sb.alloc([128, NCH], F32)
    bnw_a = sb.newbuf()
    op_dma(P, "sp", [(nw_a, d["attn_norm"])], (), [bnw_a], E.key("nw"))
    hT_all = sb.alloc([128, NCH, TS], BF16)
    bh = sb.newbufs(5)
    oT = sb.alloc([128, NCH, T], BF16)
    boT = [[sb.newbuf() for t in range(NTT)] for hp in range(NCH)]
    xl, bxl = [], []
    for k in range(2):
        xl.append(sb.alloc([128, NCH, 512], F32))
        bxl.append(sb.newbuf())
    S = norm_scratch(E)
    xin = d["xT_in"].rearrange("(c p) t -> p c t", p=128)
    kx = E.key("xl")
    for tt in range(5):
        k = tt % 2
        op_dma(P, "sp", [(xl[k], xin[:, :, tt * 512:(tt + 1) * 512])], (), [bxl[k]],
               "%s_%d" % (kx, k))
        rmsnorm_tile(E, xl[k], [bxl[k]] * NCH, 0, nw_a, bnw_a,
                     hT_all[:, :, tt * 512:(tt + 1) * 512], bh[tt], S)
    sb.free(xl[0], xl[1])
    free_norm_scratch(E, S)

    mask0 = sb.alloc([128, 640], F32)
    bmask0 = sb.newbuf()
    hmb = sb.alloc([128, 1152], F32)
    bhmb = sb.newbuf()
    op_dma(P, "sp", [(mask0, d["mask0"])], (), [bmask0], E.key("mask0"))
    op_dma(P, "sp", [(hmb, d["hmask"])], (), [bhmb], E.key("hmb"))
    wqkv, bw, Bm, bBm, kT, bk, vtok, bv, qT, bq = [], [], [], [], [], [], [], [], [], []
    for k in range(2):
        wqkv.append(sb.alloc([128, 3, NCH, 128], BF16))
        bw.append(sb.newbuf())
        Bm.append(sb.alloc([128, 2, 640], F32))
        bBm.append(sb.newbuf())
        kT.append(sb.alloc([128, TS], BF16))
        bk.append(sb.newbufs(5))
        vtok.append(sb.alloc([128, TS], BF16))
        bv.append(sb.newbufs(5))
        qT.append(sb.alloc([128, T], BF16))
        bq.append(sb.newbufs(NTT))
    s_sb, bs, p_sb, bp_, pt_sb, bpt, dg, bdg, st, bst = [], [], [], [], [], [], [], [], [], []
    for s in range(4):
        s_sb.append(sb.alloc([128, 640], F32))
        bs.append(sb.newbuf())
        p_sb.append(sb.alloc([128, 640], BF16))
        bp_.append(sb.newbuf())
        pt_sb.append(sb.alloc([128, 640], BF16))
        bpt.append(sb.newbuf())
        dg.append(sb.alloc([128, 128], BF16))
        bdg.append(sb.newbuf())
        st.append(sb.alloc([128, 4], F32))
        bst.append([sb.newbuf(), sb.newbuf(), sb.newbuf()])
    wq_v = d["w_qkv"].rearrange("(c p) (s n) -> p s c n", p=128, s=3)
    kwq, kbm = E.key("wqkv"), E.key("bm")
    S_ps = [flat2(E.psall[:, 0:2, :]), flat2(E.psall[:, 2:4, :])]
    PT_ps = flat2(E.psall[:, 4:6, :])

    def load_pair(hp):
        i = hp % 2
        op_dma(P, "pool", [(wqkv[i][:, s3], wq_v[:, s3, :, hp * 128:(hp + 1) * 128])
                           for s3 in range(3)], (), [bw[i]], "%s_%d" % (kwq, i))
        op_dma(P, "sp", [(Bm[i][:, s], d["btab"][2 * hp + s]) for s in range(2)], (), [bBm[i]],
               "%s_%d" % (kbm, i))
        for s in range(2):
            op_tt(P, "pool", Bm[i][:, s], Bm[i][:, s], mask0, ALU.add, [bBm[i], bmask0], [bBm[i]])

    def proj_items(hp):
        i = hp % 2
        items = []
        for tt in range(5):
            cs = slice(tt * 512, (tt + 1) * 512)

            def kproj(tt=tt, cs=cs):
                for c in range(NCH):
                    op_mm(P, ps[7], wqkv[i][:, 1, c, :], hT_all[:, c, cs], c == 0, c == NCH - 1,
                          [bw[i], bh[tt]], [bps[7]])
                op_copy(P, "act", kT[i][:, cs], ps[7], [bps[7]], [bk[i][tt]])
            items.append(kproj)
            if tt >= 1:
                def qproj(tt=tt, cs=cs):
                    for c in range(NCH):
                        op_mm(P, ps[7], wqkv[i][:, 0, c, :], hT_all[:, c, cs], c == 0,
                              c == NCH - 1, [bw[i], bh[tt]], [bps[7]])
                    op_act(P, qT[i][:, (tt - 1) * 512:tt * 512], ps[7], AF.Copy, [bps[7]],
                           [bq[i][tt - 1]], scale=0.125)
                items.append(qproj)

            def vproj(tt=tt, cs=cs):
                for j in range(4):
                    for c in range(NCH):
                        op_mm(P, ps[7][:, j * 128:(j + 1) * 128],
                              hT_all[:, c, tt * 512 + j * 128:tt * 512 + (j + 1) * 128],
                              wqkv[i][:, 2, c, :], c == 0, c == NCH - 1, [bw[i], bh[tt]], [bps[7]])
                op_copy(P, "act", vtok[i][:, cs], ps[7], [bps[7]], [bv[i][tt]])
            items.append(vproj)
        return items

    def attn_s1(hp, p):
        i = hp % 2
        k0 = 128 * p
        kts = sorted(set([k0 // 512, (k0 + 639) // 512]))
        for s in range(2):
            z = s * 2 + (p % 2)
            rows = slice(s * 64, (s + 1) * 64)
            bS = [bps[2 * s], bps[2 * s + 1]]
            rk = [bq[i][p // 4]] + [bk[i][t_] for t_ in kts]
            op_mm(P, S_ps[s][:, 0:512], qT[i][rows, p * 128:(p + 1) * 128],
                  kT[i][rows, k0:k0 + 512], True, True, rk, bS, tp=(s * 64, 0))
            op_mm(P, S_ps[s][:, 512:640], qT[i][rows, p * 128:(p + 1) * 128],
                  kT[i][rows, k0 + 512:k0 + 640], True, True, rk, bS, tp=(s * 64, 0))
            op_tt(P, "dve", s_sb[z], S_ps[s][:, 0:640], Bm[i][:, s], ALU.add, bS + [bBm[i]], [bs[z]])
            if p < 4:
                op_tt(P, "dve", s_sb[z], s_sb[z], hmb[:, k0:k0 + 640], ALU.add, [bs[z], bhmb], [bs[z]])
            P.add("dve", lambda e, z=z: e.reduce_max(out=st[z][:, 0:1], in_=s_sb[z], axis=AX.X,
                                                    negate=True), [bs[z]], [bst[z][0]])
            op_act(P, p_sb[z], s_sb[z], AF.Exp, [bs[z], bst[z][0]], [bp_[z], bst[z][1]],
                   bias=st[z][:, 0:1], accum=st[z][:, 1:2])
            op_recip(P, st[z][:, 2:3], st[z][:, 1:2], [bst[z][1]], [bst[z][2]])
            op_ts(P, "pool", dg[z], E.ident_b, st[z][:, 2:3], 0.0, ALU.mult, ALU.add,
                  [E.bident_b, bst[z][2]], [bdg[z]])

    def attn_s2(hp, p):
        for s in range(2):
            z = s * 2 + (p % 2)
            for kb in range(5):
                op_mm(P, PT_ps[:, kb * 128:(kb + 1) * 128], p_sb[z][:, kb * 128:(kb + 1) * 128],
                      dg[z], True, True, [bp_[z], bdg[z]], [bps[4], bps[5]])
            op_copy(P, "act", pt_sb[z], PT_ps[:, 0:640], [bps[4], bps[5]], [bpt[z]])

    def attn_s3(hp, p):
        i = hp % 2
        oc = (p % 4) * 128
        for s in range(2):
            z = s * 2 + (p % 2)
            for kb in range(5):
                tl = p + kb
                op_mm(P, ps[6][s * 64:(s + 1) * 64, oc:oc + 128],
                      vtok[i][:, tl * 128 + s * 64:tl * 128 + (s + 1) * 64],
                      pt_sb[z][:, kb * 128:(kb + 1) * 128], kb == 0, kb == 4,
                      [bv[i][tl // 4], bpt[z]], [bps[6]], tp=(0, s * 64))
        op_copy(P, "act", oT[:, hp, p * 128:(p + 1) * 128], ps[6][:, oc:oc + 128], [bps[6]],
                [boT[hp][p // 4]])

    load_pair(0)
    for it in proj_items(0):
        it()
    for hp in range(NCH):
        nxt = []
        if hp + 1 < NCH:
            load_pair(hp + 1)
            nxt = proj_items(hp + 1)
        for step in range(18):
            if step < 16:
                attn_s1(hp, step)
            if 0 <= step - 1 < 16:
                attn_s2(hp, step - 1)
            if 0 <= step - 2 < 16:
                attn_s3(hp, step - 2)
            for _ in range(2):
                if nxt and step >= 1:
                    nxt.pop(0)()
        while nxt:
            nxt.pop(0)()
    sb.free(mask0, hmb, hT_all, *wqkv, *Bm, *kT, *vtok, *qT, *s_sb, *p_sb, *pt_sb, *dg, *st)

    E.xT = sb.alloc([128, NCH, T], F32)
    E.bx = [[sb.newbuf() for t in range(NTT)] for c in range(NCH)]
    xT, bx = E.xT, E.bx
    kx = E.key("xT")
    for t in range(NTT):
        op_dma(P, "sp", [(xT[:, :, t * 512:(t + 1) * 512], xin[:, :, TH + t * 512:TH + (t + 1) * 512])],
               (), [bx[c][t] for c in range(NCH)], "%s_%d" % (kx, t))
    wo = sb.alloc([128, NCH, D], BF16)
    bwo = sb.newbuf()
    op_dma(P, "pool", [(wo, d["w_o"].rearrange("(c p) n -> p c n", p=128))], (), [bwo], E.key("wo"))
    rb = Rot(range(8))
    for t in range(NTT):
        tc_ = slice(t * 512, (t + 1) * 512)
        for m in range(NCH):
            b = rb.next()
            for hp in range(NCH):
                op_mm(P, ps[b], wo[:, hp, m * 128:(m + 1) * 128], oT[:, hp, tc_], hp == 0,
                      hp == NCH - 1, [bwo, boT[hp][t]], [bps[b]])
            op_tt(P, "dve", xT[:, m, tc_], xT[:, m, tc_], ps[b], ALU.add, [bps[b], bx[m][t]],
                  [bx[m][t]])
    sb.free(oT, wo, nw_a)

    mlp_block(E, d["mlp_norm0"], d["w_up0"], d["w_down0"])

    if out_x1 is not None:
        ko = E.key("ox1")
        ov = out_x1.rearrange("(c p) t -> p c t", p=128)
        E.b_x1_scr = []
        for t in range(NTT):
            bscr = Buf()
            E.b_x1_scr.append(bscr)
            op_dma(P, "sp", [(ov[:, :, t * 512:(t + 1) * 512], xT[:, :, t * 512:(t + 1) * 512])],
                   [bx[c][t] for c in range(NCH)], [bscr], "%s_%d" % (ko, t))
    if out_h1 is not None:
        E.b_h1_src = []
        nw = sb.alloc([128, NCH], F32)
        bnw = sb.newbuf()
        op_dma(P, "sp", [(nw, d["dn_norm"])], (), [bnw], E.key("nw"))
        S = norm_scratch(E)
        h1 = [sb.alloc([128, NCH, 512], BF16) for _ in range(2)]
        bh1 = [sb.newbuf() for _ in range(2)]
        if isinstance(out_h1, list):
            hvl = [o_.rearrange("(c p) t -> p c t", p=128) for o_ in out_h1]
        else:
            hv = out_h1.rearrange("(c p) t -> p c t", p=128)
            hvl = [hv[:, :, t * 512:(t + 1) * 512] for t in range(NTT)]
        ko = E.key("oh1")
        for t in range(NTT):
            k = t % 2
            rmsnorm_tile(E, xT, [bx[c][t] for c in range(NCH)], t * 512, nw, bnw, h1[k], bh1[k], S)
            bsrc = Buf()
            E.b_h1_src.append(bsrc)
            op_dma(P, "sp", [(hvl[t], h1[k])], [bh1[k]], [bsrc], "%s_%d" % (ko, k))
        free_norm_scratch(E, S)
        sb.free(nw, *h1)


def _pc(v):
    return np.ascontiguousarray(np.asarray(v, np.float32).reshape(NCH, 128).T)


def host_consts():
    q = np.arange(128)[:, None]
    j = np.arange(640)[None, :]
    idx = np.clip(q - j + 512, -256, 256) + 256
    mask0 = np.zeros((128, 640), np.float32)
    mask0[(q < 64) & (j >= 576)] = NEG
    mask0[(q >= 64) & (j < 64)] = NEG
    return idx, mask0


def stage_a_inputs(inp, b, half):
    idx, mask0 = host_consts()
    x = inp["x"]
    lo = half * T
    xs = np.zeros((TS, D), np.float32)
    if half > 0:
        xs[:TH] = x[b, lo - TH:lo]
    xs[TH:] = x[b, lo:lo + T]
    hm = np.zeros((128, 1152), np.float32)
    if half == 0:
        hm[:, :TH] = NEG
    return {
        "ident": np.eye(128, dtype=np.float32),
        "xT_in": np.ascontiguousarray(xs.T),
        "attn_norm": _pc(inp["attn_norm"][0]),
        "w_qkv": np.ascontiguousarray(inp["attn_w_qkv"][0]),
        "btab": np.ascontiguousarray(inp["attn_rel_bias"][0][:, idx]),
        "mask0": mask0,
        "hmask": hm,
        "w_o": np.ascontiguousarray(inp["attn_w_o"][0]),
        "mlp_norm0": _pc(inp["mlp_norm"][0]),
        "w_up0": np.ascontiguousarray(inp["mlp_w_up"][0]),
        "w_down0": np.ascontiguousarray(inp["mlp_w_down"][0]),
        "dn_norm": _pc(inp["dn_norm"][0]),
    }


STAGE_A_SHAPES = {
    "ident": [128, 128], "xT_in": [D, TS], "attn_norm": [128, NCH], "w_qkv": [D, 3 * D],
    "btab": [16, 128, 640], "mask0": [128, 640], "hmask": [128, 1152], "w_o": [D, D],
    "mlp_norm0": [128, NCH], "w_up0": [D, DFF], "w_down0": [DFF, D], "dn_norm": [128, NCH],
}


def build_stage_a():
    nc = bass.Bass("TRN2", target_bir_lowering=False)
    d = {k: nc.dram_tensor(k, shp, F32, kind="ExternalInput").ap() for k, shp in STAGE_A_SHAPES.items()}
    x1 = nc.dram_tensor("x1T", [D, T], F32, kind="ExternalOutput").ap()
    h1 = nc.dram_tensor("h1T", [D, T], BF16, kind="ExternalOutput").ap()
    P = Prog(nc)
    E = setup_env(nc, P, d)
    stage_a(E, d, out_x1=x1, out_h1=h1)
    P.emit(None)
    return nc, P, E


DBG_N = int(os.environ.get('DBG_N', '99'))
NEU_R = os.environ.get('NEU_R', '0') == '1'
F32R = mybir.dt.float32r


def _r(ap):
    return ap.bitcast(F32R) if NEU_R else ap
SEQ = 4096
NBLK = SEQ // 512
NH = 4
BIG = 30000.0


def stage_b(E, d, out_og, nblk=NBLK, nsc=4, phases=99, hv=None, og_rs=None, og_reads=None, post_block=None):
    P, sb = E.P, E.sb
    ps, bps = E.ps, E.bps
    ps16 = E.psall16
    class _BS:
        def __getitem__(self, s_):
            return bps[s_[0]]
    bslot = _BS()
    rbk = Rot([3, 4, 5, 6, 7])

    def slot32(s_, n=128):
        return E.psall[:, s_[0], s_[1] * 128:s_[1] * 128 + n]

    def slot16(s_):
        return ps16[:, s_[0], s_[1] * 256:s_[1] * 256 + 128]

    triu = sb.alloc([128, 128], F32)
    posm = sb.alloc([128, 128], F32)
    negm = sb.alloc([128, 128], F32)
    bcm = sb.newbuf()
    op_dma(P, "sp", [(triu, d["triu"]), (posm, d["posm"]), (negm, d["negm"])], (), [bcm], E.key("cm"))
    cw = sb.alloc([128, 3 * NH * 4], F32)
    alog = sb.alloc([128, NH], F32)
    dtb = sb.alloc([128, NH], F32)
    hn = sb.alloc([128, 1], F32)
    bsm = sb.newbuf()
    op_dma(P, "sp", [(cw, d["cw"]), (alog, d["alog"]), (dtb, d["dtb"]), (hn, d["hn"])], (), [bsm],
           E.key("sm"))
    nega = sb.alloc([128, NH], F32)
    bnega = sb.newbuf()
    op_act(P, nega, alog, AF.Exp, [bsm], [bnega])
    op_ts(P, "dve", nega, nega, -1.0, None, ALU.mult, ALU.bypass, [bnega], [bnega])
    W = sb.alloc([128, NH * 4, NCH, 128], BF16)
    bW = sb.newbuf()
    wv = d["w_in"].rearrange("(c p) n -> p c n", p=128)
    kW = E.key("W")
    for sec in range(4):
        op_dma(P, "pool", [(W[:, h * 4 + sec], wv[:, :, sec * 512 + h * 128:sec * 512 + (h + 1) * 128])
                           for h in range(NH)], (), [bW], kW)
    wab = sb.alloc([128, NCH, 8], BF16)
    bwab = sb.newbuf()
    op_dma(P, "pool", [(wab, wv[:, :, 2048:2056])], (), [bwab], E.key("wab"))
    dgw = sb.alloc([128, 3 * NH * 4, 128], BF16)
    bdgw = sb.newbuf()
    for idx in range(3 * NH * 4):
        op_ts(P, "pool", dgw[:, idx, :], E.ident_b, cw[:, idx:idx + 1], 0.0, ALU.mult, ALU.add,
              [E.bident_b, bsm], [bdgw])
    pcb = sb.alloc([128, 3 * NH, 516], BF16)
    bpcb = [[sb.newbuf() for s in range(3)] for h in range(NH)]
    for h in range(NH):
        for s in range(3):
            op_memset(P, "pool", pcb[:, h * 3 + s, 0:3], 0.0, [bpcb[h][s]])
    S = sb.alloc([128, NH, 128], F32)
    Sb = sb.alloc([128, NH, 128], BF16)
    bS = sb.newbufs(NH)
    bSb = [Buf() for _ in range(NH)]
    for h in range(NH):
        op_memset(P, "pool", S[:, h, :], 0.0, [bS[h]])
        op_memset(P, "pool", Sb[:, h, :], 0.0, [bSb[h]])
    hblk = [sb.alloc([128, NCH, 512], BF16) for _ in range(2)]
    bhblk = [Buf(), Buf()]
    ab_sb = sb.alloc([128, 4, 8], F32)
    bab = sb.newbuf()
    gsb = sb.alloc([128, 4, NH], F32)
    beta = sb.alloc([128, 4, NH], F32)
    bg = sb.newbufs(4)
    bbeta = [Buf() for _ in range(4)]
    tmpg = sb.alloc([128, 8, NH], F32)
    btmpg = sb.newbuf()
    y = [sb.alloc([128, 2, 512], F32) for _ in range(NH)]
    by = [[Buf() for s in range(2)] for _ in range(NH)]
    vT = sb.alloc([128, NH, 512], BF16)
    bvT = sb.newbufs(NH)
    sz = sb.alloc([128, NH, 512], F32)
    bsz = sb.newbufs(NH)
    sq = [sb.alloc([128, 512], BF16) for _ in range(2)]
    bsq = [Buf(), Buf()]
    lnv = [sb.alloc([128, 512], F32) for _ in range(2)]
    blnv = [Buf(), Buf()]
    qn = sb.alloc([128, NH, 512], BF16)
    kn = sb.alloc([128, NH, 512], BF16)
    bqn = sb.newbufs(NH)
    bkn = [Buf() for _ in range(NH)]
    og = [sb.alloc([128, NH, 512], BF16) for _ in range(2)]
    bog = [[Buf() for h in range(NH)] for _ in range(2)]
    sm = sb.alloc([128, 8, NH], F32)
    bsmv = sb.newbufs(8)
    onrm = sb.alloc([128, NH, 4], F32)
    bonrm = [sb.newbufs(3) for _ in range(NH)]
    MB = {}
    names32 = ["gB", "ts", "Ds", "ti", "Dti", "Egc", "u", "L", "U", "Lp0", "Lp1", "Up0", "Up1", "P0", "P1"]
    names16 = ["AT", "kd", "kbd", "vb", "qd", "TTb", "wT", "vnew", "on", "junk"]
    for h in range(NH):
        for nm in names32:
            MB[(nm, h)] = (sb.alloc([128, 128], F32), sb.newbuf())
        for nm in names16:
            MB[(nm, h)] = (sb.alloc([128, 128], BF16), sb.newbuf())

    if hv is None:
        hv_ = d["h1T_full"].rearrange("(c p) t -> p c t", p=128)

        def hv(nb):
            return hv_[:, :, nb * 512:(nb + 1) * 512]
    if og_rs is not None:
        m01 = sb.alloc([128, 2], F32)
        bm01 = sb.newbuf()
        op_dma(P, "sp", [(m01, d["m01"])], (), [bm01], E.key("m01"))
        ogm = [sb.alloc([128, NH, 512], BF16) for _ in range(2)]
        bogm = [sb.newbuf() for _ in range(2)]
        E.bog_src = []
    khb = E.key("hblk")
    kog = E.key("og")
    rproj = Rot([0, 1])
    ones_col = E.ones_f[:, 0:1]

    def load_hblk(nb):
        op_dma(P, "sp", [(hblk[nb % 2], hv(nb))], (og_reads(nb) if og_reads else ()), [bhblk[nb % 2]],
               "%s_%d" % (khb, nb % 2))
    load_hblk(0)
    for nb in range(nblk):
        kb_ = nb % 2
        if nb + 1 < nblk:
            load_hblk(nb + 1)
        s_ab = (rbk.next(), 0)
        for j in range(4):
            for c in range(NCH):
                op_mm(P, slot32(s_ab)[:, j * 8:(j + 1) * 8], hblk[kb_][:, c, j * 128:(j + 1) * 128],
                      wab[:, c, :], c == 0, c == NCH - 1, [bhblk[kb_], bwab], [bslot[s_ab]])
        op_copy(P, "dve", flat2(ab_sb), slot32(s_ab)[:, 0:32], [bslot[s_ab]], [bab])
        for j in range(4):
            a_ = ab_sb[:, j, 0:NH]
            b_ = ab_sb[:, j, NH:2 * NH]
            t0, t1, t2, t3, t4, t5 = (tmpg[:, i_, :] for i_ in range(6))
            op_tt(P, "dve", t0, a_, dtb, ALU.add, [bab, bsm], [btmpg])
            op_act(P, t1, t0, AF.Abs, [btmpg], [btmpg])
            op_act(P, t2, t1, AF.Exp, [btmpg], [btmpg], scale=-1.0)
            op_act(P, t3, t2, AF.Ln, [btmpg], [btmpg], bias=ones_col)
            op_stt(P, t4, t0, 0.0, t3, ALU.max, ALU.add, [btmpg], [btmpg])
            op_tt(P, "dve", gsb[:, j, :], t4, nega, ALU.mult, [btmpg, bnega], [bg[j]])
            op_act(P, t5, b_, AF.Exp, [bab], [btmpg], scale=-1.0)
            op_ts(P, "dve", t5, t5, 1.0, None, ALU.add, ALU.bypass, [btmpg], [btmpg])
            op_recip(P, beta[:, j, :], t5, [btmpg], [bbeta[j]])
        for h in range(NH):
            yi = h
            for sec in range(3):
                pb = rproj.next()
                for c in range(NCH):
                    op_mm(P, ps[pb], W[:, h * 4 + sec, c, :], hblk[kb_][:, c, :], c == 0,
                          c == NCH - 1, [bW, bhblk[kb_]], [bps[pb]])
                pc = pcb[:, h * 3 + sec, :]
                op_copy(P, "act", pc[:, 3:515], ps[pb], [bps[pb]], [bpcb[h][sec]])
                for k in range(4):
                    op_mm(P, ps[2], dgw[:, (sec * NH + h) * 4 + k, :], pc[:, k:k + 512], k == 0,
                          k == 3, [bdgw, bpcb[h][sec]], [bps[2]])
                if sec < 2:
                    op_act(P, y[yi][:, sec, :], ps[2], AF.Silu, [bps[2]], [by[yi][sec]])
                else:
                    op_act(P, vT[:, h, :], ps[2], AF.Silu, [bps[2]], [bvT[h]])
                op_copy(P, "pool", pc[:, 0:3], pc[:, 512:515], [bpcb[h][sec]], [bpcb[h][sec]])
            pb = rproj.next()
            for c in range(NCH):
                op_mm(P, ps[pb], W[:, h * 4 + 3, c, :], hblk[kb_][:, c, :], c == 0, c == NCH - 1,
                      [bW, bhblk[kb_]], [bps[pb]])
            op_act(P, sz[:, h, :], ps[pb], AF.Silu, [bps[pb]], [bsz[h]])
        for h in range(NH):
            yi = h
            for sec in range(2):
                op_tt(P, "pool", sq[sec], y[yi][:, sec, :], y[yi][:, sec, :], ALU.mult,
                      [by[yi][sec]], [bsq[sec]])
                op_mm(P, ps[2], E.ones_b, sq[sec], True, True, [E.bconst2, bsq[sec]], [bps[2]])
                op_act(P, lnv[sec], ps[2], AF.Ln, [bps[2], E.bconst2], [blnv[sec]], bias=E.eps)
                op_act(P, lnv[sec], lnv[sec], AF.Exp, [blnv[sec]], [blnv[sec]], scale=-0.5)
                dst, bdst = (qn, bqn) if sec == 0 else (kn, bkn)
                op_stt(P, dst[:, h, :], y[yi][:, sec, :], (128.0 ** -0.5) if sec == 0 else 1.0,
                       lnv[sec], ALU.mult, ALU.mult, [by[yi][sec], blnv[sec]], [bdst[h]])
        oi = nb % 2
        for j in range(nsc):
            cs = slice(j * 128, (j + 1) * 128)
            bk_g = rbk.next()
            s_gc, s_gl = (bk_g, 0), (bk_g, 1)
            op_mm(P, slot32(s_gc, NH), triu, gsb[:, j, :], True, True, [bcm, bg[j]], [bslot[s_gc]])
            op_mm(P, slot32(s_gl, NH), E.ones_f, gsb[:, j, :], True, True, [E.bconst2, bg[j]],
                  [bslot[s_gl]])
            gc, gl, egc, ebk, dk, ekd, glast = (sm[:, i_, :] for i_ in range(7))
            op_copy(P, "dve", gc, slot32(s_gc, NH), [bslot[s_gc]], [bsmv[0]])
            op_copy(P, "dve", gl, slot32(s_gl, NH), [bslot[s_gl]], [bsmv[1]])
            op_act(P, egc, gc, AF.Exp, [bsmv[0]], [bsmv[2]])
            op_tt(P, "dve", ebk, beta[:, j, :], egc, ALU.mult, [bbeta[j], bsmv[2]], [bsmv[3]])
            op_tt(P, "dve", dk, gl, gc, ALU.subtract, [bsmv[0], bsmv[1]], [bsmv[4]])
            op_act(P, ekd, dk, AF.Exp, [bsmv[4]], [bsmv[5]])
            op_act(P, glast, gl, AF.Exp, [bsmv[1]], [bsmv[6]])

            st_ = [dict() for _ in range(NH)]

            def ph_a(h, BK):
                X = st_[h]
                gB, bgB = MB[("gB", h)]
                op_ts(P, "pool", gB, E.ones_f, gsb[:, j, h:h + 1], 0.0, ALU.mult, ALU.add,
                      [E.bconst2, bg[j]], [bgB])
                X["gbc"] = (BK["gbc"], h)
                op_mm(P, slot32(X["gbc"]), gB, triu, True, True, [bgB, bcm], [bslot[X["gbc"]]])
                X["kk"] = (BK["kk"], h)
                op_mm(P, slot32(X["kk"]), kn[:, h, cs], kn[:, h, cs], True, True, [bkn[h]],
                      [bslot[X["kk"]]])
                X["qk"] = (BK["qk"], h)
                op_mm(P, slot32(X["qk"]), kn[:, h, cs], qn[:, h, cs], True, True, [bkn[h], bqn[h]],
                      [bslot[X["qk"]]])
                X["kt"] = (BK["kt"], h)
                op_tr(P, slot16(X["kt"]), kn[:, h, cs], E.ident_b, [bkn[h], E.bident_b],
                      [bslot[X["kt"]]])
                X["vt"] = (BK["vt"], h)
                op_tr(P, slot16(X["vt"]), vT[:, h, cs], E.ident_b, [bvT[h], E.bident_b],
                      [bslot[X["vt"]]])

            def ph_b(h, BK):
                X = st_[h]
                G = slot32(X["gbc"])
                bG = bslot[X["gbc"]]
                ts, bts = MB[("ts", h)]
                Ds, bDs = MB[("Ds", h)]
                ti, bti = MB[("ti", h)]
                Dti, bDti = MB[("Dti", h)]
                Egc, bEgc = MB[("Egc", h)]
                Gs, bGs = MB[("gB", h)]
                op_copy(P, "dve", Gs, G, [bG], [bGs])
                op_stt(P, ts, Gs, sm[:, 0, h:h + 1], posm, ALU.subtract, ALU.max, [bGs, bsmv[0], bcm], [bts])
                op_act(P, Ds, ts, AF.Exp, [bts], [bDs], scale=-1.0)
                op_stt(P, ti, Gs, sm[:, 0, h:h + 1], negm, ALU.subtract, ALU.min, [bGs, bsmv[0], bcm], [bti])
                op_act(P, Dti, ti, AF.Exp, [bti], [bDti])
                op_act(P, Egc, Gs, AF.Exp, [bGs], [bEgc])
                L, bL = MB[("L", h)]
                if DBG_N > 5:
                    op_stt(P, _r(L), slot32(X["kk"]), beta[:, j, h:h + 1], Ds, ALU.mult, ALU.mult,
                           [bslot[X["kk"]], bbeta[j], bDs], [bL])
                AT, bAT = MB[("AT", h)]
                if DBG_N > 6:
                    op_tt(P, "dve", AT, slot32(X["qk"]), Dti, ALU.mult, [bslot[X["qk"]], bDti], [bAT])
                kd, bkd = MB[("kd", h)]
                kbd, bkbd = MB[("kbd", h)]
                vb, bvb = MB[("vb", h)]
                if DBG_N > 7:
                    op_act(P, kd, slot16(X["kt"]), AF.Copy, [bslot[X["kt"]], bsmv[5]], [bkd],
                           scale=sm[:, 5, h:h + 1])
                if DBG_N > 8:
                    op_act(P, kbd, slot16(X["kt"]), AF.Copy, [bslot[X["kt"]], bsmv[3]], [bkbd],
                           scale=sm[:, 3, h:h + 1])
                if DBG_N > 9:
                    op_act(P, vb, slot16(X["vt"]), AF.Copy, [bslot[X["vt"]], bbeta[j]], [bvb],
                           scale=beta[:, j, h:h + 1])
                qd, bqd = MB[("qd", h)]
                if DBG_N > 10:
                    op_tt(P, "pool", qd, qn[:, h, cs], Egc, ALU.mult, [bqn[h], bEgc], [bqd])

            def ph_c(h, BK):
                X = st_[h]
                L, bL = MB[("L", h)]
                U, bU = MB[("U", h)]
                X["ut"] = (BK["ut"], h)
                op_mm(P, slot32(X["ut"]), L, E.ident_f, True, True, [bL, E.bident_f], [bslot[X["ut"]]])
                op_copy(P, "act", _r(U), slot32(X["ut"]), [bslot[X["ut"]]], [bU])
                P0, bP0 = MB[("P0", h)]
                op_tt(P, "pool", _r(P0), E.ident_f, U, ALU.subtract, [E.bident_f, bU], [bP0])
                X["Lp"], X["Up"], X["Pm"] = ("L", h), ("U", h), ("P0", h)

            def ph_pow1(h, BK, m):
                X = st_[h]
                Lp, bLp = MB[X["Lp"]]
                Up, bUp = MB[X["Up"]]
                nl = ("Lp%d" % (m % 2), h)
                nu = ("Up%d" % (m % 2), h)
                s1 = (BK["s1"], h)
                op_mm(P, slot32(s1), _r(Up), _r(Lp), True, True, [bUp, bLp], [bslot[s1]])
                op_copy(P, "act", _r(MB[nl][0]), slot32(s1), [bslot[s1]], [MB[nl][1]])
                if m < 6:
                    s2 = (BK["s2"], h)
                    op_mm(P, slot32(s2), _r(Lp), _r(Up), True, True, [bUp, bLp], [bslot[s2]])
                    op_copy(P, "dve", _r(MB[nu][0]), slot32(s2), [bslot[s2]], [MB[nu][1]])
                    X["Up"] = nu
                X["Lp"] = nl
                X.setdefault("Lpm", {})[m] = nl

            def ph_pow2(h, BK, m):
                X = st_[h]
                nl = X["Lpm"][m]
                Pm, bPm = MB[X["Pm"]]
                npm = ("P%d" % (m % 2), h)
                s3 = (BK["s3"], h)
                op_mm(P, slot32(s3), _r(MB[nl][0]), _r(Pm), True, True, [MB[nl][1], bPm], [bslot[s3]])
                op_tt(P, "dve", _r(MB[npm][0]), Pm, slot32(s3), ALU.add, [bPm, bslot[s3]], [MB[npm][1]])
                X["Pm"] = npm

            def ph_d(h, BK):
                X = st_[h]
                TT_, bTT = MB[("TTb", h)]
                op_copy(P, "pool", TT_, MB[X["Pm"]][0], [MB[X["Pm"]][1]], [bTT])
                vb, bvb = MB[("vb", h)]
                kbd, bkbd = MB[("kbd", h)]
                u, bu = MB[("u", h)]
                wT, bwT = MB[("wT", h)]
                s1 = (BK["s1"], h)
                op_mm(P, slot32(s1), TT_, vb, True, True, [bTT, bvb], [bslot[s1]])
                op_copy(P, "act", u, slot32(s1), [bslot[s1]], [bu])
                s2 = (BK["s2"], h)
                op_mm(P, slot32(s2), kbd, TT_, True, True, [bTT, bkbd], [bslot[s2]])
                op_copy(P, "dve", wT, slot32(s2), [bslot[s2]], [bwT])

            def ph_e1(h, BK):
                u, bu = MB[("u", h)]
                wT, bwT = MB[("wT", h)]
                vnew, bvn = MB[("vnew", h)]
                s1 = (BK["s1"], h)
                op_mm(P, slot32(s1), wT, Sb[:, h, :], True, True, [bwT, bSb[h]], [bslot[s1]])
                op_tt(P, "dve", vnew, u, slot32(s1), ALU.subtract, [bu, bslot[s1]], [bvn])

            def ph_e2(h, BK):
                X = st_[h]
                vnew, bvn = MB[("vnew", h)]
                qd, bqd = MB[("qd", h)]
                AT, bAT = MB[("AT", h)]
                kd, bkd = MB[("kd", h)]
                X["o"] = (BK["o"], h)
                op_mm(P, slot32(X["o"]), qd, Sb[:, h, :], True, False, [bqd, bSb[h]], [bslot[X["o"]]])
                op_mm(P, slot32(X["o"]), AT, vnew, False, True, [bAT, bvn], [bslot[X["o"]]])
                s2 = (BK["s2"], h)
                op_mm(P, slot32(s2), kd, vnew, True, True, [bkd, bvn], [bslot[s2]])
                op_stt(P, S[:, h, :], S[:, h, :], sm[:, 6, h:h + 1], slot32(s2), ALU.mult, ALU.add,
                       [bS[h], bsmv[6], bslot[s2]], [bS[h]])
                op_copy(P, "pool", Sb[:, h, :], S[:, h, :], [bS[h]], [bSb[h]])

            def ph_f(h, BK):
                X = st_[h]
                o_ps = slot32(X["o"])
                bo = bslot[X["o"]]
                junk, bjunk = MB[("junk", h)]
                on, bon = MB[("on", h)]
                ssq, lv, rstd = onrm[:, h, 0:1], onrm[:, h, 1:2], onrm[:, h, 2:3]
                b0, b1, b2 = bonrm[h]
                op_act(P, junk, o_ps, AF.Square, [bo], [bjunk, b0], accum=ssq)
                op_act(P, lv, ssq, AF.Ln, [b0, E.bconst2], [b1], bias=E.eps, scale=1.0 / 128)
                op_act(P, rstd, lv, AF.Exp, [b1], [b2], scale=-0.5)
                op_act(P, on, o_ps, AF.Copy, [bo, b2], [bon], scale=rstd)
                s1 = (BK["s1"], h)
                op_tr(P, slot16(s1), on, E.ident_b, [bon, E.bident_b], [bslot[s1]])
                op_stt(P, og[oi][:, h, cs], slot16(s1), hn[:, 0:1], sz[:, h, cs], ALU.mult, ALU.mult,
                       [bslot[s1], bsm, bsz[h]], [bog[oi][h]])

            phl = [(ph_a, ("gbc", "kk", "qk", "kt", "vt")), (ph_b, ()), (ph_c, ("ut",))]
            phl.append((lambda h, BK: ph_pow1(h, BK, 1), ("s1", "s2")))
            for m in range(2, 7):
                phl.append(((lambda h, BK, m=m: ph_pow1(h, BK, m)), ("s1", "s2"),
                            (lambda h, BK, m=m: ph_pow2(h, BK, m - 1)), ("s3",)))
            phl.append((lambda h, BK: ph_pow2(h, BK, 6), ("s3",)))
            phl += [(ph_d, ("s1", "s2")), (ph_e1, ("s1",)), (ph_e2, ("o", "s2")), (ph_f, ("s1",))]
            BKo = None
            for ent in phl[:phases]:
                ph, names = ent[0], ent[1]
                BK = {nm: rbk.next() for nm in names}
                if "o" in BK:
                    BKo = BK["o"]
                if ph is ph_f:
                    BK["o"] = BKo
                for h in range(NH):
                    ph(h, BK)
                if len(ent) == 4:
                    BK2 = {nm: rbk.next() for nm in ent[3]}
                    for h in range(NH):
                        ent[2](h, BK2)
        if og_rs is None:
            ov = out_og.rearrange("(h p) t -> p h t", p=128)
            op_dma(P, "sp", [(ov[:, :, nb * 512:(nb + 1) * 512], og[oi])], bog[oi], (), "%s_%d" % (kog, oi))
        else:
            ovs = og_rs(nb).rearrange("(g h p) t -> p g h t", g=2, p=128)
            blk_bufs = []
            for g_ in range(2):
                bsrc = Buf()
                blk_bufs.append(bsrc)
                op_ts(P, "pool", ogm[g_], og[oi], m01[:, g_:g_ + 1], 0.0, ALU.mult, ALU.add,
                      bog[oi] + [bm01], [bogm[g_]])
                op_dma(P, "sp", [(ovs[:, g_], ogm[g_])], [bogm[g_]], [bsrc], "%s_%d" % (kog, g_))
            E.bog_src.append(blk_bufs)
            if post_block is not None:
                post_block(nb)
    if "dbg" in d:
        hh = int(os.environ.get("DBG_H", "3"))
        names = ["L", "U", "AT", "TTb", "u", "vnew", "kd", "kbd", "vb", "qd", "Ds", "Dti"]
        dbg = sb.alloc([128, 128 * len(names)], F32)
        bdbg = sb.newbuf()
        for ii, nm in enumerate(names):
            op_copy(P, "dve", dbg[:, ii * 128:(ii + 1) * 128], MB[(nm, hh)][0], [MB[(nm, hh)][1]], [bdbg])
        op_dma(P, "sp", [(d["dbg"], dbg)], [bdbg], (), E.key("dbg"))


STAGE_B_SHAPES = {
    "ident": ([128, 128], F32), "h1T_full": ([D, SEQ], BF16), "w_in": ([D, 2056], F32),
    "cw": ([128, 48], F32), "alog": ([128, NH], F32), "dtb": ([128, NH], F32), "hn": ([128, 1], F32),
    "triu": ([128, 128], F32), "posm": ([128, 128], F32), "negm": ([128, 128], F32),
}


def stage_b_inputs(inp, hg, h1T_full=None):
    hs = slice(hg * NH, (hg + 1) * NH)
    w = inp["dn_w_in"][0]
    cols = []
    for sec in range(4):
        cols.append(w[:, sec * 1024 + hg * 512:sec * 1024 + (hg + 1) * 512])
    cols.append(w[:, 4096 + hg * NH:4096 + (hg + 1) * NH])
    cols.append(w[:, 4104 + hg * NH:4104 + (hg + 1) * NH])
    cwf = inp["dn_conv_w"][0]
    cw = cwf.reshape(4, 3, 8, 128)[:, :, hs, :]
    cw = np.ascontiguousarray(cw.transpose(3, 1, 2, 0)).reshape(128, 48)
    ii = np.arange(128)[:, None]
    jj = np.arange(128)[None, :]
    return {
        "ident": np.eye(128, dtype=np.float32),
        "h1T_full": h1T_full,
        "w_in": np.ascontiguousarray(np.concatenate(cols, axis=1)),
        "cw": cw,
        "alog": np.ascontiguousarray(np.broadcast_to(inp["dn_a_log"][0][hs][None, :], (128, NH))),
        "dtb": np.ascontiguousarray(np.broadcast_to(inp["dn_dt_bias"][0][hs][None, :], (128, NH))),
        "hn": np.ascontiguousarray(inp["dn_head_norm"][0].reshape(128, 1)),
        "triu": (ii <= jj).astype(np.float32),
        "posm": np.where(ii > jj, 0.0, BIG).astype(np.float32),
        "negm": np.where(jj >= ii, 0.0, -BIG).astype(np.float32),
    }


def build_stage_b(**kw):
    nc = bass.Bass("TRN2", target_bir_lowering=False)
    d = {k: nc.dram_tensor(k, shp, dt, kind="ExternalInput").ap() for k, (shp, dt) in STAGE_B_SHAPES.items()}
    og = nc.dram_tensor("ogT", [NH * 128, SEQ], BF16, kind="ExternalOutput").ap()
    if os.environ.get("DBG_OUT"):
        d["dbg"] = nc.dram_tensor("dbg", [128, 128 * 12], F32, kind="ExternalOutput").ap()
    P = Prog(nc)
    E = setup_env(nc, P, d)
    stage_b(E, d, og, **kw)
    P.emit(None)
    return nc, P, E


def oproj_residual(E, oT, boT, wo_dram):
    P, sb = E.P, E.sb
    ps, bps = E.ps, E.bps
    xT, bx = E.xT, E.bx
    wo = sb.alloc([128, NCH, D], BF16)
    bwo = sb.newbuf()
    op_dma(P, "pool", [(wo, wo_dram.rearrange("(c p) n -> p c n", p=128))], (), [bwo], E.key("wo"))
    rb = Rot(range(8))
    for t in range(NTT):
        tc_ = slice(t * 512, (t + 1) * 512)
        for m in range(NCH):
            b = rb.next()
            for hp in range(NCH):
                op_mm(P, ps[b], wo[:, hp, m * 128:(m + 1) * 128], oT[:, hp, tc_], hp == 0,
                      hp == NCH - 1, [bwo, boT[hp][t]], [bps[b]])
            op_tt(P, "dve", xT[:, m, tc_], xT[:, m, tc_], ps[b], ALU.add, [bps[b], bx[m][t]],
                  [bx[m][t]])
    sb.free(wo)


def stage_c(E, d, out, load_x=True):
    P, sb = E.P, E.sb
    if load_x:
        E.xT = sb.alloc([128, NCH, T], F32)
        E.bx = [[sb.newbuf() for t in range(NTT)] for c in range(NCH)]
        xin = d["x1T"].rearrange("(c p) t -> p c t", p=128)
        kx = E.key("xT")
        for t in range(NTT):
            op_dma(P, "sp", [(E.xT[:, :, t * 512:(t + 1) * 512], xin[:, :, t * 512:(t + 1) * 512])],
                   ([E.b_x1_scr[t]] if hasattr(E, "b_x1_scr") else ()),
                   [E.bx[c][t] for c in range(NCH)], "%s_%d" % (kx, t))
    xT, bx = E.xT, E.bx
    oT = sb.alloc([128, NCH, T], BF16)
    boT = [[sb.newbuf() for t in range(NTT)] for hp in range(NCH)]
    if isinstance(d["ogT_own"], list):
        ovl = [o_.rearrange("(c p) t -> p c t", p=128) for o_ in d["ogT_own"]]
    else:
        ov = d["ogT_own"].rearrange("(c p) t -> p c t", p=128)
        ovl = [ov[:, :, t * 512:(t + 1) * 512] for t in range(NTT)]
    ko = E.key("og")
    for t in range(NTT):
        cr = [E.c_reads[t]] if hasattr(E, "c_reads") else ()
        op_dma(P, "sp", [(oT[:, :, t * 512:(t + 1) * 512], ovl[t])],
               cr, [boT[hp][t] for hp in range(NCH)], "%s_%d" % (ko, t))
    oproj_residual(E, oT, boT, d["dn_w_o"])
    sb.free(oT)
    mlp_block(E, d["mlp_norm1"], d["w_up1"], d["w_down1"])
    nw = sb.alloc([128, NCH], F32)
    bnw = sb.newbuf()
    op_dma(P, "sp", [(nw, d["final_norm"])], (), [bnw], E.key("nw"))
    S = norm_scratch(E)
    yo = [sb.alloc([128, NCH, 512], F32) for _ in range(2)]
    byo = [sb.newbuf() for _ in range(2)]
    outv = out.rearrange("(c p) t -> p c t", p=128)
    ko = E.key("out")
    for t in range(NTT):
        k = t % 2
        rmsnorm_tile(E, xT, [bx[c][t] for c in range(NCH)], t * 512, nw, bnw, yo[k], byo[k], S)
        op_dma(P, "sp", [(outv[:, :, t * 512:(t + 1) * 512], yo[k])], [byo[k]], (), "%s_%d" % (ko, k))
    free_norm_scratch(E, S)
    sb.free(nw, *yo)


STAGE_C_SHAPES = {
    "ident": ([128, 128], F32), "x1T": ([D, T], F32), "ogT_own": ([D, T], BF16),
    "dn_w_o": ([D, D], F32), "mlp_norm1": ([128, NCH], F32), "w_up1": ([D, DFF], F32),
    "w_down1": ([DFF, D], F32), "final_norm": ([128, NCH], F32),
}


def stage_c_inputs(inp, x1T=None, ogT_own=None):
    return {
        "ident": np.eye(128, dtype=np.float32),
        "x1T": x1T,
        "ogT_own": ogT_own,
        "dn_w_o": np.ascontiguousarray(inp["dn_w_o"][0]),
        "mlp_norm1": _pc(inp["mlp_norm"][1]),
        "w_up1": np.ascontiguousarray(inp["mlp_w_up"][1]),
        "w_down1": np.ascontiguousarray(inp["mlp_w_down"][1]),
        "final_norm": _pc(inp["final_norm"]),
    }


def build_stage_c():
    nc = bass.Bass("TRN2", target_bir_lowering=False)
    d = {k: nc.dram_tensor(k, shp, dt, kind="ExternalInput").ap() for k, (shp, dt) in STAGE_C_SHAPES.items()}
    out = nc.dram_tensor("outT", [D, T], F32, kind="ExternalOutput").ap()
    P = Prog(nc)
    E = setup_env(nc, P, d)
    stage_c(E, d, out)
    P.emit(None)
    return nc, P, E


BATCH = 4
CORES = [(b, s) for b in range(BATCH) for s in range(2)]


def kernel_unfused(**inp):
    inp = {k: np.asarray(v) for k, v in inp.items()}
    ids = list(range(8))
    ncA, _, _ = build_stage_a()
    resA = run_bass_kernel_spmd(ncA, [stage_a_inputs(inp, b, s) for (b, s) in CORES], core_ids=ids)
    x1 = [np.asarray(r["x1T"]) for r in resA.results]
    h1 = [np.asarray(r["h1T"]) for r in resA.results]
    ncB, _, _ = build_stage_b()
    mapsB = []
    for (b, hg) in CORES:
        h1_full = np.ascontiguousarray(np.concatenate([h1[2 * b], h1[2 * b + 1]], axis=1))
        mapsB.append(stage_b_inputs(inp, hg, h1_full))
    resB = run_bass_kernel_spmd(ncB, mapsB, core_ids=ids)
    og = [np.asarray(r["ogT"]) for r in resB.results]
    ncC, _, _ = build_stage_c()
    mapsC = []
    for ci, (b, s) in enumerate(CORES):
        og_own = np.ascontiguousarray(np.concatenate(
            [og[2 * b][:, s * T:(s + 1) * T], og[2 * b + 1][:, s * T:(s + 1) * T]], axis=0))
        mapsC.append(stage_c_inputs(inp, x1[ci], og_own))
    resC = run_bass_kernel_spmd(ncC, mapsC, core_ids=ids)
    out = np.empty((BATCH, SEQ, D), np.float32)
    for ci, (b, s) in enumerate(CORES):
        out[b, s * T:(s + 1) * T, :] = np.asarray(resC.results[ci]["outT"]).T
    return out


PAIRS = [[0, 1], [2, 3], [4, 5], [6, 7]]


def fused_shapes():
    sh = {}
    for k, v in STAGE_A_SHAPES.items():
        sh[k] = (v, F32)
    for k, v in STAGE_B_SHAPES.items():
        if k not in ("h1T_full",):
            sh[k] = v
    for k, v in STAGE_C_SHAPES.items():
        if k not in ("x1T", "ogT_own"):
            sh[k] = v
    sh["m01"] = ([128, 2], F32)
    return sh


def fused_inputs(inp, b, s):
    m = {}
    m.update(stage_a_inputs(inp, b, s))
    mb = stage_b_inputs(inp, s, None)
    mb.pop("h1T_full")
    m.update(mb)
    mc = stage_c_inputs(inp, None, None)
    mc.pop("x1T")
    mc.pop("ogT_own")
    m.update(mc)
    m01 = np.zeros((128, 2), np.float32)
    m01[:, s] = 1.0
    m["m01"] = m01
    return m


def build_fused():
    nc = bass.Bass("TRN2", target_bir_lowering=False)
    d = {k: nc.dram_tensor(k, shp, dt, kind="ExternalInput").ap() for k, (shp, dt) in fused_shapes().items()}
    out = nc.dram_tensor("outT", [D, T], F32, kind="ExternalOutput").ap()
    x1_scr = nc.dram_tensor("x1_scr", [D, T], F32).ap()
    h1_src = [nc.dram_tensor("h1_src%d" % t, [D, 512], BF16).ap() for t in range(NTT)]
    h1_all = [nc.dram_tensor("h1_all%d" % t, [2 * D, 512], BF16).ap() for t in range(NTT)]
    og_src = [nc.dram_tensor("og_src%d" % t, [2 * D, 512], BF16).ap() for t in range(NTT)]
    og_own = [nc.dram_tensor("og_own%d" % t, [D, 512], BF16).ap() for t in range(NTT)]
    P = Prog(nc)
    E = setup_env(nc, P, d)
    sb = E.sb
    stage_a(E, d, out_x1=x1_scr, out_h1=h1_src)
    sb.free(E.xT)
    b_h1all = [Buf() for _ in range(NTT)]
    for t in range(NTT):
        def ag(e, s_, t=t):
            e.collective_compute("AllGather", ALU.bypass, replica_groups=PAIRS, ins=[h1_src[t].opt()],
                                 outs=[h1_all[t].opt()]).then_inc(s_)
        P.add("pool", ag, [E.b_h1_src[t]], [b_h1all[t]], dma=1, semkey="cc_ag", inc=1, selfwait=True)

    def hv(nb):
        return h1_all[nb % 4].rearrange("(r c p) t -> p r c t", r=2, p=128)[:, nb // 4]

    def og_rs(nb):
        return og_src[nb % 4].rearrange("(s f) t -> s f t", s=2)[nb // 4]
    b_ogown = [Buf() for _ in range(NTT)]

    def post_block(nb):
        if nb < 4:
            return
        j = nb - 4

        def rs(e, s_, j=j):
            e.collective_compute("ReduceScatter", ALU.add, replica_groups=PAIRS, ins=[og_src[j].opt()],
                                 outs=[og_own[j].opt()]).then_inc(s_)
        P.add("pool", rs, E.bog_src[j] + E.bog_src[j + 4], [b_ogown[j]], dma=1, semkey="cc_rs", inc=1,
              selfwait=True)
    mark = dict(sb.allocs)
    stage_b(E, d, None, hv=hv, og_rs=og_rs, og_reads=lambda nb: [b_h1all[nb % 4]], post_block=post_block)
    for k_, (v_, st_, nb_, bl_) in list(sb.allocs.items()):
        if k_ not in mark:
            sb.free(v_)
    d["x1T"] = x1_scr
    d["ogT_own"] = og_own
    E.c_reads = b_ogown
    stage_c(E, d, out, load_x=True)
    P.emit(None)
    return nc, P, E


def kernel(**inp):
    inp = {k: np.asarray(v) for k, v in inp.items()}
    nc, _, _ = build_fused()
    res = run_bass_kernel_spmd(nc, [fused_inputs(inp, b, s) for (b, s) in CORES], core_ids=list(range(8)))
    out = np.empty((BATCH, SEQ, D), np.float32)
    for ci, (b, s) in enumerate(CORES):
        out[b, s * T:(s + 1) * T, :] = np.asarray(res.results[ci]["outT"]).T
    return out
```
